# Optimizing a Trainium2 kernel written in Bass

```python
import math
import jax, jax.numpy as jnp
from jax import lax
import numpy as np

D_MODEL = 1024
BATCH = 8
SEQ = 4096
DEPTH = 4

N_MIXERS = 3
N_A = (DEPTH + 2) // 3
N_B = (DEPTH + 1) // 3
N_C = DEPTH // 3
N_META = 16
Q_BLOCK = 128
EPS = 1e-6

MLA_HEADS = 16
MLA_Q_RANK = 256
MLA_KV_RANK = 128
MLA_NOPE = 64
MLA_ROPE = 32
MLA_V = 64
ROPE_THETA = 10000.0

SC_WIDTH = 3

DIFF_HEADS = 8
DIFF_HEAD_DIM = D_MODEL // DIFF_HEADS // 2
LAMBDA_INIT_SCALE = 0.1

D_FF = 2816
FFN_CONV_WIDTH = 3

kernel_name = "hybrid_mla_shortconv_diffattn_trunk"


def rms_norm(x, g):
    xf = x.astype(jnp.float32)
    y = xf * lax.rsqrt(jnp.mean(xf * xf, axis=-1, keepdims=True) + EPS)
    return (y * g.astype(jnp.float32)).astype(x.dtype)


def causal_dwconv(h, w):
    K = w.shape[0]
    L = h.shape[1]
    hp = jnp.pad(h, ((0, 0), (K - 1, 0), (0, 0)))
    y = w[K - 1] * hp[:, K - 1:K - 1 + L]
    for j in range(K - 1):
        y = y + w[j] * hp[:, j:j + L]
    return y


def rope_tables(L, dtype):
    inv_freq = ROPE_THETA ** (-jnp.arange(0, MLA_ROPE, 2, dtype=jnp.float32) / MLA_ROPE)
    ang = jnp.arange(L, dtype=jnp.float32)[:, None] * inv_freq[None, :]
    return jnp.cos(ang).astype(dtype), jnp.sin(ang).astype(dtype)


def apply_rope(x, cos, sin):
    x1, x2 = jnp.split(x, 2, axis=-1)
    return jnp.concatenate([x1 * cos - x2 * sin, x2 * cos + x1 * sin], axis=-1)


def alibi_slopes(n_heads):
    return 2.0 ** (-8.0 * jnp.arange(1, n_heads + 1, dtype=jnp.float32) / n_heads)


def sweep_causal_queries(block_fn, q):
    B, L = q.shape[0], q.shape[1]
    n_blk = (L - N_META) // Q_BLOCK
    pos = jnp.arange(L, dtype=jnp.int32)
    out_meta = block_fn(q[:, :N_META], pos[:N_META])
    q_rest = jnp.moveaxis(q[:, N_META:].reshape(B, n_blk, Q_BLOCK, *q.shape[2:]), 1, 0)
    pos_rest = pos[N_META:].reshape(n_blk, Q_BLOCK)
    out_rest = lax.map(lambda a: block_fn(a[0], a[1]), (q_rest, pos_rest))
    out_rest = jnp.moveaxis(out_rest, 0, 1).reshape(B, L - N_META, *out_rest.shape[3:])
    return jnp.concatenate([out_meta, out_rest], axis=1)


def mla_mixer(h, w_in, g_q, g_kv, w_uq, w_ukv, w_o, cos, sin):
    B, L, _ = h.shape
    c = h @ w_in
    c_q, c_kv, k_r = jnp.split(c, [MLA_Q_RANK, MLA_Q_RANK + MLA_KV_RANK], axis=-1)
    q = (rms_norm(c_q, g_q) @ w_uq).reshape(B, L, MLA_HEADS, MLA_NOPE + MLA_ROPE)
    kv = (rms_norm(c_kv, g_kv) @ w_ukv).reshape(B, L, MLA_HEADS, MLA_NOPE + MLA_V)
    q_nope, q_rope = jnp.split(q, [MLA_NOPE], axis=-1)
    k_nope, v = jnp.split(kv, [MLA_NOPE], axis=-1)
    q = jnp.concatenate([q_nope, apply_rope(q_rope, cos[:, None, :], sin[:, None, :])], axis=-1)
    k_rope = apply_rope(k_r, cos, sin)
    k = jnp.concatenate(
        [k_nope, jnp.broadcast_to(k_rope[:, :, None, :], (B, L, MLA_HEADS, MLA_ROPE))], axis=-1)
    scale = (MLA_NOPE + MLA_ROPE) ** -0.5
    k_pos = jnp.arange(L, dtype=jnp.int32)

    def block(qb, q_pos):
        s = jnp.einsum('bqhd,bkhd->bhqk', qb, k).astype(jnp.float32) * scale
        s = jnp.where(k_pos[None, :] <= q_pos[:, None], s, -jnp.inf)
        p = jax.nn.softmax(s, axis=-1).astype(v.dtype)
        return jnp.einsum('bhqk,bkhd->bqhd', p, v)

    o = sweep_causal_queries(block, q)
    return o.reshape(B, L, MLA_HEADS * MLA_V) @ w_o


def short_conv_mixer(h, w_in, w_conv, w_out):
    gate_b, gate_c, u = jnp.split(h @ w_in, 3, axis=-1)
    return (gate_b * causal_dwconv(gate_c * u, w_conv)) @ w_out


def diff_attn_mixer(h, w_in, lq1, lk1, lq2, lk2, g_sub, w_o, lambda_init):
    B, L, _ = h.shape
    H, d = DIFF_HEADS, DIFF_HEAD_DIM
    q, k, v = jnp.split(h @ w_in, 3, axis=-1)
    q = q.reshape(B, L, H, 2, d)
    k = k.reshape(B, L, H, 2, d)
    v = v.reshape(B, L, H, 2 * d)
    f32 = jnp.float32
    lam = (jnp.exp(jnp.sum(lq1.astype(f32) * lk1.astype(f32)))
           - jnp.exp(jnp.sum(lq2.astype(f32) * lk2.astype(f32))) + lambda_init)
    slopes = alibi_slopes(H)
    k_pos = jnp.arange(L, dtype=jnp.int32)
    scale = d ** -0.5

    def block(qb, q_pos):
        s = jnp.einsum('bqhmd,bkhmd->bhmqk', qb, k).astype(f32) * scale
        dist = (q_pos[:, None] - k_pos[None, :]).astype(f32)
        s = s - slopes[None, :, None, None, None] * dist
        s = jnp.where(k_pos[None, :] <= q_pos[:, None], s, -jnp.inf)
        p = jax.nn.softmax(s, axis=-1)
        a = (p[:, :, 0] - lam * p[:, :, 1]).astype(v.dtype)
        return jnp.einsum('bhqk,bkhe->bqhe', a, v)

    o = sweep_causal_queries(block, q)
    o = rms_norm(o, g_sub) * (1.0 - lambda_init)
    return o.reshape(B, L, H * 2 * d) @ w_o


def conv_glu_ffn(h, w_up, w_conv, w_down):
    g, u = jnp.split(causal_dwconv(h @ w_up, w_conv), 2, axis=-1)
    return (jax.nn.silu(g) * u) @ w_down


def setup_inputs(seed: int = 0) -> dict:
    key = jax.random.key(seed)
    ks = jax.random.split(key, 23)
    f32 = jnp.float32

    def nrm(k, shape, scale):
        return jax.random.normal(k, shape, f32) * scale

    def gain(k, shape):
        return 1.0 + 0.1 * jax.random.normal(k, shape, f32)

    D, F = D_MODEL, D_FF
    Hd = DIFF_HEADS * 2 * DIFF_HEAD_DIM
    return {
        "x": nrm(ks[0], (BATCH, SEQ, D), 1.0),
        "meta_tokens": nrm(ks[1], (N_META, D), 1.0),
        "norms": gain(ks[2], (DEPTH, 4, D)),
        "mla_w_in": nrm(ks[3], (N_A, D, MLA_Q_RANK + MLA_KV_RANK + MLA_ROPE), D ** -0.5),
        "mla_norm_q": gain(ks[4], (N_A, MLA_Q_RANK)),
        "mla_norm_kv": gain(ks[5], (N_A, MLA_KV_RANK)),
        "mla_w_uq": nrm(ks[6], (N_A, MLA_Q_RANK, MLA_HEADS * (MLA_NOPE + MLA_ROPE)), MLA_Q_RANK ** -0.5),
        "mla_w_ukv": nrm(ks[7], (N_A, MLA_KV_RANK, MLA_HEADS * (MLA_NOPE + MLA_V)), MLA_KV_RANK ** -0.5),
        "mla_w_o": nrm(ks[8], (N_A, MLA_HEADS * MLA_V, D), (MLA_HEADS * MLA_V) ** -0.5),
        "sc_w_in": nrm(ks[9], (N_B, D, 3 * D), D ** -0.5),
        "sc_conv": nrm(ks[10], (N_B, SC_WIDTH, D), SC_WIDTH ** -0.5),
        "sc_w_out": nrm(ks[11], (N_B, D, D), D ** -0.5),
        "diff_w_in": nrm(ks[12], (N_C, D, 3 * Hd), D ** -0.5),
        "diff_lambda_q1": nrm(ks[13], (N_C, DIFF_HEAD_DIM), LAMBDA_INIT_SCALE),
        "diff_lambda_k1": nrm(ks[14], (N_C, DIFF_HEAD_DIM), LAMBDA_INIT_SCALE),
        "diff_lambda_q2": nrm(ks[15], (N_C, DIFF_HEAD_DIM), LAMBDA_INIT_SCALE),
        "diff_lambda_k2": nrm(ks[16], (N_C, DIFF_HEAD_DIM), LAMBDA_INIT_SCALE),
        "diff_subln": gain(ks[17], (N_C, 2 * DIFF_HEAD_DIM)),
        "diff_w_o": nrm(ks[18], (N_C, Hd, D), Hd ** -0.5),
        "ffn_w_up": nrm(ks[19], (DEPTH, D, 2 * F), D ** -0.5),
        "ffn_conv": nrm(ks[20], (DEPTH, FFN_CONV_WIDTH, 2 * F), FFN_CONV_WIDTH ** -0.5),
        "ffn_w_down": nrm(ks[21], (DEPTH, F, D), F ** -0.5),
    }


def reference(x, meta_tokens, norms, mla_w_in, mla_norm_q, mla_norm_kv, mla_w_uq, mla_w_ukv,
              mla_w_o, sc_w_in, sc_conv, sc_w_out, diff_w_in, diff_lambda_q1, diff_lambda_k1,
              diff_lambda_q2, diff_lambda_k2, diff_subln, diff_w_o, ffn_w_up, ffn_conv, ffn_w_down):
    B = x.shape[0]
    meta = jnp.broadcast_to(meta_tokens[None].astype(x.dtype), (B, N_META, D_MODEL))
    h = jnp.concatenate([meta, x], axis=1)
    L = h.shape[1]
    cos, sin = rope_tables(L, x.dtype)
    for i in range(DEPTH):
        kind, j = i % N_MIXERS, i // N_MIXERS
        hn = rms_norm(h, norms[i, 0])
        if kind == 0:
            m = mla_mixer(hn, mla_w_in[j], mla_norm_q[j], mla_norm_kv[j], mla_w_uq[j],
                          mla_w_ukv[j], mla_w_o[j], cos, sin)
        elif kind == 1:
            m = short_conv_mixer(hn, sc_w_in[j], sc_conv[j], sc_w_out[j])
        else:
            lambda_init = 0.8 - 0.6 * math.exp(-0.3 * i)
            m = diff_attn_mixer(hn, diff_w_in[j], diff_lambda_q1[j], diff_lambda_k1[j],
                                diff_lambda_q2[j], diff_lambda_k2[j], diff_subln[j],
                                diff_w_o[j], lambda_init)
        h = h + rms_norm(m, norms[i, 1])
        f = conv_glu_ffn(rms_norm(h, norms[i, 2]), ffn_w_up[i], ffn_conv[i], ffn_w_down[i])
        h = h + rms_norm(f, norms[i, 3])
    return h[:, N_META:]
```

```python
import math
import numpy as np
import concourse.bass as bass
import concourse.mybir as mybir
from concourse.bass_utils import run_bass_kernel_spmd
from contextlib import ExitStack

F32 = mybir.dt.float32
BF16 = mybir.dt.bfloat16
ALU = mybir.AluOpType
AF = mybir.ActivationFunctionType

D = 1024
SEQ = 4096
NMETA = 16
L = SEQ + NMETA
PAD = 2
LC = L + PAD
DEPTH = 4
EPS = 1e-6
FF = 2816
NKT = 33
NCORES = 8
DUP_QK = False


class Buf:
    __slots__ = ("name", "lw", "rd", "didx", "dcnt", "dphase")

    def __init__(self, name):
        self.name = name
        self.lw = None
        self.rd = []
        self.didx = -1
        self.dcnt = 0
        self.dphase = -1


class DBuf:
    __slots__ = ("name", "wd", "rdd")

    def __init__(self, name):
        self.name = name
        self.wd = {}
        self.rdd = {}


class Op:
    __slots__ = ("eng", "idx", "fn", "deps", "is_dma", "didx", "inc")

    def __init__(self, eng, idx, fn, deps, is_dma=False, didx=-1):
        self.eng = eng
        self.idx = idx
        self.fn = fn
        self.deps = deps
        self.is_dma = is_dma
        self.didx = didx
        self.inc = False


class Prog:
    ENGS = ("pe", "act", "dve", "pool", "sp")

    def __init__(self, nc, stack, npool=72):
        self.nc = nc
        self.outer = stack
        self.scope = stack
        self.esem = {e: stack.enter_context(nc.semaphore("sem_" + e)) for e in self.ENGS}
        self.pool = [stack.enter_context(nc.semaphore("dp%d" % i)) for i in range(npool)]
        self.pool_cnt = [0] * npool
        self.pool_used = 0
        self.rank_base = {e: 0 for e in self.ENGS}
        self.waited = {e: {} for e in self.ENGS}
        self.ops = {e: [] for e in self.ENGS}
        self.phase = 0
        self.stats = []

    def begin_phase(self):
        self.scope = ExitStack()
        self.scope.__enter__()

    def sbuf(self, name, shape, dt):
        return self.scope.enter_context(self.nc.sbuf_tensor(name, shape, dt))

    def psum(self, name, shape, dt=F32):
        return self.scope.enter_context(self.nc.psum_tensor(name, shape, dt))

    def buf(self, name):
        return Buf(name)

    def bufs_n(self, name, n):
        return [self.buf("%s%d" % (name, i)) for i in range(n)]

    def _deps(self, eng, reads, writes, is_dma):
        deps = []
        for b in reads:
            if b.lw is not None:
                deps.append(b.lw)
        for b in writes:
            if b.lw is not None:
                d = b.lw
                if is_dma or d[0] == "d" or d[1] != eng:
                    deps.append(d)
            for d in b.rd:
                if is_dma or d[0] == "d" or d[1] != eng:
                    deps.append(d)
        return deps

    def op(self, eng, fn, reads=(), writes=()):
        lst = self.ops[eng]
        idx = len(lst)
        deps = self._deps(eng, reads, writes, False)
        lst.append(Op(eng, idx, fn, deps))
        me = ("c", eng, idx, self.phase)
        for b in reads:
            b.rd.append(me)
        for b in writes:
            b.lw = me
            b.rd = []

    def dma(self, eng, fn, reads=(), writes=(), track=None, dreads=(), dwrites=()):
        lst = self.ops[eng]
        idx = len(lst)
        deps = self._deps(eng, reads, writes, True)
        for db in dreads:
            for t, v in db.wd.items():
                deps.append(("d", t, v))
        for db in dwrites:
            for t, v in db.rdd.items():
                deps.append(("d", t, v))
            for t, v in db.wd.items():
                deps.append(("d", t, v))
        if track.dphase != self.phase:
            track.dphase = self.phase
            track.didx = self.pool_used
            self.pool_used += 1
            assert self.pool_used <= len(self.pool), "out of dma semaphores"
            track.dcnt = self.pool_cnt[track.didx]
        track.dcnt += 16
        self.pool_cnt[track.didx] = track.dcnt
        val = track.dcnt
        di = track.didx
        lst.append(Op(eng, idx, fn, deps, True, di))
        me = ("d", di, val)
        for b in reads:
            b.rd.append(me)
        for b in writes:
            b.lw = me
            b.rd = []
        for db in dreads:
            db.rdd[di] = max(val, db.rdd.get(di, 0))
        for db in dwrites:
            db.wd[di] = max(val, db.wd.get(di, 0))

    def end_phase(self):
        nc = self.nc
        ph = self.phase
        need = {e: set() for e in self.ENGS}
        for e in self.ENGS:
            for o in self.ops[e]:
                for d in o.deps:
                    if d[0] == "c" and d[3] == ph:
                        need[d[1]].add(d[2])
        comp = [e for e in self.ENGS if e != "sp"]
        for e in comp:
            lst = self.ops[e]
            for o in reversed(lst):
                if not o.is_dma:
                    need[e].add(o.idx)
                    break
        rank = {}
        final_rank = {}
        for e in self.ENGS:
            base = self.rank_base[e]
            srt = sorted(need[e])
            for r, idx in enumerate(srt):
                rank[(e, idx)] = base + r + 1
                self.ops[e][idx].inc = True
            final_rank[e] = base + len(srt)
        final_pool = [(i, self.pool_cnt[i]) for i in range(self.pool_used)]

        def resolve(d):
            if d[0] == "c":
                if d[3] != ph:
                    return None
                return ("e", d[1]), self.esem[d[1]], rank[(d[1], d[2])]
            return ("p", d[1]), self.pool[d[1]], d[2]

        def do_waits(h, deps, waited):
            best = {}
            for d in deps:
                r = resolve(d)
                if r is None:
                    continue
                k, s, v = r
                if waited.get(k, 0) >= v:
                    continue
                if k not in best or best[k][1] < v:
                    best[k] = (s, v)
            for k, (s, v) in best.items():
                h.wait_ge(s, v)
                waited[k] = v
            return len(best)

        stats = {}

        def run(ename, h):
            waited = self.waited[ename]
            nwait = 0
            for o in self.ops[ename]:
                nwait += do_waits(h, o.deps, waited)
                ins = o.fn(h)
                if o.is_dma:
                    ins.then_inc(self.pool[o.didx], 16)
                elif o.inc:
                    ins.then_inc(self.esem[ename], 1)
            for e2 in comp:
                if e2 != ename and final_rank[e2] > waited.get(("e", e2), 0):
                    h.wait_ge(self.esem[e2], final_rank[e2])
                    waited[("e", e2)] = final_rank[e2]
            for i, v in final_pool:
                if v > waited.get(("p", i), 0):
                    h.wait_ge(self.pool[i], v)
                    waited[("p", i)] = v
            stats[ename] = (len(self.ops[ename]), nwait)

        with nc.Block() as block:
            @block.tensor
            def _(h):
                run("pe", h)

            @block.scalar
            def _(h):
                run("act", h)

            @block.vector
            def _(h):
                run("dve", h)

            @block.gpsimd
            def _(h):
                run("pool", h)

            @block.sync
            def _(h):
                run("sp", h)

        self.stats.append(stats)
        for e in self.ENGS:
            self.rank_base[e] = final_rank[e]
            self.ops[e] = []
        self.pool_used = 0
        self.phase += 1
        if self.scope is not self.outer:
            self.scope.__exit__(None, None, None)
            self.scope = self.outer


class Rot:
    def __init__(self, P, name, n, shape, dt):
        self.t = [P.sbuf("%s%d" % (name, i), shape, dt) for i in range(n)]
        self.b = [P.buf("%s%d" % (name, i)) for i in range(n)]
        self.n = n
        self.i = 0

    def next(self):
        k = self.i % self.n
        self.i += 1
        return self.t[k], self.b[k]


SP_NORMS = 0
SP_FCONV = SP_NORMS + 128
SP_SCONV = SP_FCONV + 528
SP_NQ = SP_SCONV + 24
SP_NKV = SP_NQ + 4
SP_SUBLN = SP_NKV + 2
SP_LAM = SP_SUBLN + 1
SP_TOT = SP_LAM + 256


def pack_small(inp):
    sp = np.zeros((128, SP_TOT), np.float32)
    norms = inp["norms"]
    sp[:, SP_NORMS:SP_NORMS + 128] = norms.reshape(16, 8, 128).transpose(2, 0, 1).reshape(128, 128)
    fc = inp["ffn_conv"]
    sp[:, SP_FCONV:SP_FCONV + 528] = fc.reshape(4, 3, 44, 128).transpose(3, 0, 2, 1).reshape(128, 528)
    sc = inp["sc_conv"][0]
    sp[:, SP_SCONV:SP_SCONV + 24] = sc.reshape(3, 8, 128).transpose(2, 1, 0).reshape(128, 24)
    nq = inp["mla_norm_q"]
    sp[:, SP_NQ:SP_NQ + 4] = nq.reshape(2, 2, 128).transpose(2, 0, 1).reshape(128, 4)
    nkv = inp["mla_norm_kv"]
    sp[:, SP_NKV:SP_NKV + 2] = nkv.T
    sp[:, SP_SUBLN] = inp["diff_subln"][0]
    lam = np.stack([inp["diff_lambda_q1"][0], inp["diff_lambda_k1"][0],
                    inp["diff_lambda_q2"][0], inp["diff_lambda_k2"][0]]).reshape(1, 256)
    sp[:, SP_LAM:SP_LAM + 256] = np.broadcast_to(lam, (128, 256))
    return sp


def make_consts():
    c = {}
    tri = (np.arange(128)[:, None] <= np.arange(128)[None, :]).astype(np.float32)
    c["tri"] = tri
    c["ident"] = np.eye(128, dtype=np.float32)
    c["mneg"] = ((1.0 - tri) * -30000.0).astype(np.float32)
    inv_freq = (10000.0 ** (-np.arange(0, 32, 2, dtype=np.float32) / np.float32(32))).astype(np.float32)
    ang = (np.arange(L, dtype=np.float32)[:, None] * inv_freq[None, :]).astype(np.float32)
    cos = np.cos(ang).astype(np.float32).T
    sin = np.sin(ang).astype(np.float32).T
    C = np.ones((128, L), np.float32)
    S = np.zeros((128, L), np.float32)
    C[64:80] = cos
    C[80:96] = cos
    S[64:80] = sin
    S[80:96] = sin
    c["ropeC"] = C
    c["ropeS"] = S
    pos = np.arange(L)
    hi = (pos // 64).astype(np.float32)
    lo = (pos % 64).astype(np.float32)
    aK = np.zeros((8, 4, L), np.float32)
    aQ = np.zeros((8, 4, L), np.float32)
    for h in range(8):
        slope = 2.0 ** (-(h + 1))
        aK[h, 0] = 8.0 * slope * 64.0 * hi
        aK[h, 1] = 8.0 * slope * lo
        aK[h, 2] = 1.0
        aK[h, 3] = 1.0
        aQ[h, 0] = 1.0
        aQ[h, 1] = 1.0
        aQ[h, 2] = -8.0 * slope * 64.0 * hi
        aQ[h, 3] = -8.0 * slope * lo
    c["alibiK"] = aK.reshape(32, L)
    c["alibiQ"] = aQ.reshape(32, L)
    return c


class Ctx:
    pass


def declare_io(nc, C):
    def din(name, shape, dt=F32):
        return nc.dram_tensor(name, list(shape), dt, kind="ExternalInput").ap()
    C.h0 = din("h0", [D, LC])
    C.small = din("small", [128, SP_TOT])
    C.tri = din("tri", [128, 128])
    C.ident = din("ident", [128, 128])
    C.mneg = din("mneg", [128, 128])
    C.ropeC = din("ropeC", [128, L])
    C.ropeS = din("ropeS", [128, L])
    C.alibiK = din("alibiK", [32, L])
    C.alibiQ = din("alibiQ", [32, L])
    C.mla_w_in = din("mla_w_in", [2, D, 416])
    C.mla_w_uq = din("mla_w_uq", [2, 256, 1536])
    C.mla_w_ukv = din("mla_w_ukv", [2, 128, 2048])
    C.mla_w_o = din("mla_w_o", [2, D, D])
    C.sc_w_in = din("sc_w_in", [1, D, 3 * D])
    C.sc_w_out = din("sc_w_out", [1, D, D])
    C.diff_w_in = din("diff_w_in", [1, D, 3 * D])
    C.diff_w_o = din("diff_w_o", [1, D, D])
    C.ffn_w_up = din("ffn_w_up", [4, D, 2 * FF])
    C.ffn_w_down = din("ffn_w_down", [4, FF, D])


def setup_common(P, C):
    nc = P.nc
    C.small_t = P.sbuf("small_t", [128, SP_TOT], F32)
    C.b_small = P.buf("small")
    P.dma("sp", lambda h: h.dma_start(out=C.small_t[:], in_=C.small), writes=[C.b_small], track=C.b_small)
    C.ones_bf = P.sbuf("ones_bf", [128, 128], BF16)
    C.b_ones = P.buf("ones_bf")
    P.op("pool", lambda h: h.memset(C.ones_bf[:], 1.0), writes=[C.b_ones])
    C.ones_f = P.sbuf("ones_f", [128, 128], F32)
    C.b_onesf = P.buf("ones_f")
    P.op("pool", lambda h: h.memset(C.ones_f[:], 1.0), writes=[C.b_onesf])
    C.tri_bf = P.sbuf("tri_bf", [128, 128], BF16)
    C.b_tri = P.buf("tri_bf")
    P.dma("pool", lambda h: h.dma_start(out=C.tri_bf[:], in_=C.tri), writes=[C.b_tri], track=C.b_tri)
    C.eps_t = P.sbuf("eps_t", [128, 1], F32)
    C.b_eps = P.buf("eps_t")
    P.op("pool", lambda h: h.memset(C.eps_t[:], EPS), writes=[C.b_eps])
    C.ident_bf = P.sbuf("ident_bf", [128, 128], BF16)
    C.b_ident = P.buf("ident_bf")
    P.dma("pool", lambda h: h.dma_start(out=C.ident_bf[:], in_=C.ident), writes=[C.b_ident], track=C.b_ident)
    C.mneg_bf = P.sbuf("mneg_bf", [128, 128], BF16)
    C.b_mneg = P.buf("mneg_bf")
    P.dma("pool", lambda h: h.dma_start(out=C.mneg_bf[:], in_=C.mneg), writes=[C.b_mneg], track=C.b_mneg)
    C.ps = P.psum("ps", [128, 8, 512], F32)
    C.b_ps = P.bufs_n("psb", 8)


def gain(C, l, j, c):
    col = SP_NORMS + (l * 4 + j) * 8 + c
    return C.small_t[:, col:col + 1]


def rstd_from_psum(P, C, st_ps, b_st, n, rs_t, b_rs, dim):
    P.op("act", lambda h: h.activation(rs_t[:, :n], st_ps[:, :n], AF.Ln, bias=C.eps_t[:, 0:1], scale=1.0 / dim),
         reads=[b_st, C.b_eps], writes=[b_rs])
    P.op("act", lambda h: h.activation(rs_t[:, :n], rs_t[:, :n], AF.Exp, scale=-0.5), reads=[b_rs], writes=[b_rs])


def prenorm(P, C, W, l, j, h_in, db_in, c0, n, xn, b_xn):
    st_ps = C.ps[:, 7, :]
    b_st = C.b_ps[7]
    rs1, b_rs1 = W.rs1, W.b_rs1
    for c in range(8):
        t, bt = W.hin.next()
        P.dma("sp", lambda h, t=t, c=c: h.dma_start(out=t[:, :n], in_=h_in[c * 128:(c + 1) * 128, c0:c0 + n]),
              writes=[bt], track=bt, dreads=[db_in])
        s, bs = W.sq.next()
        P.op("act", lambda h, t=t, s=s: h.activation(s[:, :n], t[:, :n], AF.Square), reads=[bt], writes=[bs])
        P.op("pe", lambda h, s=s, c=c: h.matmul(st_ps[:, :n], C.ones_bf[:], s[:, :n], start=(c == 0), stop=(c == 7)),
             reads=[bs, C.b_ones], writes=[b_st])
    rstd_from_psum(P, C, st_ps, b_st, n, rs1, b_rs1, float(D))
    for c in range(8):
        t, bt = W.hin.next()
        P.dma("sp", lambda h, t=t, c=c: h.dma_start(out=t[:, :n], in_=h_in[c * 128:(c + 1) * 128, c0:c0 + n]),
              writes=[bt], track=bt, dreads=[db_in])
        P.op("dve", lambda h, t=t, c=c: h.scalar_tensor_tensor(xn[:, c, :n], t[:, :n], gain(C, l, j, c),
                                                                rs1[:, :n], ALU.mult, ALU.mult),
             reads=[bt, b_rs1, C.b_small], writes=[b_xn])


def tail(P, C, W, l, j, mm_chunk, nv, h_in, db_in, cin0, h_out, db_out, cout0, bank0):
    st_ps = C.ps[:, 7, :]
    b_st = C.b_ps[7]
    lag = None
    flip_f(W)
    f_t, b_f, rs2, b_rs2 = W.f, W.b_f, W.rs2, W.b_rs2
    for c in range(8):
        bk = bank0 + (c % 2)
        ps = C.ps[:, bk, :]
        bps = C.b_ps[bk]
        mm_chunk(c, ps, bps)
        P.op("act", lambda h, ps=ps, c=c: h.activation(f_t[:, c, :nv], ps[:, :nv], AF.Copy), reads=[bps], writes=[b_f])
        s, bs = W.sq.next()
        P.op("act", lambda h, ps=ps, s=s: h.activation(s[:, :nv], ps[:, :nv], AF.Square), reads=[bps], writes=[bs])
        if lag is not None:
            lag()
        lag = (lambda s=s, bs=bs, c=c: P.op(
            "pe", lambda h: h.matmul(st_ps[:, :nv], C.ones_bf[:], s[:, :nv], start=(c == 0), stop=(c == 7)),
            reads=[bs, C.b_ones], writes=[b_st]))
    lag()
    rstd_from_psum(P, C, st_ps, b_st, nv, rs2, b_rs2, float(D))
    for c in range(8):
        t, bt = W.hin.next()
        P.dma("sp", lambda h, t=t, c=c: h.dma_start(out=t[:, :nv], in_=h_in[c * 128:(c + 1) * 128, cin0:cin0 + nv]),
              writes=[bt], track=bt, dreads=[db_in])
        o, bo = W.out.next()
        P.op("dve", lambda h, o=o, c=c: h.scalar_tensor_tensor(o[:, :nv], f_t[:, c, :nv], gain(C, l, j, c),
                                                                rs2[:, :nv], ALU.mult, ALU.mult),
             reads=[b_f, b_rs2, C.b_small], writes=[bo])
        P.op("pool", lambda h, o=o, t=t: h.tensor_tensor(o[:, :nv], o[:, :nv], t[:, :nv], ALU.add),
             reads=[bo, bt], writes=[bo])
        P.dma("pool", lambda h, o=o, c=c: h.dma_start(out=h_out[c * 128:(c + 1) * 128, cout0:cout0 + nv], in_=o[:, :nv]),
              reads=[bo], track=bo, dwrites=[db_out])


class Work:
    pass


def common_work(P, W, nmax, tag, nbuf=1, need_f=True):
    W.hin = Rot(P, tag + "hin", 3, [128, nmax], F32)
    W.out = Rot(P, tag + "out", 3, [128, nmax], F32)
    W.sq = Rot(P, tag + "sq", 2, [128, nmax], BF16)
    W.xs = []
    W.fs = []
    for i in range(nbuf):
        W.xs.append((P.sbuf(tag + "xn%d" % i, [128, 8, nmax], BF16), P.buf(tag + "xn"),
                     P.sbuf(tag + "rs1%d" % i, [128, nmax], F32), P.buf(tag + "rs1")))
        if need_f:
            W.fs.append((P.sbuf(tag + "f%d" % i, [128, 8, nmax], F32), P.buf(tag + "f"),
                         P.sbuf(tag + "rs2%d" % i, [128, nmax], F32), P.buf(tag + "rs2")))
    W.xi = -1
    W.fi = -1
    flip_x(W)
    if need_f:
        flip_f(W)


def flip_x(W):
    W.xi = (W.xi + 1) % len(W.xs)
    W.xn, W.b_xn, W.rs1, W.b_rs1 = W.xs[W.xi]


def set_x(W, g):
    W.xi = g % len(W.xs)
    W.xn, W.b_xn, W.rs1, W.b_rs1 = W.xs[W.xi]


def flip_f(W):
    W.fi = (W.fi + 1) % len(W.fs)
    W.f, W.b_f, W.rs2, W.b_rs2 = W.fs[W.fi]


def load_w(P, t, b, src, kchunks, per):
    v = src.rearrange("(k p) f -> p k f", p=128)
    for k0 in range(0, kchunks, per):
        k1 = min(kchunks, k0 + per)
        P.dma("pool", lambda h, k0=k0, k1=k1: h.dma_start(out=t[:, k0:k1, :], in_=v[:, k0:k1, :]), writes=[b], track=b)


FG = 412


def ffn_phase(P, C, l, h_in, db_in, h_out, db_out, out_off, out_lo):
    nmax = FG + 2
    W = Work()
    tag = "f%d_" % l
    W.hin = Rot(P, tag + "hin", 3, [128, nmax], F32)
    W.out = Rot(P, tag + "out", 2, [128, nmax], F32)
    W.sq = Rot(P, tag + "sq", 2, [128, nmax], BF16)
    xns = [(P.sbuf(tag + "xn%d" % i, [128, 8, nmax], BF16), P.buf(tag + "xn")) for i in range(2)]
    rs1 = P.sbuf(tag + "rs1", [128, nmax], F32)
    b_rs1 = P.buf(tag + "rs1")
    rs2 = P.sbuf(tag + "rs2", [128, nmax], F32)
    b_rs2 = P.buf(tag + "rs2")
    f_t = P.sbuf(tag + "f", [128, 8, nmax], F32)
    b_f = P.buf(tag + "f")
    wup = P.sbuf(tag + "wup", [128, 8, 2 * FF], BF16)
    b_wup = P.buf(tag + "wup")
    wdn = P.sbuf(tag + "wdn", [128, 22, D], BF16)
    b_wdn = P.buf(tag + "wdn")
    load_w(P, wup, b_wup, C.ffn_w_up[l], 8, 1)
    load_w(P, wdn, b_wdn, C.ffn_w_down[l], 22, 6)
    gc = Rot(P, tag + "gc", 2, [128, nmax], F32)
    uc = Rot(P, tag + "uc", 2, [128, nmax], F32)
    act = P.sbuf(tag + "act", [128, 22, nmax], BF16)
    b_act = P.buf(tag + "act")
    st1 = C.ps[:, 7, :]
    b_st1 = C.b_ps[7]
    st2 = C.ps[:, 6, :]
    b_st2 = C.b_ps[6]

    def cw(ch, j):
        col = SP_FCONV + ((l * 44) + ch) * 3 + j
        return C.small_t[:, col:col + 1]

    ngroups = (L + FG - 1) // FG

    def gdims(g):
        c0 = FG * g
        n = min(nmax, LC - c0)
        return c0, n, n - 2

    def pre_sq(g, c):
        c0, n, nv = gdims(g)
        t, bt = W.hin.next()
        P.dma("sp", lambda h: h.dma_start(out=t[:, :n], in_=h_in[c * 128:(c + 1) * 128, c0:c0 + n]),
              writes=[bt], track=bt, dreads=[db_in])
        s_, bs = W.sq.next()
        P.op("act", lambda h: h.activation(s_[:, :n], t[:, :n], AF.Square), reads=[bt], writes=[bs])
        return (s_, bs)

    def pre_mm(g, c, sb):
        c0, n, nv = gdims(g)
        s_, bs = sb
        P.op("pe", lambda h: h.matmul(st1[:, :n], C.ones_bf[:], s_[:, :n], start=(c == 0), stop=(c == 7)),
             reads=[bs, C.b_ones], writes=[b_st1])

    def pre_chain(g):
        c0, n, nv = gdims(g)
        rstd_from_psum(P, C, st1, b_st1, n, rs1, b_rs1, float(D))

    def pre_xn(g, c):
        c0, n, nv = gdims(g)
        xn, b_xn = xns[g % 2]
        t, bt = W.hin.next()
        P.dma("sp", lambda h: h.dma_start(out=t[:, :n], in_=h_in[c * 128:(c + 1) * 128, c0:c0 + n]),
              writes=[bt], track=bt, dreads=[db_in])
        P.op("dve", lambda h: h.scalar_tensor_tensor(xn[:, c, :n], t[:, :n], gain(C, l, 2, c), rs1[:, :n], ALU.mult, ALU.mult),
             reads=[bt, b_rs1, C.b_small], writes=[b_xn])

    def pre_hooks(g):
        hk = {}
        sqs = {}

        def mk_sq(c):
            def f():
                sqs[c] = pre_sq(g, c)
            return f

        def mk_mm(c):
            return lambda: pre_mm(g, c, sqs[c])

        for c in range(8):
            hk.setdefault(4 + c, []).append(mk_sq(c))
            hk.setdefault(5 + c, []).append(mk_mm(c))
        hk.setdefault(12, []).append(lambda: pre_chain(g))
        for c in range(8):
            hk.setdefault(13 + c, []).append(lambda c=c: pre_xn(g, c))
        return hk

    def tail_chunk(g, c):
        c0, n, nv = gdims(g)
        p0 = c0
        skip = max(p0, out_lo) - p0
        cin0 = c0 + 2
        cout0 = p0 + out_off
        t, bt = W.hin.next()
        P.dma("sp", lambda h: h.dma_start(out=t[:, :nv], in_=h_in[c * 128:(c + 1) * 128, cin0:cin0 + nv]),
              writes=[bt], track=bt, dreads=[db_in])
        o, bo = W.out.next()
        P.op("dve", lambda h: h.scalar_tensor_tensor(o[:, :nv], f_t[:, c, :nv], gain(C, l, 3, c), rs2[:, :nv], ALU.mult, ALU.mult),
             reads=[b_f, b_rs2, C.b_small], writes=[bo])
        P.op("pool", lambda h: h.tensor_tensor(o[:, :nv], o[:, :nv], t[:, :nv], ALU.add), reads=[bo, bt], writes=[bo])
        P.dma("pool", lambda h: h.dma_start(out=h_out[c * 128:(c + 1) * 128, cout0 + skip:cout0 + nv], in_=o[:, skip:nv]),
              reads=[bo], track=bo, dwrites=[db_out])

    def mid(g, hooks):
        c0, n, nv = gdims(g)
        xn, b_xn = xns[g % 2]
        for jp in range(22):
            banks = [(jp % 3) * 2, (jp % 3) * 2 + 1]
            tiles = []
            for which, ch in enumerate((jp, 22 + jp)):
                ps = C.ps[:, banks[which], :]
                bps = C.b_ps[banks[which]]
                for k in range(8):
                    P.op("pe", lambda h, ps=ps, k=k, ch=ch: h.matmul(ps[:, :n], wup[:, k, ch * 128:(ch + 1) * 128],
                                                                      xn[:, k, :n], start=(k == 0), stop=(k == 7)),
                         reads=[b_wup, b_xn], writes=[bps])
                tiles.append((ps, bps, ch))
            g_t, b_g = gc.next()
            u_t, b_u = uc.next()
            for (ps, bps, ch), (d_t, b_d) in zip(tiles, ((g_t, b_g), (u_t, b_u))):
                P.op("act", lambda h, ps=ps, ch=ch, d_t=d_t: h.activation(d_t[:, :nv], ps[:, 2:n], AF.Copy, scale=cw(ch, 2)),
                     reads=[bps, C.b_small], writes=[b_d])
            for tap in (1, 0):
                for (ps, bps, ch), (d_t, b_d) in zip(tiles, ((g_t, b_g), (u_t, b_u))):
                    P.op("dve", lambda h, ps=ps, ch=ch, d_t=d_t, tap=tap: h.scalar_tensor_tensor(
                        d_t[:, :nv], ps[:, tap:tap + nv], cw(ch, tap), d_t[:, :nv], ALU.mult, ALU.add),
                        reads=[bps, b_d, C.b_small], writes=[b_d])
            P.op("act", lambda h, g_t=g_t: h.activation(g_t[:, :nv], g_t[:, :nv], AF.Silu), reads=[b_g], writes=[b_g])
            P.op("pool", lambda h, g_t=g_t, u_t=u_t, jp=jp: h.tensor_tensor(act[:, jp, :nv], g_t[:, :nv], u_t[:, :nv], ALU.mult),
                 reads=[b_g, b_u], writes=[b_act])
            for fn in hooks.get(jp, ()):
                fn()

    def back(g):
        c0, n, nv = gdims(g)
        lag = None
        for c in range(8):
            bk = 2 + (c % 2)
            ps = C.ps[:, bk, :]
            bps = C.b_ps[bk]
            for k in range(22):
                P.op("pe", lambda h, ps=ps, k=k, c=c: h.matmul(ps[:, :nv], wdn[:, k, c * 128:(c + 1) * 128], act[:, k, :nv],
                                                                start=(k == 0), stop=(k == 21)),
                     reads=[b_wdn, b_act], writes=[bps])
            P.op("act", lambda h, ps=ps, c=c: h.activation(f_t[:, c, :nv], ps[:, :nv], AF.Copy), reads=[bps], writes=[b_f])
            s_, bs = W.sq.next()
            P.op("act", lambda h, ps=ps, s_=s_: h.activation(s_[:, :nv], ps[:, :nv], AF.Square), reads=[bps], writes=[bs])
            if lag is not None:
                lag()
            lag = (lambda s_=s_, bs=bs, c=c: P.op(
                "pe", lambda h: h.matmul(st2[:, :nv], C.ones_bf[:], s_[:, :nv], start=(c == 0), stop=(c == 7)),
                reads=[bs, C.b_ones], writes=[b_st2]))
        lag()
        rstd_from_psum(P, C, st2, b_st2, nv, rs2, b_rs2, float(D))

    for c in range(8):
        pre_mm(0, c, pre_sq(0, c))
    pre_chain(0)
    for c in range(8):
        pre_xn(0, c)
    for g in range(ngroups):
        hooks = pre_hooks(g + 1) if g + 1 < ngroups else {}
        if g > 0:
            for c in range(8):
                hooks.setdefault(c, []).insert(0, (lambda c=c, gg=g - 1: tail_chunk(gg, c)))
        mid(g, hooks)
        back(g)
    for c in range(8):
        tail_chunk(ngroups - 1, c)


def tail_skip(P, C, W, l, j, mm_chunk, nv, skip, h_in, db_in, cin0, h_out, db_out, cout0, bank0):
    st_ps = C.ps[:, 7, :]
    b_st = C.b_ps[7]
    lag = None
    flip_f(W)
    f_t, b_f, rs2, b_rs2 = W.f, W.b_f, W.rs2, W.b_rs2
    for c in range(8):
        bk = bank0 + (c % 2)
        ps = C.ps[:, bk, :]
        bps = C.b_ps[bk]
        mm_chunk(c, ps, bps)
        P.op("act", lambda h, ps=ps, c=c: h.activation(f_t[:, c, :nv], ps[:, :nv], AF.Copy), reads=[bps], writes=[b_f])
        s, bs = W.sq.next()
        P.op("act", lambda h, ps=ps, s=s: h.activation(s[:, :nv], ps[:, :nv], AF.Square), reads=[bps], writes=[bs])
        if lag is not None:
            lag()
        lag = (lambda s=s, bs=bs, c=c: P.op(
            "pe", lambda h: h.matmul(st_ps[:, :nv], C.ones_bf[:], s[:, :nv], start=(c == 0), stop=(c == 7)),
            reads=[bs, C.b_ones], writes=[b_st]))
    lag()
    rstd_from_psum(P, C, st_ps, b_st, nv, rs2, b_rs2, float(D))
    for c in range(8):
        t, bt = W.hin.next()
        P.dma("sp", lambda h, t=t, c=c: h.dma_start(out=t[:, :nv], in_=h_in[c * 128:(c + 1) * 128, cin0:cin0 + nv]),
              writes=[bt], track=bt, dreads=[db_in])
        o, bo = W.out.next()
        P.op("dve", lambda h, o=o, c=c: h.scalar_tensor_tensor(o[:, :nv], f_t[:, c, :nv], gain(C, l, j, c),
                                                                rs2[:, :nv], ALU.mult, ALU.mult),
             reads=[b_f, b_rs2, C.b_small], writes=[bo])
        P.op("pool", lambda h, o=o, t=t: h.tensor_tensor(o[:, :nv], o[:, :nv], t[:, :nv], ALU.add),
             reads=[bo, bt], writes=[bo])
        P.dma("sp", lambda h, o=o, c=c: h.dma_start(out=h_out[c * 128:(c + 1) * 128, cout0 + skip:cout0 + nv],
                                                     in_=o[:, skip:nv]),
              reads=[bo], track=bo, dwrites=[db_out])


def proj_phase(P, C, l, w_src, oT, db_o, h_in, db_in, h_out, db_out, tag):
    W = Work()
    common_work(P, W, 512, tag, nbuf=2)
    wo = P.sbuf(tag + "wo", [128, 8, D], BF16)
    b_wo = P.buf(tag + "wo")
    load_w(P, wo, b_wo, w_src, 8, 4)
    og = Rot(P, tag + "og", 2, [128, 8, 512], BF16)
    ov = oT.rearrange("(c p) t -> p c t", p=128)

    def group(g):
        t0 = 512 * g
        n = min(512, L - t0)
        o_t, b_o = og.next()
        P.dma("sp", lambda h: h.dma_start(out=o_t[:, :, :n], in_=ov[:, :, t0:t0 + n]), writes=[b_o], track=b_o, dreads=[db_o])

        def mm_chunk(c, ps, bps):
            for k in range(8):
                P.op("pe", lambda h, k=k: h.matmul(ps[:, :n], wo[:, k, c * 128:(c + 1) * 128], o_t[:, k, :n],
                                                   start=(k == 0), stop=(k == 7)),
                     reads=[b_wo, b_o], writes=[bps])

        tail(P, C, W, l, 1, mm_chunk, n, h_in, db_in, PAD + t0, h_out, db_out, PAD + t0, 4)

    for g in range((L + 511) // 512):
        group(g)


SG = 510


def sc_phase(P, C, l, h_in, db_in, h_out, db_out):
    tag = "sc_"
    nmax = SG + 2
    W = Work()
    common_work(P, W, nmax, tag, nbuf=2)
    win = P.sbuf(tag + "win", [128, 8, 3 * D], BF16)
    b_win = P.buf(tag + "win")
    wout = P.sbuf(tag + "wout", [128, 8, D], BF16)
    b_wout = P.buf(tag + "wout")
    load_w(P, win, b_win, C.sc_w_in[0], 8, 1)
    load_w(P, wout, b_wout, C.sc_w_out[0], 8, 4)
    ub = Rot(P, tag + "ub", 2, [128, nmax], F32)
    cu = Rot(P, tag + "cu", 2, [128, nmax], F32)
    yy = Rot(P, tag + "yy", 2, [128, nmax], F32)
    yb = P.sbuf(tag + "yb", [128, 8, nmax], BF16)
    b_yb = P.buf(tag + "yb")

    def cw(c, j):
        col = SP_SCONV + c * 3 + j
        return C.small_t[:, col:col + 1]

    def gdims(g):
        c0 = SG * g
        n = min(nmax, LC - c0)
        return c0, n, n - 2

    def front(g):
        c0, n, nv = gdims(g)
        set_x(W, g)
        prenorm(P, C, W, l, 0, h_in, db_in, c0, n, W.xn, W.b_xn)

    def mid(g):
        c0, n, nv = gdims(g)
        xn, b_xn = W.xs[g % 2][0], W.xs[g % 2][1]
        for i in range(8):
            bks = [(i % 2) * 3 + q for q in range(3)]
            pss = []
            for q in range(3):
                ps = C.ps[:, bks[q], :]
                bps = C.b_ps[bks[q]]
                col = q * D + i * 128
                for k in range(8):
                    P.op("pe", lambda h, ps=ps, k=k, col=col: h.matmul(ps[:, :n], win[:, k, col:col + 128], xn[:, k, :n],
                                                                        start=(k == 0), stop=(k == 7)),
                         reads=[b_win, b_xn], writes=[bps])
                pss.append((ps, bps))
            (pb, bpb), (pc, bpc), (pu, bpu) = pss
            u_t, b_u = ub.next()
            c_t, b_c = cu.next()
            y_t, b_y = yy.next()
            P.op("act", lambda h, pu=pu, u_t=u_t: h.activation(u_t[:, :n], pu[:, :n], AF.Copy), reads=[bpu], writes=[b_u])
            P.op("dve", lambda h, pc=pc, u_t=u_t, c_t=c_t: h.tensor_tensor(c_t[:, :n], pc[:, :n], u_t[:, :n], ALU.mult),
                 reads=[bpc, b_u], writes=[b_c])
            P.op("act", lambda h, c_t=c_t, y_t=y_t, i=i: h.activation(y_t[:, :nv], c_t[:, 2:n], AF.Copy, scale=cw(i, 2)),
                 reads=[b_c, C.b_small], writes=[b_y])
            for tap in (1, 0):
                P.op("dve", lambda h, c_t=c_t, y_t=y_t, i=i, tap=tap: h.scalar_tensor_tensor(
                    y_t[:, :nv], c_t[:, tap:tap + nv], cw(i, tap), y_t[:, :nv], ALU.mult, ALU.add),
                    reads=[b_c, b_y, C.b_small], writes=[b_y])
            P.op("dve", lambda h, pb=pb, y_t=y_t, i=i: h.tensor_tensor(yb[:, i, :nv], pb[:, 2:n], y_t[:, :nv], ALU.mult),
                 reads=[bpb, b_y], writes=[b_yb])

    def back(g):
        c0, n, nv = gdims(g)

        def mm_chunk(c, ps, bps):
            for k in range(8):
                P.op("pe", lambda h, k=k: h.matmul(ps[:, :nv], wout[:, k, c * 128:(c + 1) * 128], yb[:, k, :nv],
                                                   start=(k == 0), stop=(k == 7)),
                     reads=[b_wout, b_yb], writes=[bps])

        tail(P, C, W, l, 1, mm_chunk, nv, h_in, db_in, c0 + 2, h_out, db_out, c0 + 2, 4)

    ngroups = (L + SG - 1) // SG
    front(0)
    for g in range(ngroups):
        if g + 1 < ngroups:
            front(g + 1)
        mid(g)
        back(g)


def mla_phase(P, C, l, j, h_in, db_in, oT, db_o):
    tag = "ml%d_" % l
    W = Work()
    W.hin = Rot(P, tag + "hin", 3, [128, 512], F32)
    W.sq = Rot(P, tag + "sq", 2, [128, 512], BF16)
    W.xs = [(P.sbuf(tag + "xn%d" % i, [128, 8, 512], BF16), P.buf(tag + "xn"),
             P.sbuf(tag + "rs1%d" % i, [128, 512], F32), P.buf(tag + "rs1")) for i in range(2)]
    W.xi = -1
    rsq = P.sbuf(tag + "rsq", [128, 512], F32)
    b_rsq = P.buf("rsq")
    rskv = P.sbuf(tag + "rskv", [128, 512], F32)
    b_rskv = P.buf("rskv")
    win = P.sbuf(tag + "win", [128, 8, 416], BF16)
    b_win = P.buf("win")
    load_w(P, win, b_win, C.mla_w_in[j], 8, 8)
    wkr = P.sbuf(tag + "wkr", [128, 8, 96], BF16)
    b_wkr = P.buf("wkr")
    wkrr = P.sbuf(tag + "wkrr", [128, 8, 96], BF16)
    b_wkrr = P.buf("wkrr")
    P.op("pool", lambda h: h.memset(wkr[:], 0.0), writes=[b_wkr])
    P.op("pool", lambda h: h.memset(wkrr[:], 0.0), writes=[b_wkrr])
    P.op("dve", lambda h: h.tensor_copy(wkr[:, :, 64:96], win[:, :, 384:416]), reads=[b_win], writes=[b_wkr])
    P.op("dve", lambda h: h.tensor_scalar(wkrr[:, :, 64:80], win[:, :, 400:416], -1.0, None, ALU.mult), reads=[b_win], writes=[b_wkrr])
    P.op("dve", lambda h: h.tensor_copy(wkrr[:, :, 80:96], win[:, :, 384:400]), reads=[b_win], writes=[b_wkrr])
    wuq = P.sbuf(tag + "wuq", [128, 2, 16, 96], BF16)
    b_wuq = P.buf("wuq")
    wuqr = P.sbuf(tag + "wuqr", [128, 2, 16, 96], BF16)
    b_wuqr = P.buf("wuqr")
    for c in range(2):
        src = C.mla_w_uq[j][c * 128:(c + 1) * 128, :].rearrange("p (h e) -> p h e", e=96)
        P.dma("pool", lambda h, c=c, src=src: h.dma_start(out=wuq[:, c, :, :], in_=src), writes=[b_wuq], track=b_wuq)
    P.op("pool", lambda h: h.memset(wuqr[:], 0.0), writes=[b_wuqr])
    for c in range(2):
        P.op("dve", lambda h, c=c: h.tensor_scalar(wuqr[:, c, :, 64:80], wuq[:, c, :, 80:96], -1.0, None, ALU.mult),
             reads=[b_wuq], writes=[b_wuqr])
        P.op("dve", lambda h, c=c: h.tensor_copy(wuqr[:, c, :, 80:96], wuq[:, c, :, 64:80]), reads=[b_wuq], writes=[b_wuqr])
    wukv = P.sbuf(tag + "wukv", [128, 2048], BF16)
    b_wukv = P.buf("wukv")
    P.dma("pool", lambda h: h.dma_start(out=wukv[:], in_=C.mla_w_ukv[j]), writes=[b_wukv], track=b_wukv)
    rc = P.sbuf(tag + "rc", [128, L], F32)
    b_rc = P.buf("rc")
    rsn = P.sbuf(tag + "rsn", [128, L], F32)
    b_rsn = P.buf("rsn")
    P.dma("sp", lambda h: h.dma_start(out=rc[:], in_=C.ropeC), writes=[b_rc], track=b_rc)
    P.dma("sp", lambda h: h.dma_start(out=rsn[:], in_=C.ropeS), writes=[b_rsn], track=b_rsn)
    cqn = P.sbuf(tag + "cqn", [128, 2, L], BF16)
    b_cqn = P.buf("cqn")
    ckvn = P.sbuf(tag + "ckvn", [128, L], BF16)
    b_ckvn = P.buf("ckvn")
    KT = [P.sbuf(tag + "KT%d" % i, [128, L], BF16) for i in range(2)]
    b_KT = P.bufs_n("KT", 2)
    QT = [P.sbuf(tag + "QT%d" % i, [128, L], BF16) for i in range(2)]
    b_QT = P.bufs_n("QT", 2)
    VT = [P.sbuf(tag + "VT%d" % i, [128, NKT, 65], BF16) for i in range(2)]
    b_VT = P.bufs_n("VT", 2)
    for i in range(2):
        P.op("pool", lambda h, i=i: h.memset(VT[i][:, :, 64:65], 1.0), writes=[b_VT[i]])
    t1 = Rot(P, tag + "t1", 2, [128, 512], F32)
    t2 = Rot(P, tag + "t2", 2, [128, 512], F32)
    pT = Rot(P, tag + "pT", 6, [128, 512], BF16)
    osb = Rot(P, tag + "osb", 3, [128, 512], F32)
    rsum = Rot(P, tag + "rsum", 3, [128, 512], F32)
    onb = Rot(P, tag + "onb", 3, [64, 512], BF16)

    def nq_col(c):
        col = SP_NQ + j * 2 + c
        return C.small_t[:, col:col + 1]

    nkv_col = C.small_t[:, SP_NKV + j:SP_NKV + j + 1]
    ngr = (L + 511) // 512

    def frontA(g):
        t0 = 512 * g
        n = min(512, L - t0)
        flip_x(W)
        prenorm(P, C, W, l, 0, h_in, db_in, PAD + t0, n, W.xn, W.b_xn)

    def groupA(g):
        t0 = 512 * g
        n = min(512, L - t0)
        xn, b_xn = W.xs[g % 2][0], W.xs[g % 2][1]
        specs = [(0, win, b_win, 0, 128, 128), (1, win, b_win, 128, 128, 128), (2, win, b_win, 256, 128, 128)]
        for bk, wt, bw, col, m, _ in specs:
            ps = C.ps[:, bk, :]
            for k in range(8):
                P.op("pe", lambda h, ps=ps, k=k, col=col: h.matmul(ps[:, :n], win[:, k, col:col + 128], xn[:, k, :n],
                                                                    start=(k == 0), stop=(k == 7)),
                     reads=[b_win, b_xn], writes=[C.b_ps[bk]])
        for bk, wt, bw in ((3, wkr, b_wkr), (4, wkrr, b_wkrr)):
            ps = C.ps[:, bk, :]
            for k in range(8):
                P.op("pe", lambda h, ps=ps, k=k, wt=wt: h.matmul(ps[0:96, :n], wt[:, k, :], xn[:, k, :n],
                                                                  start=(k == 0), stop=(k == 7)),
                     reads=[bw, b_xn], writes=[C.b_ps[bk]])
        for c in range(2):
            s, bs = W.sq.next()
            P.op("act", lambda h, s=s, c=c: h.activation(s[:, :n], C.ps[:, c, :n], AF.Square), reads=[C.b_ps[c]], writes=[bs])
            P.op("pe", lambda h, s=s, c=c: h.matmul(C.ps[:, 5, :n], C.ones_bf[:], s[:, :n], start=(c == 0), stop=(c == 1)),
                 reads=[bs, C.b_ones], writes=[C.b_ps[5]])
        rstd_from_psum(P, C, C.ps[:, 5, :], C.b_ps[5], n, rsq, b_rsq, 256.0)
        for c in range(2):
            P.op("dve", lambda h, c=c: h.scalar_tensor_tensor(cqn[:, c, t0:t0 + n], C.ps[:, c, :n], nq_col(c), rsq[:, :n],
                                                              ALU.mult, ALU.mult),
                 reads=[C.b_ps[c], b_rsq, C.b_small], writes=[b_cqn])
        s, bs = W.sq.next()
        P.op("act", lambda h, s=s: h.activation(s[:, :n], C.ps[:, 2, :n], AF.Square), reads=[C.b_ps[2]], writes=[bs])
        P.op("pe", lambda h, s=s: h.matmul(C.ps[:, 6, :n], C.ones_bf[:], s[:, :n], start=True, stop=True),
             reads=[bs, C.b_ones], writes=[C.b_ps[6]])
        rstd_from_psum(P, C, C.ps[:, 6, :], C.b_ps[6], n, rskv, b_rskv, 128.0)
        P.op("dve", lambda h: h.scalar_tensor_tensor(ckvn[:, t0:t0 + n], C.ps[:, 2, :n], nkv_col, rskv[:, :n], ALU.mult, ALU.mult),
             reads=[C.b_ps[2], b_rskv, C.b_small], writes=[b_ckvn])
        a_t, b_a = t1.next()
        b_t, b_b = t2.next()
        P.op("dve", lambda h: h.tensor_tensor(a_t[64:96, :n], C.ps[64:96, 3, :n], rc[64:96, t0:t0 + n], ALU.mult),
             reads=[C.b_ps[3], b_rc], writes=[b_a])
        P.op("dve", lambda h: h.tensor_tensor(b_t[64:96, :n], C.ps[64:96, 4, :n], rsn[64:96, t0:t0 + n], ALU.mult),
             reads=[C.b_ps[4], b_rsn], writes=[b_b])
        for i in range(2):
            P.op("pool", lambda h, i=i: h.tensor_tensor(KT[i][64:96, t0:t0 + n], a_t[64:96, :n], b_t[64:96, :n], ALU.add),
                 reads=[b_a, b_b], writes=[b_KT[i]])

    frontA(0)
    for g in range(ngr):
        if g + 1 < ngr:
            frontA(g + 1)
        groupA(g)

    scale = 96.0 ** -0.5

    def head_proj_v(hd, i, kt):
        nk = min(128, L - 128 * kt)
        bk = (3, 6)[kt % 2]
        P.op("pe", lambda h: h.matmul(C.ps[0:nk, bk, 0:64], ckvn[:, kt * 128:kt * 128 + nk],
                                      wukv[:, hd * 128 + 64:hd * 128 + 128], start=True, stop=True),
             reads=[b_ckvn, b_wukv], writes=[C.b_ps[bk]])
        P.op("dve", lambda h: h.tensor_copy(VT[i][0:nk, kt, 0:64], C.ps[0:nk, bk, 0:64]),
             reads=[C.b_ps[bk]], writes=[b_VT[i]])

    def head_proj_items(hd, i):
        items = []
        for g in range(ngr):
            items.append(lambda g=g: head_proj_g(hd, i, g))
        for kt0 in range(0, NKT, 2):
            def vv(kt0=kt0):
                for kt in range(kt0, min(NKT, kt0 + 2)):
                    head_proj_v(hd, i, kt)
            items.append(vv)
        return items

    def head_proj(hd, i):
        for it in head_proj_items(hd, i):
            it()

    def head_proj_g(hd, i, g):
        t0 = 512 * g
        n = min(512, L - t0)
        for bk, wt, bw in ((3, wuq, b_wuq), (6, wuqr, b_wuqr)):
            for c in range(2):
                P.op("pe", lambda h, bk=bk, wt=wt, c=c: h.matmul(C.ps[0:96, bk, :n], wt[:, c, hd, :], cqn[:, c, t0:t0 + n],
                                                                  start=(c == 0), stop=(c == 1)),
                     reads=[bw, b_cqn], writes=[C.b_ps[bk]])
        a_t, b_a = t1.next()
        b_t, b_b = t2.next()
        P.op("dve", lambda h: h.tensor_tensor(a_t[0:96, :n], C.ps[0:96, 3, :n], rc[0:96, t0:t0 + n], ALU.mult),
             reads=[C.b_ps[3], b_rc], writes=[b_a])
        P.op("dve", lambda h: h.tensor_tensor(b_t[0:96, :n], C.ps[0:96, 6, :n], rsn[0:96, t0:t0 + n], ALU.mult),
             reads=[C.b_ps[6], b_rsn], writes=[b_b])
        P.op("pool", lambda h: h.tensor_tensor(QT[i][0:96, t0:t0 + n], a_t[0:96, :n], b_t[0:96, :n], ALU.add),
             reads=[b_a, b_b], writes=[b_QT[i]])
        P.op("pe", lambda h: h.matmul(C.ps[0:64, 7, :n], wukv[:, hd * 128:hd * 128 + 64], ckvn[:, t0:t0 + n], start=True, stop=True),
             reads=[b_wukv, b_ckvn], writes=[C.b_ps[7]])
        P.op("dve", lambda h: h.tensor_copy(KT[i][0:64, t0:t0 + n], C.ps[0:64, 7, :n]), reads=[C.b_ps[7]], writes=[b_KT[i]])

    cnt = {"st": 0, "ob": 0}
    LA = 2

    def norm_a(hd, q0, nq, ob):
        o_t, b_o = osb.next()
        r_t, b_r = rsum.next()
        P.op("dve", lambda h: h.tensor_copy(o_t[0:65, :nq], C.ps[0:65, ob, :nq]), reads=[C.b_ps[ob]], writes=[b_o])
        P.op("dve", lambda h: h.reciprocal(r_t[64:65, :nq], o_t[64:65, :nq]), reads=[b_o], writes=[b_r])
        return (hd, q0, nq, o_t, b_o, r_t, b_r)

    def norm_b(st):
        hd, q0, nq, o_t, b_o, r_t, b_r = st
        n_t, b_n = onb.next()
        P.op("pe", lambda h: h.matmul(C.ps[0:64, 7, :nq], C.ones_f[64:65, 0:64], r_t[64:65, :nq], start=True, stop=True),
             reads=[C.b_onesf, b_r], writes=[C.b_ps[7]])
        P.op("dve", lambda h: h.tensor_tensor(n_t[0:64, :nq], C.ps[0:64, 7, :nq], o_t[0:64, :nq], ALU.mult),
             reads=[C.b_ps[7], b_o], writes=[b_n])
        P.dma("sp", lambda h: h.dma_start(out=oT[hd * 64:(hd + 1) * 64, q0:q0 + nq], in_=n_t[0:64, :nq]),
              reads=[b_n], track=b_n, dwrites=[db_o])

    def blk_front(i, qg, q0, nq, kt):
        nk = min(128, L - 128 * kt)
        jj = kt - 4 * qg
        cs = 0 if (jj < 0 or qg == 8) else 128 * jj
        diag = (jj >= 0) if qg < 8 else (kt == NKT - 1)
        ncol = nq - cs
        sb = cnt["st"] % 3
        cnt["st"] += 1
        if DUP_QK and ncol >= 256:
            P.op("pe", lambda h: h.matmul(C.ps[0:nk, sb, 0:ncol], KT[i][0:96, kt * 128:kt * 128 + nk],
                                          QT[i][0:96, q0 + cs:q0 + nq], start=True, stop=True),
                 reads=[b_KT[i], b_QT[i]], writes=[C.b_ps[sb]])
        P.op("pe", lambda h: h.matmul(C.ps[0:nk, sb, 0:ncol], KT[i][0:96, kt * 128:kt * 128 + nk],
                                      QT[i][0:96, q0 + cs:q0 + nq], start=True, stop=not diag),
             reads=[b_KT[i], b_QT[i]], writes=[C.b_ps[sb]])
        if diag:
            dc = min(128, ncol)
            P.op("pe", lambda h: h.matmul(C.ps[0:nk, sb, 0:dc], C.ident_bf[0:nk, 0:nk], C.mneg_bf[0:nk, 0:dc],
                                          start=False, stop=True),
                 reads=[C.b_ident, C.b_mneg], writes=[C.b_ps[sb]])
        p_t, b_p = pT.next()
        P.op("act", lambda h: h.activation(p_t[0:nk, 0:ncol], C.ps[0:nk, sb, 0:ncol], AF.Exp, scale=scale),
             reads=[C.b_ps[sb]], writes=[b_p])
        return (nk, cs, ncol, p_t, b_p)

    def blk_back(i, nq, ob, kt, first, last, fr):
        nk, cs, ncol, p_t, b_p = fr
        P.op("pe", lambda h: h.matmul(C.ps[0:65, ob, cs:nq], VT[i][0:nk, kt, 0:65], p_t[0:nk, 0:ncol],
                                      start=first, stop=last),
             reads=[b_VT[i], b_p], writes=[C.b_ps[ob]])

    def attn_head(hd, i, items):
        blocks = []
        for qg in range(ngr):
            q0 = 512 * qg
            nq = min(512, L - q0)
            ob = 4 + (cnt["ob"] % 2)
            cnt["ob"] += 1
            kts = list(range(0, min(4 * qg + 4, NKT)))
            for kt in kts:
                blocks.append((qg, q0, nq, ob, kt, kt == kts[0], kt == kts[-1]))
        pend = []
        norms = []

        def retire():
            (qg, q0, nq, ob, kt, first, last), fr = pend.pop(0)
            blk_back(i, nq, ob, kt, first, last, fr)
            for nm in norms:
                nm[1] -= 1
            while norms and norms[0][1] <= 0:
                norm_b(norms.pop(0)[0])
            if last:
                norms.append([norm_a(hd, q0, nq, ob), 12])

        every = max(1, (len(blocks) - 8) // max(1, len(items)))
        for bi, b in enumerate(blocks):
            pend.append((b, blk_front(i, b[0], b[1], b[2], b[4])))
            if len(pend) > LA:
                retire()
            if items and bi % every == every - 1:
                items.pop(0)()
        while pend:
            retire()
        while norms:
            norm_b(norms.pop(0)[0])
        while items:
            items.pop(0)()

    head_proj(0, 0)
    for hd in range(16):
        i = hd % 2
        items = head_proj_items(hd + 1, 1 - i) if hd + 1 < 16 else []
        attn_head(hd, i, items)


def diff_phase_a(P, C, l, h_in, db_in, qT, db_q, kT, db_k, vtok, db_v):
    tag = "da_"
    W = Work()
    W.hin = Rot(P, tag + "hin", 3, [128, 512], F32)
    W.sq = Rot(P, tag + "sq", 2, [128, 512], BF16)
    W.xs = [(P.sbuf(tag + "xn%d" % i, [128, 8, 512], BF16), P.buf(tag + "xn"),
             P.sbuf(tag + "rs1%d" % i, [128, 512], F32), P.buf(tag + "rs1")) for i in range(2)]
    W.xi = -1
    win = P.sbuf(tag + "win", [128, 8, 3 * D], BF16)
    b_win = P.buf("win")
    load_w(P, win, b_win, C.diff_w_in[0], 8, 1)
    stg = Rot(P, tag + "stg", 2, [128, 16, 512], BF16)
    vst = Rot(P, tag + "vst", 2, [128, 4, D], BF16)
    qv = qT.rearrange("(c p) t -> p c t", p=128)
    kv = kT.rearrange("(c p) t -> p c t", p=128)
    cnt = {"b": 0}

    def front(g):
        t0 = 512 * g
        n = min(512, L - t0)
        flip_x(W)
        prenorm(P, C, W, l, 0, h_in, db_in, PAD + t0, n, W.xn, W.b_xn)

    def group(g):
        t0 = 512 * g
        n = min(512, L - t0)
        xn, b_xn = W.xs[g % 2][0], W.xs[g % 2][1]
        s_t, b_s = stg.next()
        for i in range(16):
            bk = cnt["b"] % 4
            cnt["b"] += 1
            ps = C.ps[:, bk, :]
            for k in range(8):
                P.op("pe", lambda h, ps=ps, k=k, i=i: h.matmul(ps[:, :n], win[:, k, i * 128:(i + 1) * 128], xn[:, k, :n],
                                                                start=(k == 0), stop=(k == 7)),
                     reads=[b_win, b_xn], writes=[C.b_ps[bk]])
            if i % 2 == 0:
                P.op("act", lambda h, ps=ps, i=i: h.activation(s_t[:, i, :n], ps[:, :n], AF.Copy), reads=[C.b_ps[bk]], writes=[b_s])
            else:
                P.op("dve", lambda h, ps=ps, i=i: h.tensor_copy(s_t[:, i, :n], ps[:, :n]), reads=[C.b_ps[bk]], writes=[b_s])
        P.dma("sp", lambda h: h.dma_start(out=qv[:, :, t0:t0 + n], in_=s_t[:, 0:8, :n]), reads=[b_s], track=b_s, dwrites=[db_q])
        P.dma("sp", lambda h: h.dma_start(out=kv[:, :, t0:t0 + n], in_=s_t[:, 8:16, :n]), reads=[b_s], track=b_s, dwrites=[db_k])
        v_t, b_v = vst.next()
        ntt = (n + 127) // 128
        for tt in range(ntt):
            nt = min(128, n - 128 * tt)
            for half in range(2):
                bk = cnt["b"] % 4
                cnt["b"] += 1
                for k in range(8):
                    P.op("pe", lambda h, bk=bk, k=k, tt=tt, nt=nt, half=half: h.matmul(
                        C.ps[0:nt, bk, :], xn[:, k, tt * 128:tt * 128 + nt], win[:, k, 2 * D + half * 512:2 * D + (half + 1) * 512],
                        start=(k == 0), stop=(k == 7)),
                        reads=[b_win, b_xn], writes=[C.b_ps[bk]])
                if half == 0:
                    P.op("act", lambda h, bk=bk, tt=tt, nt=nt: h.activation(v_t[0:nt, tt, 0:512], C.ps[0:nt, bk, :], AF.Copy),
                         reads=[C.b_ps[bk]], writes=[b_v])
                else:
                    P.op("dve", lambda h, bk=bk, tt=tt, nt=nt: h.tensor_copy(v_t[0:nt, tt, 512:1024], C.ps[0:nt, bk, :]),
                         reads=[C.b_ps[bk]], writes=[b_v])
            P.dma("sp", lambda h, tt=tt, nt=nt: h.dma_start(out=vtok[t0 + tt * 128:t0 + tt * 128 + nt, :], in_=v_t[0:nt, tt, :]),
                  reads=[b_v], track=b_v, dwrites=[db_v])

    ngr = (L + 511) // 512
    front(0)
    for g in range(ngr):
        if g + 1 < ngr:
            front(g + 1)
        group(g)


def diff_phase_b(P, C, l, qT, db_q, kT, db_k, vtok, db_v, oT, db_o):
    tag = "db_"
    lambda_init = 0.8 - 0.6 * math.exp(-0.3 * l)
    lam = P.sbuf(tag + "lam", [128, 8], F32)
    b_lam = P.buf("lam")
    lt = P.sbuf(tag + "lt", [128, 2, 64], F32)
    b_lt = P.buf("lt")
    sm = C.small_t
    for q in range(2):
        P.op("dve", lambda h, q=q: h.tensor_tensor(lt[:, q, :], sm[:, SP_LAM + q * 128:SP_LAM + q * 128 + 64],
                                                    sm[:, SP_LAM + q * 128 + 64:SP_LAM + q * 128 + 128], ALU.mult),
             reads=[C.b_small], writes=[b_lt])
        P.op("dve", lambda h, q=q: h.reduce_sum(lam[:, q:q + 1], lt[:, q, :], mybir.AxisListType.X), reads=[b_lt], writes=[b_lam])
    P.op("act", lambda h: h.activation(lam[:, 2:4], lam[:, 0:2], AF.Exp), reads=[b_lam], writes=[b_lam])
    P.op("dve", lambda h: h.tensor_tensor(lam[:, 4:5], lam[:, 3:4], lam[:, 2:3], ALU.subtract), reads=[b_lam], writes=[b_lam])
    P.op("dve", lambda h: h.tensor_scalar(lam[:, 4:5], lam[:, 4:5], -lambda_init, None, ALU.add), reads=[b_lam], writes=[b_lam])
    P.op("dve", lambda h: h.tensor_scalar(lam[:, 5:6], sm[:, SP_SUBLN:SP_SUBLN + 1], 1.0 - lambda_init, None, ALU.mult),
         reads=[C.b_small, b_lam], writes=[b_lam])
    neg_lam = lam[:, 4:5]
    gsub = lam[:, 5:6]

    QA = [[P.sbuf(tag + "QA%d%d" % (i, m), [68, L], BF16) for m in range(2)] for i in range(2)]
    KA = [[P.sbuf(tag + "KA%d%d" % (i, m), [68, L], BF16) for m in range(2)] for i in range(2)]
    b_QA = [[P.buf("QA") for m in range(2)] for i in range(2)]
    b_KA = [[P.buf("KA") for m in range(2)] for i in range(2)]
    VH = [P.sbuf(tag + "VH%d" % i, [128, NKT, 128], BF16) for i in range(2)]
    b_VH = P.bufs_n("VH", 2)
    pT = Rot(P, tag + "pT", 6, [128, 512], BF16)
    r0 = Rot(P, tag + "r0", 3, [128, 512], F32)
    r1 = Rot(P, tag + "r1", 3, [128, 512], F32)
    oo = Rot(P, tag + "oo", 3, [128, 512], F32)
    ev0 = Rot(P, tag + "ev0", 3, [128, 512], F32)
    ev1 = Rot(P, tag + "ev1", 3, [128, 512], F32)
    sq = Rot(P, tag + "sq", 3, [128, 512], BF16)
    rs = P.sbuf(tag + "rs", [128, 512], F32)
    b_rs = P.buf("rs")
    onb = Rot(P, tag + "onb", 3, [128, 512], BF16)
    vv = vtok.rearrange("(kt p) d -> p kt d", p=128)
    ngr = (L + 511) // 512
    cnt = {"st": 0}

    def load_head(hd, i):
        for m in range(2):
            r0_ = hd * 128 + m * 64
            P.dma("sp", lambda h, m=m, r0_=r0_: h.dma_start(out=QA[i][m][0:64, :], in_=qT[r0_:r0_ + 64, :]),
                  writes=[b_QA[i][m]], track=b_QA[i][m], dreads=[db_q])
            P.dma("pool", lambda h, m=m: h.dma_start(out=QA[i][m][64:68, :], in_=C.alibiQ[hd * 4:hd * 4 + 4, :]),
                  writes=[b_QA[i][m]], track=b_QA[i][m])
            P.dma("sp", lambda h, m=m, r0_=r0_: h.dma_start(out=KA[i][m][0:64, :], in_=kT[r0_:r0_ + 64, :]),
                  writes=[b_KA[i][m]], track=b_KA[i][m], dreads=[db_k])
            P.dma("pool", lambda h, m=m: h.dma_start(out=KA[i][m][64:68, :], in_=C.alibiK[hd * 4:hd * 4 + 4, :]),
                  writes=[b_KA[i][m]], track=b_KA[i][m])
        P.dma("sp", lambda h: h.dma_start(out=VH[i][:, :, :], in_=vv[:, :, hd * 128:(hd + 1) * 128]),
              writes=[b_VH[i]], track=b_VH[i], dreads=[db_v])

    LA = 3
    STB = (0, 1, 2, 7)

    def unit_front(i, qg, q0, nq, kt, m):
        nk = min(128, L - 128 * kt)
        jj = kt - 4 * qg
        cs = 0 if (jj < 0 or qg == 8) else 128 * jj
        diag = (jj >= 0) if qg < 8 else (kt == NKT - 1)
        ncol = nq - cs
        sb = STB[cnt["st"] % 4]
        cnt["st"] += 1
        P.op("pe", lambda h: h.matmul(C.ps[0:nk, sb, 0:ncol], KA[i][m][0:68, kt * 128:kt * 128 + nk],
                                      QA[i][m][0:68, q0 + cs:q0 + nq], start=True, stop=not diag),
             reads=[b_KA[i][m], b_QA[i][m]], writes=[C.b_ps[sb]])
        if diag:
            dc = min(128, ncol)
            P.op("pe", lambda h: h.matmul(C.ps[0:nk, sb, 0:dc], C.ident_bf[0:nk, 0:nk], C.mneg_bf[0:nk, 0:dc],
                                          start=False, stop=True),
                 reads=[C.b_ident, C.b_mneg], writes=[C.b_ps[sb]])
        p_t, b_p = pT.next()
        P.op("act", lambda h: h.activation(p_t[0:nk, 0:ncol], C.ps[0:nk, sb, 0:ncol], AF.Exp, scale=0.125),
             reads=[C.b_ps[sb]], writes=[b_p])
        return (nk, cs, ncol, p_t, b_p)

    def unit_back(i, nq, kt, m, first, last, fr):
        nk, cs, ncol, p_t, b_p = fr
        P.op("pe", lambda h: h.matmul(C.ps[:, 3 + m, cs:nq], VH[i][0:nk, kt, :], p_t[0:nk, 0:ncol], start=first, stop=last),
             reads=[b_VH[i], b_p], writes=[C.b_ps[3 + m]])
        P.op("pe", lambda h: h.matmul(C.ps[:, 5 + m, cs:nq], C.ones_bf[0:nk, :], p_t[0:nk, 0:ncol], start=first, stop=last),
             reads=[C.b_ones, b_p], writes=[C.b_ps[5 + m]])

    def norm_a(hd, q0, nq):
        a_t, b_a = r0.next()
        c_t, b_c = r1.next()
        o_t, b_o = oo.next()
        e0, b_e0 = ev0.next()
        e1, b_e1 = ev1.next()
        P.op("dve", lambda h: h.tensor_copy(a_t[:, :nq], C.ps[:, 5, :nq]), reads=[C.b_ps[5]], writes=[b_a])
        P.op("act", lambda h: h.activation(e0[:, :nq], C.ps[:, 3, :nq], AF.Copy), reads=[C.b_ps[3]], writes=[b_e0])
        P.op("dve", lambda h: h.tensor_copy(c_t[:, :nq], C.ps[:, 6, :nq]), reads=[C.b_ps[6]], writes=[b_c])
        P.op("act", lambda h: h.activation(e1[:, :nq], C.ps[:, 4, :nq], AF.Copy), reads=[C.b_ps[4]], writes=[b_e1])
        P.op("dve", lambda h: h.reciprocal(a_t[:, :nq], a_t[:, :nq]), reads=[b_a], writes=[b_a])
        P.op("dve", lambda h: h.reciprocal(c_t[:, :nq], c_t[:, :nq]), reads=[b_c], writes=[b_c])
        P.op("dve", lambda h: h.tensor_tensor(a_t[:, :nq], e0[:, :nq], a_t[:, :nq], ALU.mult), reads=[b_e0, b_a], writes=[b_a])
        P.op("dve", lambda h: h.tensor_tensor(c_t[:, :nq], e1[:, :nq], c_t[:, :nq], ALU.mult), reads=[b_e1, b_c], writes=[b_c])
        P.op("dve", lambda h: h.scalar_tensor_tensor(o_t[:, :nq], c_t[:, :nq], neg_lam, a_t[:, :nq], ALU.mult, ALU.add),
             reads=[b_a, b_c, b_lam], writes=[b_o])
        s_t, b_s = sq.next()
        P.op("act", lambda h: h.activation(s_t[:, :nq], o_t[:, :nq], AF.Square), reads=[b_o], writes=[b_s])
        return (hd, q0, nq, o_t, b_o, s_t, b_s)

    def norm_b(st):
        hd, q0, nq, o_t, b_o, s_t, b_s = st
        sb = STB[cnt["st"] % 4]
        cnt["st"] += 1
        P.op("pe", lambda h: h.matmul(C.ps[:, sb, :nq], C.ones_bf[:], s_t[:, :nq], start=True, stop=True),
             reads=[C.b_ones, b_s], writes=[C.b_ps[sb]])
        rstd_from_psum(P, C, C.ps[:, sb, :], C.b_ps[sb], nq, rs, b_rs, 128.0)
        n_t, b_n = onb.next()
        P.op("dve", lambda h: h.scalar_tensor_tensor(n_t[:, :nq], o_t[:, :nq], gsub, rs[:, :nq], ALU.mult, ALU.mult),
             reads=[b_o, b_rs, b_lam], writes=[b_n])
        P.dma("sp", lambda h: h.dma_start(out=oT[hd * 128:(hd + 1) * 128, q0:q0 + nq], in_=n_t[:, :nq]),
              reads=[b_n], track=b_n, dwrites=[db_o])

    def attn_head(hd, i):
        units = []
        for qg in range(ngr):
            q0 = 512 * qg
            nq = min(512, L - q0)
            kts = list(range(0, min(4 * qg + 4, NKT)))
            for kt in kts:
                for m in range(2):
                    units.append((qg, q0, nq, kt, m, kt == kts[0], kt == kts[-1]))
        pend = []
        norms = []

        def retire():
            (qg, q0, nq, kt, m, first, last), fr = pend.pop(0)
            unit_back(i, nq, kt, m, first, last, fr)
            for nm in norms:
                nm[1] -= 1
            while norms and norms[0][1] <= 0:
                norm_b(norms.pop(0)[0])
            if last and m == 1:
                norms.append([norm_a(hd, q0, nq), 16])

        for u in units:
            pend.append((u, unit_front(i, u[0], u[1], u[2], u[3], u[4])))
            if len(pend) > LA:
                retire()
        while pend:
            retire()
        while norms:
            norm_b(norms.pop(0)[0])

    load_head(0, 0)
    for hd in range(8):
        i = hd % 2
        if hd + 1 < 8:
            load_head(hd + 1, 1 - i)
        attn_head(hd, i)


def build_program():
    nc = bass.Bass("TRN2", target_bir_lowering=False)
    C = Ctx()
    declare_io(nc, C)
    yT = nc.dram_tensor("yT", [D, SEQ], F32, kind="ExternalOutput").ap()
    hB = nc.dram_tensor("hB", [D, LC], F32, kind="Internal").ap()
    hC = nc.dram_tensor("hC", [D, LC], F32, kind="Internal").ap()
    oT = nc.dram_tensor("oT", [D, L], BF16, kind="Internal").ap()
    qT = nc.dram_tensor("qT", [D, L], BF16, kind="Internal").ap()
    kT = nc.dram_tensor("kT", [D, L], BF16, kind="Internal").ap()
    vtok = nc.dram_tensor("vtok", [NKT * 128, D], BF16, kind="Internal").ap()
    d_h0, d_hB, d_hC, d_oT, d_qT, d_kT, d_v, d_y = [DBuf(n) for n in ("h0", "hB", "hC", "oT", "qT", "kT", "vtok", "yT")]
    with ExitStack() as st:
        P = Prog(nc, st)
        setup_common(P, C)
        zt = P.sbuf("zt", [128, 2], F32)
        b_zt = P.buf("zt")
        P.op("pool", lambda h: h.memset(zt[:], 0.0), writes=[b_zt])
        for hx, dx in ((hB, d_hB), (hC, d_hC)):
            for c in range(8):
                P.dma("sp", lambda h, hx=hx, c=c: h.dma_start(out=hx[c * 128:(c + 1) * 128, 0:2], in_=zt[:]),
                      reads=[b_zt], track=b_zt, dwrites=[dx])
        P.begin_phase()
        mla_phase(P, C, 0, 0, C.h0, d_h0, oT, d_oT)
        P.end_phase()
        P.begin_phase()
        proj_phase(P, C, 0, C.mla_w_o[0], oT, d_oT, C.h0, d_h0, hB, d_hB, "p0_")
        P.end_phase()
        P.begin_phase()
        ffn_phase(P, C, 0, hB, d_hB, hC, d_hC, PAD, 0)
        P.end_phase()
        P.begin_phase()
        sc_phase(P, C, 1, hC, d_hC, hB, d_hB)
        P.end_phase()
        P.begin_phase()
        ffn_phase(P, C, 1, hB, d_hB, hC, d_hC, PAD, 0)
        P.end_phase()
        P.begin_phase()
        diff_phase_a(P, C, 2, hC, d_hC, qT, d_qT, kT, d_kT, vtok, d_v)
        P.end_phase()
        P.begin_phase()
        diff_phase_b(P, C, 2, qT, d_qT, kT, d_kT, vtok, d_v, oT, d_oT)
        P.end_phase()
        P.begin_phase()
        proj_phase(P, C, 2, C.diff_w_o[0], oT, d_oT, hC, d_hC, hB, d_hB, "p2_")
        P.end_phase()
        P.begin_phase()
        ffn_phase(P, C, 2, hB, d_hB, hC, d_hC, PAD, 0)
        P.end_phase()
        P.begin_phase()
        mla_phase(P, C, 3, 1, hC, d_hC, oT, d_oT)
        P.end_phase()
        P.begin_phase()
        proj_phase(P, C, 3, C.mla_w_o[1], oT, d_oT, hC, d_hC, hB, d_hB, "p3_")
        P.end_phase()
        P.begin_phase()
        ffn_phase(P, C, 3, hB, d_hB, yT, d_y, -NMETA, NMETA)
        P.end_phase()
        stats = P.stats
    return nc, stats


_CACHE = {}


def kernel(**inputs):
    inp = {k: np.asarray(v) for k, v in inputs.items()}
    x = inp["x"].astype(np.float32, copy=False)
    B = x.shape[0]
    meta = inp["meta_tokens"].astype(np.float32, copy=False)
    if "nc" not in _CACHE:
        _CACHE["nc"] = build_program()[0]
    nc = _CACHE["nc"]
    shared = {"small": pack_small(inp)}
    shared.update(make_consts())
    for k in ("mla_w_in", "mla_w_uq", "mla_w_ukv", "mla_w_o", "sc_w_in", "sc_w_out", "diff_w_in", "diff_w_o",
              "ffn_w_up", "ffn_w_down"):
        shared[k] = np.ascontiguousarray(inp[k], dtype=np.float32)
    in_maps = []
    for b in range(B):
        h0 = np.zeros((D, LC), np.float32)
        h0[:, PAD:PAD + NMETA] = meta.T
        h0[:, PAD + NMETA:] = x[b].T
        m = dict(shared)
        m["h0"] = h0
        in_maps.append(m)
    res = run_bass_kernel_spmd(nc, in_maps, core_ids=list(range(B)))
    out = np.empty((B, SEQ, D), np.float32)
    for b in range(B):
        out[b] = np.asarray(res.results[b]["yT"]).T
    return out
```

```python
import math
import numpy as np
import concourse.bass as bass
import concourse.mybir as mybir
from concourse.bass_utils import run_bass_kernel_spmd
from contextlib import ExitStack

F32 = mybir.dt.float32
BF16 = mybir.dt.bfloat16
ALU = mybir.AluOpType
AF = mybir.ActivationFunctionType

D = 1024
SEQ = 4096
NMETA = 16
L = SEQ + NMETA
PAD = 2
LC = L + PAD
DEPTH = 4
EPS = 1e-6
FF = 2816
NKT = 33
NCORES = 8
DUP_QK = False


class Buf:
    __slots__ = ("name", "lw", "rd", "didx", "dcnt", "dphase")

    def __init__(self, name):
        self.name = name
        self.lw = None
        self.rd = []
        self.didx = -1
        self.dcnt = 0
        self.dphase = -1


class DBuf:
    __slots__ = ("name", "wd", "rdd")

    def __init__(self, name):
        self.name = name
        self.wd = {}
        self.rdd = {}


class Op:
    __slots__ = ("eng", "idx", "fn", "deps", "is_dma", "didx", "inc")

    def __init__(self, eng, idx, fn, deps, is_dma=False, didx=-1):
        self.eng = eng
        self.idx = idx
        self.fn = fn
        self.deps = deps
        self.is_dma = is_dma
        self.didx = didx
        self.inc = False


class Prog:
    ENGS = ("pe", "act", "dve", "pool", "sp")

    def __init__(self, nc, stack, npool=72):
        self.nc = nc
        self.outer = stack
        self.scope = stack
        self.esem = {e: stack.enter_context(nc.semaphore("sem_" + e)) for e in self.ENGS}
        self.pool = [stack.enter_context(nc.semaphore("dp%d" % i)) for i in range(npool)]
        self.pool_cnt = [0] * npool
        self.pool_used = 0
        self.rank_base = {e: 0 for e in self.ENGS}
        self.waited = {e: {} for e in self.ENGS}
        self.ops = {e: [] for e in self.ENGS}
        self.phase = 0
        self.stats = []

    def begin_phase(self):
        self.scope = ExitStack()
        self.scope.__enter__()

    def sbuf(self, name, shape, dt):
        return self.scope.enter_context(self.nc.sbuf_tensor(name, shape, dt))

    def psum(self, name, shape, dt=F32):
        return self.scope.enter_context(self.nc.psum_tensor(name, shape, dt))

    def buf(self, name):
        return Buf(name)

    def bufs_n(self, name, n):
        return [self.buf("%s%d" % (name, i)) for i in range(n)]

    def _deps(self, eng, reads, writes, is_dma):
        deps = []
        for b in reads:
            if b.lw is not None:
                deps.append(b.lw)
        for b in writes:
            if b.lw is not None:
                d = b.lw
                if is_dma or d[0] == "d" or d[1] != eng:
                    deps.append(d)
            for d in b.rd:
                if is_dma or d[0] == "d" or d[1] != eng:
                    deps.append(d)
        return deps

    def op(self, eng, fn, reads=(), writes=()):
        lst = self.ops[eng]
        idx = len(lst)
        deps = self._deps(eng, reads, writes, False)
        lst.append(Op(eng, idx, fn, deps))
        me = ("c", eng, idx, self.phase)
        for b in reads:
            b.rd.append(me)
        for b in writes:
            b.lw = me
            b.rd = []

    def dma(self, eng, fn, reads=(), writes=(), track=None, dreads=(), dwrites=()):
        lst = self.ops[eng]
        idx = len(lst)
        deps = self._deps(eng, reads, writes, True)
        for db in dreads:
            for t, v in db.wd.items():
                deps.append(("d", t, v))
        for db in dwrites:
            for t, v in db.rdd.items():
                deps.append(("d", t, v))
            for t, v in db.wd.items():
                deps.append(("d", t, v))
        if track.dphase != self.phase:
            track.dphase = self.phase
            track.didx = self.pool_used
            self.pool_used += 1
            assert self.pool_used <= len(self.pool), "out of dma semaphores"
            track.dcnt = self.pool_cnt[track.didx]
        track.dcnt += 16
        self.pool_cnt[track.didx] = track.dcnt
        val = track.dcnt
        di = track.didx
        lst.append(Op(eng, idx, fn, deps, True, di))
        me = ("d", di, val)
        for b in reads:
            b.rd.append(me)
        for b in writes:
            b.lw = me
            b.rd = []
        for db in dreads:
            db.rdd[di] = max(val, db.rdd.get(di, 0))
        for db in dwrites:
            db.wd[di] = max(val, db.wd.get(di, 0))

    def end_phase(self):
        nc = self.nc
        ph = self.phase
        need = {e: set() for e in self.ENGS}
        for e in self.ENGS:
            for o in self.ops[e]:
                for d in o.deps:
                    if d[0] == "c" and d[3] == ph:
                        need[d[1]].add(d[2])
        comp = [e for e in self.ENGS if e != "sp"]
        for e in comp:
            lst = self.ops[e]
            for o in reversed(lst):
                if not o.is_dma:
                    need[e].add(o.idx)
                    break
        rank = {}
        final_rank = {}
        for e in self.ENGS:
            base = self.rank_base[e]
            srt = sorted(need[e])
            for r, idx in enumerate(srt):
                rank[(e, idx)] = base + r + 1
                self.ops[e][idx].inc = True
            final_rank[e] = base + len(srt)
        final_pool = [(i, self.pool_cnt[i]) for i in range(self.pool_used)]

        def resolve(d):
            if d[0] == "c":
                if d[3] != ph:
                    return None
                return ("e", d[1]), self.esem[d[1]], rank[(d[1], d[2])]
            return ("p", d[1]), self.pool[d[1]], d[2]

        def do_waits(h, deps, waited):
            best = {}
            for d in deps:
                r = resolve(d)
                if r is None:
                    continue
                k, s, v = r
                if waited.get(k, 0) >= v:
                    continue
                if k not in best or best[k][1] < v:
                    best[k] = (s, v)
            for k, (s, v) in best.items():
                h.wait_ge(s, v)
                waited[k] = v
            return len(best)

        stats = {}

        def run(ename, h):
            waited = self.waited[ename]
            nwait = 0
            for o in self.ops[ename]:
                nwait += do_waits(h, o.deps, waited)
                ins = o.fn(h)
                if o.is_dma:
                    ins.then_inc(self.pool[o.didx], 16)
                elif o.inc:
                    ins.then_inc(self.esem[ename], 1)
            for e2 in comp:
                if e2 != ename and final_rank[e2] > waited.get(("e", e2), 0):
                    h.wait_ge(self.esem[e2], final_rank[e2])
                    waited[("e", e2)] = final_rank[e2]
            for i, v in final_pool:
                if v > waited.get(("p", i), 0):
                    h.wait_ge(self.pool[i], v)
                    waited[("p", i)] = v
            stats[ename] = (len(self.ops[ename]), nwait)

        with nc.Block() as block:
            @block.tensor
            def _(h):
                run("pe", h)

            @block.scalar
            def _(h):
                run("act", h)

            @block.vector
            def _(h):
                run("dve", h)

            @block.gpsimd
            def _(h):
                run("pool", h)

            @block.sync
            def _(h):
                run("sp", h)

        self.stats.append(stats)
        for e in self.ENGS:
            self.rank_base[e] = final_rank[e]
            self.ops[e] = []
        self.pool_used = 0
        self.phase += 1
        if self.scope is not self.outer:
            self.scope.__exit__(None, None, None)
            self.scope = self.outer


class Rot:
    def __init__(self, P, name, n, shape, dt):
        self.t = [P.sbuf("%s%d" % (name, i), shape, dt) for i in range(n)]
        self.b = [P.buf("%s%d" % (name, i)) for i in range(n)]
        self.n = n
        self.i = 0

    def next(self):
        k = self.i % self.n
        self.i += 1
        return self.t[k], self.b[k]


SP_NORMS = 0
SP_FCONV = SP_NORMS + 128
SP_SCONV = SP_FCONV + 528
SP_NQ = SP_SCONV + 24
SP_NKV = SP_NQ + 4
SP_SUBLN = SP_NKV + 2
SP_LAM = SP_SUBLN + 1
SP_TOT = SP_LAM + 256


def pack_small(inp):
    sp = np.zeros((128, SP_TOT), np.float32)
    norms = inp["norms"]
    sp[:, SP_NORMS:SP_NORMS + 128] = norms.reshape(16, 8, 128).transpose(2, 0, 1).reshape(128, 128)
    fc = inp["ffn_conv"]
    sp[:, SP_FCONV:SP_FCONV + 528] = fc.reshape(4, 3, 44, 128).transpose(3, 0, 2, 1).reshape(128, 528)
    sc = inp["sc_conv"][0]
    sp[:, SP_SCONV:SP_SCONV + 24] = sc.reshape(3, 8, 128).transpose(2, 1, 0).reshape(128, 24)
    nq = inp["mla_norm_q"]
    sp[:, SP_NQ:SP_NQ + 4] = nq.reshape(2, 2, 128).transpose(2, 0, 1).reshape(128, 4)
    nkv = inp["mla_norm_kv"]
    sp[:, SP_NKV:SP_NKV + 2] = nkv.T
    sp[:, SP_SUBLN] = inp["diff_subln"][0]
    lam = np.stack([inp["diff_lambda_q1"][0], inp["diff_lambda_k1"][0],
                    inp["diff_lambda_q2"][0], inp["diff_lambda_k2"][0]]).reshape(1, 256)
    sp[:, SP_LAM:SP_LAM + 256] = np.broadcast_to(lam, (128, 256))
    return sp


def make_consts():
    c = {}
    tri = (np.arange(128)[:, None] <= np.arange(128)[None, :]).astype(np.float32)
    c["tri"] = tri
    c["ident"] = np.eye(128, dtype=np.float32)
    c["mneg"] = ((1.0 - tri) * -30000.0).astype(np.float32)
    inv_freq = (10000.0 ** (-np.arange(0, 32, 2, dtype=np.float32) / np.float32(32))).astype(np.float32)
    ang = (np.arange(L, dtype=np.float32)[:, None] * inv_freq[None, :]).astype(np.float32)
    cos = np.cos(ang).astype(np.float32).T
    sin = np.sin(ang).astype(np.float32).T
    C = np.ones((128, L), np.float32)
    S = np.zeros((128, L), np.float32)
    C[64:80] = cos
    C[80:96] = cos
    S[64:80] = sin
    S[80:96] = sin
    c["ropeC"] = C
    c["ropeS"] = S
    pos = np.arange(L)
    hi = (pos // 64).astype(np.float32)
    lo = (pos % 64).astype(np.float32)
    aK = np.zeros((8, 4, L), np.float32)
    aQ = np.zeros((8, 4, L), np.float32)
    for h in range(8):
        slope = 2.0 ** (-(h + 1))
        aK[h, 0] = 8.0 * slope * 64.0 * hi
        aK[h, 1] = 8.0 * slope * lo
        aK[h, 2] = 1.0
        aK[h, 3] = 1.0
        aQ[h, 0] = 1.0
        aQ[h, 1] = 1.0
        aQ[h, 2] = -8.0 * slope * 64.0 * hi
        aQ[h, 3] = -8.0 * slope * lo
    c["alibiK"] = aK.reshape(32, L)
    c["alibiQ"] = aQ.reshape(32, L)
    return c


class Ctx:
    pass


def declare_io(nc, C):
    def din(name, shape, dt=F32):
        return nc.dram_tensor(name, list(shape), dt, kind="ExternalInput").ap()
    C.h0 = din("h0", [D, LC])
    C.small = din("small", [128, SP_TOT])
    C.tri = din("tri", [128, 128])
    C.ident = din("ident", [128, 128])
    C.mneg = din("mneg", [128, 128])
    C.ropeC = din("ropeC", [128, L])
    C.ropeS = din("ropeS", [128, L])
    C.alibiK = din("alibiK", [32, L])
    C.alibiQ = din("alibiQ", [32, L])
    C.mla_w_in = din("mla_w_in", [2, D, 416])
    C.mla_w_uq = din("mla_w_uq", [2, 256, 1536])
    C.mla_w_ukv = din("mla_w_ukv", [2, 128, 2048])
    C.mla_w_o = din("mla_w_o", [2, D, D])
    C.sc_w_in = din("sc_w_in", [1, D, 3 * D])
    C.sc_w_out = din("sc_w_out", [1, D, D])
    C.diff_w_in = din("diff_w_in", [1, D, 3 * D])
    C.diff_w_o = din("diff_w_o", [1, D, D])
    C.ffn_w_up = din("ffn_w_up", [4, D, 2 * FF])
    C.ffn_w_down = din("ffn_w_down", [4, FF, D])


def setup_common(P, C):
    nc = P.nc
    C.small_t = P.sbuf("small_t", [128, SP_TOT], F32)
    C.b_small = P.buf("small")
    P.dma("sp", lambda h: h.dma_start(out=C.small_t[:], in_=C.small), writes=[C.b_small], track=C.b_small)
    C.ones_bf = P.sbuf("ones_bf", [128, 128], BF16)
    C.b_ones = P.buf("ones_bf")
    P.op("pool", lambda h: h.memset(C.ones_bf[:], 1.0), writes=[C.b_ones])
    C.ones_f = P.sbuf("ones_f", [128, 128], F32)
    C.b_onesf = P.buf("ones_f")
    P.op("pool", lambda h: h.memset(C.ones_f[:], 1.0), writes=[C.b_onesf])
    C.tri_bf = P.sbuf("tri_bf", [128, 128], BF16)
    C.b_tri = P.buf("tri_bf")
    P.dma("pool", lambda h: h.dma_start(out=C.tri_bf[:], in_=C.tri), writes=[C.b_tri], track=C.b_tri)
    C.eps_t = P.sbuf("eps_t", [128, 1], F32)
    C.b_eps = P.buf("eps_t")
    P.op("pool", lambda h: h.memset(C.eps_t[:], EPS), writes=[C.b_eps])
    C.ident_bf = P.sbuf("ident_bf", [128, 128], BF16)
    C.b_ident = P.buf("ident_bf")
    P.dma("pool", lambda h: h.dma_start(out=C.ident_bf[:], in_=C.ident), writes=[C.b_ident], track=C.b_ident)
    C.mneg_bf = P.sbuf("mneg_bf", [128, 128], BF16)
    C.b_mneg = P.buf("mneg_bf")
    P.dma("pool", lambda h: h.dma_start(out=C.mneg_bf[:], in_=C.mneg), writes=[C.b_mneg], track=C.b_mneg)
    C.ps = P.psum("ps", [128, 8, 512], F32)
    C.b_ps = P.bufs_n("psb", 8)


def gain(C, l, j, c):
    col = SP_NORMS + (l * 4 + j) * 8 + c
    return C.small_t[:, col:col + 1]


def rstd_from_psum(P, C, st_ps, b_st, n, rs_t, b_rs, dim):
    P.op("act", lambda h: h.activation(rs_t[:, :n], st_ps[:, :n], AF.Ln, bias=C.eps_t[:, 0:1], scale=1.0 / dim),
         reads=[b_st, C.b_eps], writes=[b_rs])
    P.op("act", lambda h: h.activation(rs_t[:, :n], rs_t[:, :n], AF.Exp, scale=-0.5), reads=[b_rs], writes=[b_rs])


def prenorm(P, C, W, l, j, h_in, db_in, c0, n, xn, b_xn):
    st_ps = C.ps[:, 7, :]
    b_st = C.b_ps[7]
    rs1, b_rs1 = W.rs1, W.b_rs1
    for c in range(8):
        t, bt = W.hin.next()
        P.dma("sp", lambda h, t=t, c=c: h.dma_start(out=t[:, :n], in_=h_in[c * 128:(c + 1) * 128, c0:c0 + n]),
              writes=[bt], track=bt, dreads=[db_in])
        s, bs = W.sq.next()
        P.op("act", lambda h, t=t, s=s: h.activation(s[:, :n], t[:, :n], AF.Square), reads=[bt], writes=[bs])
        P.op("pe", lambda h, s=s, c=c: h.matmul(st_ps[:, :n], C.ones_bf[:], s[:, :n], start=(c == 0), stop=(c == 7)),
             reads=[bs, C.b_ones], writes=[b_st])
    rstd_from_psum(P, C, st_ps, b_st, n, rs1, b_rs1, float(D))
    for c in range(8):
        t, bt = W.hin.next()
        P.dma("sp", lambda h, t=t, c=c: h.dma_start(out=t[:, :n], in_=h_in[c * 128:(c + 1) * 128, c0:c0 + n]),
              writes=[bt], track=bt, dreads=[db_in])
        P.op("dve", lambda h, t=t, c=c: h.scalar_tensor_tensor(xn[:, c, :n], t[:, :n], gain(C, l, j, c),
                                                                rs1[:, :n], ALU.mult, ALU.mult),
             reads=[bt, b_rs1, C.b_small], writes=[b_xn])


def tail(P, C, W, l, j, mm_chunk, nv, h_in, db_in, cin0, h_out, db_out, cout0, bank0):
    st_ps = C.ps[:, 7, :]
    b_st = C.b_ps[7]
    lag = None
    flip_f(W)
    f_t, b_f, rs2, b_rs2 = W.f, W.b_f, W.rs2, W.b_rs2
    for c in range(8):
        bk = bank0 + (c % 2)
        ps = C.ps[:, bk, :]
        bps = C.b_ps[bk]
        mm_chunk(c, ps, bps)
        P.op("act", lambda h, ps=ps, c=c: h.activation(f_t[:, c, :nv], ps[:, :nv], AF.Copy), reads=[bps], writes=[b_f])
        s, bs = W.sq.next()
        P.op("act", lambda h, ps=ps, s=s: h.activation(s[:, :nv], ps[:, :nv], AF.Square), reads=[bps], writes=[bs])
        if lag is not None:
            lag()
        lag = (lambda s=s, bs=bs, c=c: P.op(
            "pe", lambda h: h.matmul(st_ps[:, :nv], C.ones_bf[:], s[:, :nv], start=(c == 0), stop=(c == 7)),
            reads=[bs, C.b_ones], writes=[b_st]))
    lag()
    rstd_from_psum(P, C, st_ps, b_st, nv, rs2, b_rs2, float(D))
    for c in range(8):
        t, bt = W.hin.next()
        P.dma("sp", lambda h, t=t, c=c: h.dma_start(out=t[:, :nv], in_=h_in[c * 128:(c + 1) * 128, cin0:cin0 + nv]),
              writes=[bt], track=bt, dreads=[db_in])
        o, bo = W.out.next()
        P.op("dve", lambda h, o=o, c=c: h.scalar_tensor_tensor(o[:, :nv], f_t[:, c, :nv], gain(C, l, j, c),
                                                                rs2[:, :nv], ALU.mult, ALU.mult),
             reads=[b_f, b_rs2, C.b_small], writes=[bo])
        P.op("pool", lambda h, o=o, t=t: h.tensor_tensor(o[:, :nv], o[:, :nv], t[:, :nv], ALU.add),
             reads=[bo, bt], writes=[bo])
        P.dma("pool", lambda h, o=o, c=c: h.dma_start(out=h_out[c * 128:(c + 1) * 128, cout0:cout0 + nv], in_=o[:, :nv]),
              reads=[bo], track=bo, dwrites=[db_out])


class Work:
    pass


def common_work(P, W, nmax, tag, nbuf=1, need_f=True, nh=3, no=3):
    W.hin = Rot(P, tag + "hin", nh, [128, nmax], F32)
    W.out = Rot(P, tag + "out", no, [128, nmax], F32)
    W.sq = Rot(P, tag + "sq", 2, [128, nmax], BF16)
    W.xs = []
    W.fs = []
    for i in range(nbuf):
        W.xs.append((P.sbuf(tag + "xn%d" % i, [128, 8, nmax], BF16), P.buf(tag + "xn"),
                     P.sbuf(tag + "rs1%d" % i, [128, nmax], F32), P.buf(tag + "rs1")))
        if need_f:
            W.fs.append((P.sbuf(tag + "f%d" % i, [128, 8, nmax], F32), P.buf(tag + "f"),
                         P.sbuf(tag + "rs2%d" % i, [128, nmax], F32), P.buf(tag + "rs2")))
    W.xi = -1
    W.fi = -1
    flip_x(W)
    if need_f:
        flip_f(W)


def flip_x(W):
    W.xi = (W.xi + 1) % len(W.xs)
    W.xn, W.b_xn, W.rs1, W.b_rs1 = W.xs[W.xi]


def set_x(W, g):
    W.xi = g % len(W.xs)
    W.xn, W.b_xn, W.rs1, W.b_rs1 = W.xs[W.xi]


def flip_f(W):
    W.fi = (W.fi + 1) % len(W.fs)
    W.f, W.b_f, W.rs2, W.b_rs2 = W.fs[W.fi]


def load_w(P, t, b, src, kchunks, per):
    v = src.rearrange("(k p) f -> p k f", p=128)
    for k0 in range(0, kchunks, per):
        k1 = min(kchunks, k0 + per)
        P.dma("pool", lambda h, k0=k0, k1=k1: h.dma_start(out=t[:, k0:k1, :], in_=v[:, k0:k1, :]), writes=[b], track=b)


FG = 412


def ffn_phase(P, C, l, h_in, db_in, h_out, db_out, out_off, out_lo):
    nmax = FG + 2
    W = Work()
    tag = "f%d_" % l
    W.hin = Rot(P, tag + "hin", 3, [128, nmax], F32)
    W.out = Rot(P, tag + "out", 2, [128, nmax], F32)
    W.sq = Rot(P, tag + "sq", 2, [128, nmax], BF16)
    xns = [(P.sbuf(tag + "xn%d" % i, [128, 8, nmax], BF16), P.buf(tag + "xn")) for i in range(2)]
    rs1 = P.sbuf(tag + "rs1", [128, nmax], F32)
    b_rs1 = P.buf(tag + "rs1")
    rs2 = P.sbuf(tag + "rs2", [128, nmax], F32)
    b_rs2 = P.buf(tag + "rs2")
    f_t = P.sbuf(tag + "f", [128, 8, nmax], F32)
    b_f = P.buf(tag + "f")
    wup = P.sbuf(tag + "wup", [128, 8, 2 * FF], BF16)
    b_wup = P.buf(tag + "wup")
    wdn = P.sbuf(tag + "wdn", [128, 22, D], BF16)
    b_wdn = P.buf(tag + "wdn")
    load_w(P, wup, b_wup, C.ffn_w_up[l], 8, 1)
    load_w(P, wdn, b_wdn, C.ffn_w_down[l], 22, 6)
    gc = Rot(P, tag + "gc", 2, [128, nmax], F32)
    uc = Rot(P, tag + "uc", 2, [128, nmax], F32)
    act = P.sbuf(tag + "act", [128, 22, nmax], BF16)
    b_act = P.buf(tag + "act")
    st1 = C.ps[:, 7, :]
    b_st1 = C.b_ps[7]
    st2 = C.ps[:, 6, :]
    b_st2 = C.b_ps[6]

    def cw(ch, j):
        col = SP_FCONV + ((l * 44) + ch) * 3 + j
        return C.small_t[:, col:col + 1]

    ngroups = (L + FG - 1) // FG

    def gdims(g):
        c0 = FG * g
        n = min(nmax, LC - c0)
        return c0, n, n - 2

    def pre_sq(g, c):
        c0, n, nv = gdims(g)
        t, bt = W.hin.next()
        P.dma("sp", lambda h: h.dma_start(out=t[:, :n], in_=h_in[c * 128:(c + 1) * 128, c0:c0 + n]),
              writes=[bt], track=bt, dreads=[db_in])
        s_, bs = W.sq.next()
        P.op("act", lambda h: h.activation(s_[:, :n], t[:, :n], AF.Square), reads=[bt], writes=[bs])
        return (s_, bs)

    def pre_mm(g, c, sb):
        c0, n, nv = gdims(g)
        s_, bs = sb
        P.op("pe", lambda h: h.matmul(st1[:, :n], C.ones_bf[:], s_[:, :n], start=(c == 0), stop=(c == 7)),
             reads=[bs, C.b_ones], writes=[b_st1])

    def pre_chain(g):
        c0, n, nv = gdims(g)
        rstd_from_psum(P, C, st1, b_st1, n, rs1, b_rs1, float(D))

    def pre_xn(g, c):
        c0, n, nv = gdims(g)
        xn, b_xn = xns[g % 2]
        t, bt = W.hin.next()
        P.dma("sp", lambda h: h.dma_start(out=t[:, :n], in_=h_in[c * 128:(c + 1) * 128, c0:c0 + n]),
              writes=[bt], track=bt, dreads=[db_in])
        P.op("dve", lambda h: h.scalar_tensor_tensor(xn[:, c, :n], t[:, :n], gain(C, l, 2, c), rs1[:, :n], ALU.mult, ALU.mult),
             reads=[bt, b_rs1, C.b_small], writes=[b_xn])

    def pre_hooks(g):
        hk = {}
        sqs = {}

        def mk_sq(c):
            def f():
                sqs[c] = pre_sq(g, c)
            return f

        def mk_mm(c):
            return lambda: pre_mm(g, c, sqs[c])

        for c in range(8):
            hk.setdefault(4 + c, []).append(mk_sq(c))
            hk.setdefault(5 + c, []).append(mk_mm(c))
        hk.setdefault(12, []).append(lambda: pre_chain(g))
        for c in range(8):
            hk.setdefault(13 + c, []).append(lambda c=c: pre_xn(g, c))
        return hk

    def tail_chunk(g, c):
        c0, n, nv = gdims(g)
        p0 = c0
        skip = max(p0, out_lo) - p0
        cin0 = c0 + 2
        cout0 = p0 + out_off
        t, bt = W.hin.next()
        P.dma("sp", lambda h: h.dma_start(out=t[:, :nv], in_=h_in[c * 128:(c + 1) * 128, cin0:cin0 + nv]),
              writes=[bt], track=bt, dreads=[db_in])
        o, bo = W.out.next()
        P.op("dve", lambda h: h.scalar_tensor_tensor(o[:, :nv], f_t[:, c, :nv], gain(C, l, 3, c), rs2[:, :nv], ALU.mult, ALU.mult),
             reads=[b_f, b_rs2, C.b_small], writes=[bo])
        P.op("pool", lambda h: h.tensor_tensor(o[:, :nv], o[:, :nv], t[:, :nv], ALU.add), reads=[bo, bt], writes=[bo])
        P.dma("pool", lambda h: h.dma_start(out=h_out[c * 128:(c + 1) * 128, cout0 + skip:cout0 + nv], in_=o[:, skip:nv]),
              reads=[bo], track=bo, dwrites=[db_out])

    def mid(g, hooks):
        c0, n, nv = gdims(g)
        xn, b_xn = xns[g % 2]
        for jp in range(22):
            banks = [(jp % 3) * 2, (jp % 3) * 2 + 1]
            tiles = []
            for which, ch in enumerate((jp, 22 + jp)):
                ps = C.ps[:, banks[which], :]
                bps = C.b_ps[banks[which]]
                for k in range(8):
                    P.op("pe", lambda h, ps=ps, k=k, ch=ch: h.matmul(ps[:, :n], wup[:, k, ch * 128:(ch + 1) * 128],
                                                                      xn[:, k, :n], start=(k == 0), stop=(k == 7)),
                         reads=[b_wup, b_xn], writes=[bps])
                tiles.append((ps, bps, ch))
            g_t, b_g = gc.next()
            u_t, b_u = uc.next()
            for (ps, bps, ch), (d_t, b_d) in zip(tiles, ((g_t, b_g), (u_t, b_u))):
                P.op("act", lambda h, ps=ps, ch=ch, d_t=d_t: h.activation(d_t[:, :nv], ps[:, 2:n], AF.Copy, scale=cw(ch, 2)),
                     reads=[bps, C.b_small], writes=[b_d])
            for tap in (1, 0):
                for (ps, bps, ch), (d_t, b_d) in zip(tiles, ((g_t, b_g), (u_t, b_u))):
                    P.op("dve", lambda h, ps=ps, ch=ch, d_t=d_t, tap=tap: h.scalar_tensor_tensor(
                        d_t[:, :nv], ps[:, tap:tap + nv], cw(ch, tap), d_t[:, :nv], ALU.mult, ALU.add),
                        reads=[bps, b_d, C.b_small], writes=[b_d])
            P.op("act", lambda h, g_t=g_t: h.activation(g_t[:, :nv], g_t[:, :nv], AF.Silu), reads=[b_g], writes=[b_g])
            P.op("pool", lambda h, g_t=g_t, u_t=u_t, jp=jp: h.tensor_tensor(act[:, jp, :nv], g_t[:, :nv], u_t[:, :nv], ALU.mult),
                 reads=[b_g, b_u], writes=[b_act])
            for fn in hooks.get(jp, ()):
                fn()

    def back(g):
        c0, n, nv = gdims(g)
        lag = None
        for c in range(8):
            bk = 2 + (c % 2)
            ps = C.ps[:, bk, :]
            bps = C.b_ps[bk]
            for k in range(22):
                P.op("pe", lambda h, ps=ps, k=k, c=c: h.matmul(ps[:, :nv], wdn[:, k, c * 128:(c + 1) * 128], act[:, k, :nv],
                                                                start=(k == 0), stop=(k == 21)),
                     reads=[b_wdn, b_act], writes=[bps])
            P.op("act", lambda h, ps=ps, c=c: h.activation(f_t[:, c, :nv], ps[:, :nv], AF.Copy), reads=[bps], writes=[b_f])
            s_, bs = W.sq.next()
            P.op("act", lambda h, ps=ps, s_=s_: h.activation(s_[:, :nv], ps[:, :nv], AF.Square), reads=[bps], writes=[bs])
            if lag is not None:
                lag()
            lag = (lambda s_=s_, bs=bs, c=c: P.op(
                "pe", lambda h: h.matmul(st2[:, :nv], C.ones_bf[:], s_[:, :nv], start=(c == 0), stop=(c == 7)),
                reads=[bs, C.b_ones], writes=[b_st2]))
        lag()
        rstd_from_psum(P, C, st2, b_st2, nv, rs2, b_rs2, float(D))

    for c in range(8):
        pre_mm(0, c, pre_sq(0, c))
    pre_chain(0)
    for c in range(8):
        pre_xn(0, c)
    for g in range(ngroups):
        hooks = pre_hooks(g + 1) if g + 1 < ngroups else {}
        if g > 0:
            for c in range(8):
                hooks.setdefault(c, []).insert(0, (lambda c=c, gg=g - 1: tail_chunk(gg, c)))
        mid(g, hooks)
        back(g)
    for c in range(8):
        tail_chunk(ngroups - 1, c)


def tail_skip(P, C, W, l, j, mm_chunk, nv, skip, h_in, db_in, cin0, h_out, db_out, cout0, bank0):
    st_ps = C.ps[:, 7, :]
    b_st = C.b_ps[7]
    lag = None
    flip_f(W)
    f_t, b_f, rs2, b_rs2 = W.f, W.b_f, W.rs2, W.b_rs2
    for c in range(8):
        bk = bank0 + (c % 2)
        ps = C.ps[:, bk, :]
        bps = C.b_ps[bk]
        mm_chunk(c, ps, bps)
        P.op("act", lambda h, ps=ps, c=c: h.activation(f_t[:, c, :nv], ps[:, :nv], AF.Copy), reads=[bps], writes=[b_f])
        s, bs = W.sq.next()
        P.op("act", lambda h, ps=ps, s=s: h.activation(s[:, :nv], ps[:, :nv], AF.Square), reads=[bps], writes=[bs])
        if lag is not None:
            lag()
        lag = (lambda s=s, bs=bs, c=c: P.op(
            "pe", lambda h: h.matmul(st_ps[:, :nv], C.ones_bf[:], s[:, :nv], start=(c == 0), stop=(c == 7)),
            reads=[bs, C.b_ones], writes=[b_st]))
    lag()
    rstd_from_psum(P, C, st_ps, b_st, nv, rs2, b_rs2, float(D))
    for c in range(8):
        t, bt = W.hin.next()
        P.dma("sp", lambda h, t=t, c=c: h.dma_start(out=t[:, :nv], in_=h_in[c * 128:(c + 1) * 128, cin0:cin0 + nv]),
              writes=[bt], track=bt, dreads=[db_in])
        o, bo = W.out.next()
        P.op("dve", lambda h, o=o, c=c: h.scalar_tensor_tensor(o[:, :nv], f_t[:, c, :nv], gain(C, l, j, c),
                                                                rs2[:, :nv], ALU.mult, ALU.mult),
             reads=[b_f, b_rs2, C.b_small], writes=[bo])
        P.op("pool", lambda h, o=o, t=t: h.tensor_tensor(o[:, :nv], o[:, :nv], t[:, :nv], ALU.add),
             reads=[bo, bt], writes=[bo])
        P.dma("sp", lambda h, o=o, c=c: h.dma_start(out=h_out[c * 128:(c + 1) * 128, cout0 + skip:cout0 + nv],
                                                     in_=o[:, skip:nv]),
              reads=[bo], track=bo, dwrites=[db_out])


def proj_phase(P, C, l, w_src, oT, db_o, h_in, db_in, h_out, db_out, tag):
    W = Work()
    common_work(P, W, 512, tag, nbuf=2, nh=8, no=4)
    wo = P.sbuf(tag + "wo", [128, 8, D], BF16)
    b_wo = P.buf(tag + "wo")
    load_w(P, wo, b_wo, w_src, 8, 4)
    og = Rot(P, tag + "og", 2, [128, 8, 512], BF16)
    ov = oT.rearrange("(c p) t -> p c t", p=128)

    def group(g):
        t0 = 512 * g
        n = min(512, L - t0)
        o_t, b_o = og.next()
        P.dma("sp", lambda h: h.dma_start(out=o_t[:, :, :n], in_=ov[:, :, t0:t0 + n]), writes=[b_o], track=b_o, dreads=[db_o])

        def mm_chunk(c, ps, bps):
            for k in range(8):
                P.op("pe", lambda h, k=k: h.matmul(ps[:, :n], wo[:, k, c * 128:(c + 1) * 128], o_t[:, k, :n],
                                                   start=(k == 0), stop=(k == 7)),
                     reads=[b_wo, b_o], writes=[bps])

        tail(P, C, W, l, 1, mm_chunk, n, h_in, db_in, PAD + t0, h_out, db_out, PAD + t0, 4)

    for g in range((L + 511) // 512):
        group(g)


SG = 510


def sc_phase(P, C, l, h_in, db_in, h_out, db_out):
    tag = "sc_"
    nmax = SG + 2
    W = Work()
    common_work(P, W, nmax, tag, nbuf=2, nh=6, no=4)
    win = P.sbuf(tag + "win", [128, 8, 3 * D], BF16)
    b_win = P.buf(tag + "win")
    wout = P.sbuf(tag + "wout", [128, 8, D], BF16)
    b_wout = P.buf(tag + "wout")
    load_w(P, win, b_win, C.sc_w_in[0], 8, 1)
    load_w(P, wout, b_wout, C.sc_w_out[0], 8, 4)
    ub = Rot(P, tag + "ub", 2, [128, nmax], F32)
    cu = Rot(P, tag + "cu", 2, [128, nmax], F32)
    yy = Rot(P, tag + "yy", 2, [128, nmax], F32)
    yb = P.sbuf(tag + "yb", [128, 8, nmax], BF16)
    b_yb = P.buf(tag + "yb")

    def cw(c, j):
        col = SP_SCONV + c * 3 + j
        return C.small_t[:, col:col + 1]

    def gdims(g):
        c0 = SG * g
        n = min(nmax, LC - c0)
        return c0, n, n - 2

    def front(g):
        c0, n, nv = gdims(g)
        set_x(W, g)
        prenorm(P, C, W, l, 0, h_in, db_in, c0, n, W.xn, W.b_xn)

    def mid(g):
        c0, n, nv = gdims(g)
        xn, b_xn = W.xs[g % 2][0], W.xs[g % 2][1]
        for i in range(8):
            bks = [(i % 2) * 3 + q for q in range(3)]
            pss = []
            for q in range(3):
                ps = C.ps[:, bks[q], :]
                bps = C.b_ps[bks[q]]
                col = q * D + i * 128
                for k in range(8):
                    P.op("pe", lambda h, ps=ps, k=k, col=col: h.matmul(ps[:, :n], win[:, k, col:col + 128], xn[:, k, :n],
                                                                        start=(k == 0), stop=(k == 7)),
                         reads=[b_win, b_xn], writes=[bps])
                pss.append((ps, bps))
            (pb, bpb), (pc, bpc), (pu, bpu) = pss
            u_t, b_u = ub.next()
            c_t, b_c = cu.next()
            y_t, b_y = yy.next()
            P.op("act", lambda h, pu=pu, u_t=u_t: h.activation(u_t[:, :n], pu[:, :n], AF.Copy), reads=[bpu], writes=[b_u])
            P.op("dve", lambda h, pc=pc, u_t=u_t, c_t=c_t: h.tensor_tensor(c_t[:, :n], pc[:, :n], u_t[:, :n], ALU.mult),
                 reads=[bpc, b_u], writes=[b_c])
            P.op("act", lambda h, c_t=c_t, y_t=y_t, i=i: h.activation(y_t[:, :nv], c_t[:, 2:n], AF.Copy, scale=cw(i, 2)),
                 reads=[b_c, C.b_small], writes=[b_y])
            for tap in (1, 0):
                P.op("dve", lambda h, c_t=c_t, y_t=y_t, i=i, tap=tap: h.scalar_tensor_tensor(
                    y_t[:, :nv], c_t[:, tap:tap + nv], cw(i, tap), y_t[:, :nv], ALU.mult, ALU.add),
                    reads=[b_c, b_y, C.b_small], writes=[b_y])
            P.op("dve", lambda h, pb=pb, y_t=y_t, i=i: h.tensor_tensor(yb[:, i, :nv], pb[:, 2:n], y_t[:, :nv], ALU.mult),
                 reads=[bpb, b_y], writes=[b_yb])

    def back(g):
        c0, n, nv = gdims(g)

        def mm_chunk(c, ps, bps):
            for k in range(8):
                P.op("pe", lambda h, k=k: h.matmul(ps[:, :nv], wout[:, k, c * 128:(c + 1) * 128], yb[:, k, :nv],
                                                   start=(k == 0), stop=(k == 7)),
                     reads=[b_wout, b_yb], writes=[bps])

        tail(P, C, W, l, 1, mm_chunk, nv, h_in, db_in, c0 + 2, h_out, db_out, c0 + 2, 4)

    ngroups = (L + SG - 1) // SG
    front(0)
    for g in range(ngroups):
        if g + 1 < ngroups:
            front(g + 1)
        mid(g)
        back(g)


def mla_phase(P, C, l, j, h_in, db_in, oT, db_o):
    tag = "ml%d_" % l
    W = Work()
    W.hin = Rot(P, tag + "hin", 3, [128, 512], F32)
    W.sq = Rot(P, tag + "sq", 2, [128, 512], BF16)
    W.xs = [(P.sbuf(tag + "xn%d" % i, [128, 8, 512], BF16), P.buf(tag + "xn"),
             P.sbuf(tag + "rs1%d" % i, [128, 512], F32), P.buf(tag + "rs1")) for i in range(2)]
    W.xi = -1
    rsq = P.sbuf(tag + "rsq", [128, 512], F32)
    b_rsq = P.buf("rsq")
    rskv = P.sbuf(tag + "rskv", [128, 512], F32)
    b_rskv = P.buf("rskv")
    win = P.sbuf(tag + "win", [128, 8, 416], BF16)
    b_win = P.buf("win")
    load_w(P, win, b_win, C.mla_w_in[j], 8, 8)
    wkr = P.sbuf(tag + "wkr", [128, 8, 96], BF16)
    b_wkr = P.buf("wkr")
    wkrr = P.sbuf(tag + "wkrr", [128, 8, 96], BF16)
    b_wkrr = P.buf("wkrr")
    P.op("pool", lambda h: h.memset(wkr[:], 0.0), writes=[b_wkr])
    P.op("pool", lambda h: h.memset(wkrr[:], 0.0), writes=[b_wkrr])
    P.op("dve", lambda h: h.tensor_copy(wkr[:, :, 64:96], win[:, :, 384:416]), reads=[b_win], writes=[b_wkr])
    P.op("dve", lambda h: h.tensor_scalar(wkrr[:, :, 64:80], win[:, :, 400:416], -1.0, None, ALU.mult), reads=[b_win], writes=[b_wkrr])
    P.op("dve", lambda h: h.tensor_copy(wkrr[:, :, 80:96], win[:, :, 384:400]), reads=[b_win], writes=[b_wkrr])
    wuq = P.sbuf(tag + "wuq", [128, 2, 16, 96], BF16)
    b_wuq = P.buf("wuq")
    wuqr = P.sbuf(tag + "wuqr", [128, 2, 16, 96], BF16)
    b_wuqr = P.buf("wuqr")
    for c in range(2):
        src = C.mla_w_uq[j][c * 128:(c + 1) * 128, :].rearrange("p (h e) -> p h e", e=96)
        P.dma("pool", lambda h, c=c, src=src: h.dma_start(out=wuq[:, c, :, :], in_=src), writes=[b_wuq], track=b_wuq)
    P.op("pool", lambda h: h.memset(wuqr[:], 0.0), writes=[b_wuqr])
    for c in range(2):
        P.op("dve", lambda h, c=c: h.tensor_scalar(wuqr[:, c, :, 64:80], wuq[:, c, :, 80:96], -1.0, None, ALU.mult),
             reads=[b_wuq], writes=[b_wuqr])
        P.op("dve", lambda h, c=c: h.tensor_copy(wuqr[:, c, :, 80:96], wuq[:, c, :, 64:80]), reads=[b_wuq], writes=[b_wuqr])
    wukv = P.sbuf(tag + "wukv", [128, 2048], BF16)
    b_wukv = P.buf("wukv")
    P.dma("pool", lambda h: h.dma_start(out=wukv[:], in_=C.mla_w_ukv[j]), writes=[b_wukv], track=b_wukv)
    rc = P.sbuf(tag + "rc", [128, L], F32)
    b_rc = P.buf("rc")
    rsn = P.sbuf(tag + "rsn", [128, L], F32)
    b_rsn = P.buf("rsn")
    P.dma("sp", lambda h: h.dma_start(out=rc[:], in_=C.ropeC), writes=[b_rc], track=b_rc)
    P.dma("sp", lambda h: h.dma_start(out=rsn[:], in_=C.ropeS), writes=[b_rsn], track=b_rsn)
    cqn = P.sbuf(tag + "cqn", [128, 2, L], BF16)
    b_cqn = P.buf("cqn")
    ckvn = P.sbuf(tag + "ckvn", [128, L], BF16)
    b_ckvn = P.buf("ckvn")
    KT = [P.sbuf(tag + "KT%d" % i, [128, L], BF16) for i in range(2)]
    b_KT = P.bufs_n("KT", 2)
    QT = [P.sbuf(tag + "QT%d" % i, [128, L], BF16) for i in range(2)]
    b_QT = P.bufs_n("QT", 2)
    VT = [P.sbuf(tag + "VT%d" % i, [128, NKT, 65], BF16) for i in range(2)]
    b_VT = P.bufs_n("VT", 2)
    for i in range(2):
        P.op("pool", lambda h, i=i: h.memset(VT[i][:, :, 64:65], 1.0), writes=[b_VT[i]])
    t1 = Rot(P, tag + "t1", 2, [128, 512], F32)
    t2 = Rot(P, tag + "t2", 2, [128, 512], F32)
    pT = Rot(P, tag + "pT", 6, [128, 512], BF16)
    osb = Rot(P, tag + "osb", 3, [128, 512], F32)
    rsum = Rot(P, tag + "rsum", 3, [128, 512], F32)
    onb = Rot(P, tag + "onb", 3, [64, 512], BF16)

    def nq_col(c):
        col = SP_NQ + j * 2 + c
        return C.small_t[:, col:col + 1]

    nkv_col = C.small_t[:, SP_NKV + j:SP_NKV + j + 1]
    ngr = (L + 511) // 512

    def frontA(g):
        t0 = 512 * g
        n = min(512, L - t0)
        flip_x(W)
        prenorm(P, C, W, l, 0, h_in, db_in, PAD + t0, n, W.xn, W.b_xn)

    def groupA(g):
        t0 = 512 * g
        n = min(512, L - t0)
        xn, b_xn = W.xs[g % 2][0], W.xs[g % 2][1]
        specs = [(0, win, b_win, 0, 128, 128), (1, win, b_win, 128, 128, 128), (2, win, b_win, 256, 128, 128)]
        for bk, wt, bw, col, m, _ in specs:
            ps = C.ps[:, bk, :]
            for k in range(8):
                P.op("pe", lambda h, ps=ps, k=k, col=col: h.matmul(ps[:, :n], win[:, k, col:col + 128], xn[:, k, :n],
                                                                    start=(k == 0), stop=(k == 7)),
                     reads=[b_win, b_xn], writes=[C.b_ps[bk]])
        for bk, wt, bw in ((3, wkr, b_wkr), (4, wkrr, b_wkrr)):
            ps = C.ps[:, bk, :]
            for k in range(8):
                P.op("pe", lambda h, ps=ps, k=k, wt=wt: h.matmul(ps[0:96, :n], wt[:, k, :], xn[:, k, :n],
                                                                  start=(k == 0), stop=(k == 7)),
                     reads=[bw, b_xn], writes=[C.b_ps[bk]])
        for c in range(2):
            s, bs = W.sq.next()
            P.op("act", lambda h, s=s, c=c: h.activation(s[:, :n], C.ps[:, c, :n], AF.Square), reads=[C.b_ps[c]], writes=[bs])
            P.op("pe", lambda h, s=s, c=c: h.matmul(C.ps[:, 5, :n], C.ones_bf[:], s[:, :n], start=(c == 0), stop=(c == 1)),
                 reads=[bs, C.b_ones], writes=[C.b_ps[5]])
        rstd_from_psum(P, C, C.ps[:, 5, :], C.b_ps[5], n, rsq, b_rsq, 256.0)
        for c in range(2):
            P.op("dve", lambda h, c=c: h.scalar_tensor_tensor(cqn[:, c, t0:t0 + n], C.ps[:, c, :n], nq_col(c), rsq[:, :n],
                                                              ALU.mult, ALU.mult),
                 reads=[C.b_ps[c], b_rsq, C.b_small], writes=[b_cqn])
        s, bs = W.sq.next()
        P.op("act", lambda h, s=s: h.activation(s[:, :n], C.ps[:, 2, :n], AF.Square), reads=[C.b_ps[2]], writes=[bs])
        P.op("pe", lambda h, s=s: h.matmul(C.ps[:, 6, :n], C.ones_bf[:], s[:, :n], start=True, stop=True),
             reads=[bs, C.b_ones], writes=[C.b_ps[6]])
        rstd_from_psum(P, C, C.ps[:, 6, :], C.b_ps[6], n, rskv, b_rskv, 128.0)
        P.op("dve", lambda h: h.scalar_tensor_tensor(ckvn[:, t0:t0 + n], C.ps[:, 2, :n], nkv_col, rskv[:, :n], ALU.mult, ALU.mult),
             reads=[C.b_ps[2], b_rskv, C.b_small], writes=[b_ckvn])
        a_t, b_a = t1.next()
        b_t, b_b = t2.next()
        P.op("dve", lambda h: h.tensor_tensor(a_t[64:96, :n], C.ps[64:96, 3, :n], rc[64:96, t0:t0 + n], ALU.mult),
             reads=[C.b_ps[3], b_rc], writes=[b_a])
        P.op("dve", lambda h: h.tensor_tensor(b_t[64:96, :n], C.ps[64:96, 4, :n], rsn[64:96, t0:t0 + n], ALU.mult),
             reads=[C.b_ps[4], b_rsn], writes=[b_b])
        for i in range(2):
            P.op("pool", lambda h, i=i: h.tensor_tensor(KT[i][64:96, t0:t0 + n], a_t[64:96, :n], b_t[64:96, :n], ALU.add),
                 reads=[b_a, b_b], writes=[b_KT[i]])

    frontA(0)
    for g in range(ngr):
        if g + 1 < ngr:
            frontA(g + 1)
        groupA(g)

    scale = 96.0 ** -0.5

    def head_proj_v(hd, i, kt):
        nk = min(128, L - 128 * kt)
        bk = (3, 6)[kt % 2]
        P.op("pe", lambda h: h.matmul(C.ps[0:nk, bk, 0:64], ckvn[:, kt * 128:kt * 128 + nk],
                                      wukv[:, hd * 128 + 64:hd * 128 + 128], start=True, stop=True),
             reads=[b_ckvn, b_wukv], writes=[C.b_ps[bk]])
        P.op("dve", lambda h: h.tensor_copy(VT[i][0:nk, kt, 0:64], C.ps[0:nk, bk, 0:64]),
             reads=[C.b_ps[bk]], writes=[b_VT[i]])

    def head_proj_items(hd, i):
        items = []
        for g in range(ngr):
            items.append(lambda g=g: head_proj_g(hd, i, g))
        for kt0 in range(0, NKT, 2):
            def vv(kt0=kt0):
                for kt in range(kt0, min(NKT, kt0 + 2)):
                    head_proj_v(hd, i, kt)
            items.append(vv)
        return items

    def head_proj(hd, i):
        for it in head_proj_items(hd, i):
            it()

    def head_proj_g(hd, i, g):
        t0 = 512 * g
        n = min(512, L - t0)
        for bk, wt, bw in ((3, wuq, b_wuq), (6, wuqr, b_wuqr)):
            for c in range(2):
                P.op("pe", lambda h, bk=bk, wt=wt, c=c: h.matmul(C.ps[0:96, bk, :n], wt[:, c, hd, :], cqn[:, c, t0:t0 + n],
                                                                  start=(c == 0), stop=(c == 1)),
                     reads=[bw, b_cqn], writes=[C.b_ps[bk]])
        a_t, b_a = t1.next()
        b_t, b_b = t2.next()
        P.op("dve", lambda h: h.tensor_tensor(a_t[0:96, :n], C.ps[0:96, 3, :n], rc[0:96, t0:t0 + n], ALU.mult),
             reads=[C.b_ps[3], b_rc], writes=[b_a])
        P.op("dve", lambda h: h.tensor_tensor(b_t[0:96, :n], C.ps[0:96, 6, :n], rsn[0:96, t0:t0 + n], ALU.mult),
             reads=[C.b_ps[6], b_rsn], writes=[b_b])
        P.op("pool", lambda h: h.tensor_tensor(QT[i][0:96, t0:t0 + n], a_t[0:96, :n], b_t[0:96, :n], ALU.add),
             reads=[b_a, b_b], writes=[b_QT[i]])
        P.op("pe", lambda h: h.matmul(C.ps[0:64, 7, :n], wukv[:, hd * 128:hd * 128 + 64], ckvn[:, t0:t0 + n], start=True, stop=True),
             reads=[b_wukv, b_ckvn], writes=[C.b_ps[7]])
        P.op("dve", lambda h: h.tensor_copy(KT[i][0:64, t0:t0 + n], C.ps[0:64, 7, :n]), reads=[C.b_ps[7]], writes=[b_KT[i]])

    cnt = {"st": 0, "ob": 0}
    LA = 2

    def norm_a(hd, q0, nq, ob):
        o_t, b_o = osb.next()
        r_t, b_r = rsum.next()
        P.op("dve", lambda h: h.tensor_copy(o_t[0:65, :nq], C.ps[0:65, ob, :nq]), reads=[C.b_ps[ob]], writes=[b_o])
        P.op("dve", lambda h: h.reciprocal(r_t[64:65, :nq], o_t[64:65, :nq]), reads=[b_o], writes=[b_r])
        return (hd, q0, nq, o_t, b_o, r_t, b_r)

    def norm_b(st):
        hd, q0, nq, o_t, b_o, r_t, b_r = st
        n_t, b_n = onb.next()
        P.op("pe", lambda h: h.matmul(C.ps[0:64, 7, :nq], C.ones_f[64:65, 0:64], r_t[64:65, :nq], start=True, stop=True),
             reads=[C.b_onesf, b_r], writes=[C.b_ps[7]])
        P.op("dve", lambda h: h.tensor_tensor(n_t[0:64, :nq], C.ps[0:64, 7, :nq], o_t[0:64, :nq], ALU.mult),
             reads=[C.b_ps[7], b_o], writes=[b_n])
        P.dma("sp", lambda h: h.dma_start(out=oT[hd * 64:(hd + 1) * 64, q0:q0 + nq], in_=n_t[0:64, :nq]),
              reads=[b_n], track=b_n, dwrites=[db_o])

    def blk_front(i, qg, q0, nq, kt):
        nk = min(128, L - 128 * kt)
        jj = kt - 4 * qg
        cs = 0 if (jj < 0 or qg == 8) else 128 * jj
        diag = (jj >= 0) if qg < 8 else (kt == NKT - 1)
        ncol = nq - cs
        sb = cnt["st"] % 3
        cnt["st"] += 1
        if DUP_QK and ncol >= 256:
            P.op("pe", lambda h: h.matmul(C.ps[0:nk, sb, 0:ncol], KT[i][0:96, kt * 128:kt * 128 + nk],
                                          QT[i][0:96, q0 + cs:q0 + nq], start=True, stop=True),
                 reads=[b_KT[i], b_QT[i]], writes=[C.b_ps[sb]])
        P.op("pe", lambda h: h.matmul(C.ps[0:nk, sb, 0:ncol], KT[i][0:96, kt * 128:kt * 128 + nk],
                                      QT[i][0:96, q0 + cs:q0 + nq], start=True, stop=not diag),
             reads=[b_KT[i], b_QT[i]], writes=[C.b_ps[sb]])
        if diag:
            dc = min(128, ncol)
            P.op("pe", lambda h: h.matmul(C.ps[0:nk, sb, 0:dc], C.ident_bf[0:nk, 0:nk], C.mneg_bf[0:nk, 0:dc],
                                          start=False, stop=True),
                 reads=[C.b_ident, C.b_mneg], writes=[C.b_ps[sb]])
        p_t, b_p = pT.next()
        P.op("act", lambda h: h.activation(p_t[0:nk, 0:ncol], C.ps[0:nk, sb, 0:ncol], AF.Exp, scale=scale),
             reads=[C.b_ps[sb]], writes=[b_p])
        return (nk, cs, ncol, p_t, b_p)

    def blk_back(i, nq, ob, kt, first, last, fr):
        nk, cs, ncol, p_t, b_p = fr
        P.op("pe", lambda h: h.matmul(C.ps[0:65, ob, cs:nq], VT[i][0:nk, kt, 0:65], p_t[0:nk, 0:ncol],
                                      start=first, stop=last),
             reads=[b_VT[i], b_p], writes=[C.b_ps[ob]])

    def attn_head(hd, i, items):
        blocks = []
        for qg in range(ngr):
            q0 = 512 * qg
            nq = min(512, L - q0)
            ob = 4 + (cnt["ob"] % 2)
            cnt["ob"] += 1
            kts = list(range(0, min(4 * qg + 4, NKT)))
            for kt in kts:
                blocks.append((qg, q0, nq, ob, kt, kt == kts[0], kt == kts[-1]))
        pend = []
        norms = []

        def retire():
            (qg, q0, nq, ob, kt, first, last), fr = pend.pop(0)
            blk_back(i, nq, ob, kt, first, last, fr)
            for nm in norms:
                nm[1] -= 1
            while norms and norms[0][1] <= 0:
                norm_b(norms.pop(0)[0])
            if last:
                norms.append([norm_a(hd, q0, nq, ob), 8])

        every = max(1, (len(blocks) - 8) // max(1, len(items)))
        for bi, b in enumerate(blocks):
            pend.append((b, blk_front(i, b[0], b[1], b[2], b[4])))
            if len(pend) > LA:
                retire()
            if items and bi % every == every - 1:
                items.pop(0)()
        while pend:
            retire()
        while norms:
            norm_b(norms.pop(0)[0])
        while items:
            items.pop(0)()

    head_proj(0, 0)
    for hd in range(16):
        i = hd % 2
        items = head_proj_items(hd + 1, 1 - i) if hd + 1 < 16 else []
        attn_head(hd, i, items)


def diff_phase_a(P, C, l, h_in, db_in, qT, db_q, kT, db_k, vtok, db_v):
    tag = "da_"
    W = Work()
    W.hin = Rot(P, tag + "hin", 3, [128, 512], F32)
    W.sq = Rot(P, tag + "sq", 2, [128, 512], BF16)
    W.xs = [(P.sbuf(tag + "xn%d" % i, [128, 8, 512], BF16), P.buf(tag + "xn"),
             P.sbuf(tag + "rs1%d" % i, [128, 512], F32), P.buf(tag + "rs1")) for i in range(2)]
    W.xi = -1
    win = P.sbuf(tag + "win", [128, 8, 3 * D], BF16)
    b_win = P.buf("win")
    load_w(P, win, b_win, C.diff_w_in[0], 8, 1)
    stg = Rot(P, tag + "stg", 2, [128, 16, 512], BF16)
    vst = Rot(P, tag + "vst", 2, [128, 4, D], BF16)
    qv = qT.rearrange("(c p) t -> p c t", p=128)
    kv = kT.rearrange("(c p) t -> p c t", p=128)
    cnt = {"b": 0}

    def front(g):
        t0 = 512 * g
        n = min(512, L - t0)
        flip_x(W)
        prenorm(P, C, W, l, 0, h_in, db_in, PAD + t0, n, W.xn, W.b_xn)

    def group(g):
        t0 = 512 * g
        n = min(512, L - t0)
        xn, b_xn = W.xs[g % 2][0], W.xs[g % 2][1]
        s_t, b_s = stg.next()
        for i in range(16):
            bk = cnt["b"] % 4
            cnt["b"] += 1
            ps = C.ps[:, bk, :]
            for k in range(8):
                P.op("pe", lambda h, ps=ps, k=k, i=i: h.matmul(ps[:, :n], win[:, k, i * 128:(i + 1) * 128], xn[:, k, :n],
                                                                start=(k == 0), stop=(k == 7)),
                     reads=[b_win, b_xn], writes=[C.b_ps[bk]])
            if i % 2 == 0:
                P.op("act", lambda h, ps=ps, i=i: h.activation(s_t[:, i, :n], ps[:, :n], AF.Copy), reads=[C.b_ps[bk]], writes=[b_s])
            else:
                P.op("dve", lambda h, ps=ps, i=i: h.tensor_copy(s_t[:, i, :n], ps[:, :n]), reads=[C.b_ps[bk]], writes=[b_s])
        P.dma("sp", lambda h: h.dma_start(out=qv[:, :, t0:t0 + n], in_=s_t[:, 0:8, :n]), reads=[b_s], track=b_s, dwrites=[db_q])
        P.dma("sp", lambda h: h.dma_start(out=kv[:, :, t0:t0 + n], in_=s_t[:, 8:16, :n]), reads=[b_s], track=b_s, dwrites=[db_k])
        v_t, b_v = vst.next()
        ntt = (n + 127) // 128
        for tt in range(ntt):
            nt = min(128, n - 128 * tt)
            for half in range(2):
                bk = cnt["b"] % 4
                cnt["b"] += 1
                for k in range(8):
                    P.op("pe", lambda h, bk=bk, k=k, tt=tt, nt=nt, half=half: h.matmul(
                        C.ps[0:nt, bk, :], xn[:, k, tt * 128:tt * 128 + nt], win[:, k, 2 * D + half * 512:2 * D + (half + 1) * 512],
                        start=(k == 0), stop=(k == 7)),
                        reads=[b_win, b_xn], writes=[C.b_ps[bk]])
                if half == 0:
                    P.op("act", lambda h, bk=bk, tt=tt, nt=nt: h.activation(v_t[0:nt, tt, 0:512], C.ps[0:nt, bk, :], AF.Copy),
                         reads=[C.b_ps[bk]], writes=[b_v])
                else:
                    P.op("dve", lambda h, bk=bk, tt=tt, nt=nt: h.tensor_copy(v_t[0:nt, tt, 512:1024], C.ps[0:nt, bk, :]),
                         reads=[C.b_ps[bk]], writes=[b_v])
            P.dma("sp", lambda h, tt=tt, nt=nt: h.dma_start(out=vtok[t0 + tt * 128:t0 + tt * 128 + nt, :], in_=v_t[0:nt, tt, :]),
                  reads=[b_v], track=b_v, dwrites=[db_v])

    ngr = (L + 511) // 512
    front(0)
    for g in range(ngr):
        if g + 1 < ngr:
            front(g + 1)
        group(g)


def diff_phase_b(P, C, l, qT, db_q, kT, db_k, vtok, db_v, oT, db_o):
    tag = "db_"
    lambda_init = 0.8 - 0.6 * math.exp(-0.3 * l)
    lam = P.sbuf(tag + "lam", [128, 8], F32)
    b_lam = P.buf("lam")
    lt = P.sbuf(tag + "lt", [128, 2, 64], F32)
    b_lt = P.buf("lt")
    sm = C.small_t
    for q in range(2):
        P.op("dve", lambda h, q=q: h.tensor_tensor(lt[:, q, :], sm[:, SP_LAM + q * 128:SP_LAM + q * 128 + 64],
                                                    sm[:, SP_LAM + q * 128 + 64:SP_LAM + q * 128 + 128], ALU.mult),
             reads=[C.b_small], writes=[b_lt])
        P.op("dve", lambda h, q=q: h.reduce_sum(lam[:, q:q + 1], lt[:, q, :], mybir.AxisListType.X), reads=[b_lt], writes=[b_lam])
    P.op("act", lambda h: h.activation(lam[:, 2:4], lam[:, 0:2], AF.Exp), reads=[b_lam], writes=[b_lam])
    P.op("dve", lambda h: h.tensor_tensor(lam[:, 4:5], lam[:, 3:4], lam[:, 2:3], ALU.subtract), reads=[b_lam], writes=[b_lam])
    P.op("dve", lambda h: h.tensor_scalar(lam[:, 4:5], lam[:, 4:5], -lambda_init, None, ALU.add), reads=[b_lam], writes=[b_lam])
    P.op("dve", lambda h: h.tensor_scalar(lam[:, 5:6], sm[:, SP_SUBLN:SP_SUBLN + 1], 1.0 - lambda_init, None, ALU.mult),
         reads=[C.b_small, b_lam], writes=[b_lam])
    neg_lam = lam[:, 4:5]
    gsub = lam[:, 5:6]

    QA = [[P.sbuf(tag + "QA%d%d" % (i, m), [68, L], BF16) for m in range(2)] for i in range(2)]
    KA = [[P.sbuf(tag + "KA%d%d" % (i, m), [68, L], BF16) for m in range(2)] for i in range(2)]
    b_QA = [[P.buf("QA") for m in range(2)] for i in range(2)]
    b_KA = [[P.buf("KA") for m in range(2)] for i in range(2)]
    VH = [P.sbuf(tag + "VH%d" % i, [128, NKT, 128], BF16) for i in range(2)]
    b_VH = P.bufs_n("VH", 2)
    pT = Rot(P, tag + "pT", 6, [128, 512], BF16)
    r0 = Rot(P, tag + "r0", 2, [128, 512], F32)
    r1 = Rot(P, tag + "r1", 2, [128, 512], F32)
    oo = Rot(P, tag + "oo", 2, [128, 512], F32)
    ev0 = Rot(P, tag + "ev0", 2, [128, 512], F32)
    ev1 = Rot(P, tag + "ev1", 2, [128, 512], F32)
    sq = Rot(P, tag + "sq", 2, [128, 512], BF16)
    rs = P.sbuf(tag + "rs", [128, 512], F32)
    b_rs = P.buf("rs")
    onb = Rot(P, tag + "onb", 2, [128, 512], BF16)
    vv = vtok.rearrange("(kt p) d -> p kt d", p=128)
    ngr = (L + 511) // 512
    cnt = {"st": 0}

    def load_head(hd, i):
        for m in range(2):
            r0_ = hd * 128 + m * 64
            P.dma("sp", lambda h, m=m, r0_=r0_: h.dma_start(out=QA[i][m][0:64, :], in_=qT[r0_:r0_ + 64, :]),
                  writes=[b_QA[i][m]], track=b_QA[i][m], dreads=[db_q])
            P.dma("pool", lambda h, m=m: h.dma_start(out=QA[i][m][64:68, :], in_=C.alibiQ[hd * 4:hd * 4 + 4, :]),
                  writes=[b_QA[i][m]], track=b_QA[i][m])
            P.dma("sp", lambda h, m=m, r0_=r0_: h.dma_start(out=KA[i][m][0:64, :], in_=kT[r0_:r0_ + 64, :]),
                  writes=[b_KA[i][m]], track=b_KA[i][m], dreads=[db_k])
            P.dma("pool", lambda h, m=m: h.dma_start(out=KA[i][m][64:68, :], in_=C.alibiK[hd * 4:hd * 4 + 4, :]),
                  writes=[b_KA[i][m]], track=b_KA[i][m])
        P.dma("sp", lambda h: h.dma_start(out=VH[i][:, :, :], in_=vv[:, :, hd * 128:(hd + 1) * 128]),
              writes=[b_VH[i]], track=b_VH[i], dreads=[db_v])

    LA = 3
    STB = (0, 1, 2, 7)

    def unit_front(i, qg, q0, nq, kt, m):
        nk = min(128, L - 128 * kt)
        jj = kt - 4 * qg
        cs = 0 if (jj < 0 or qg == 8) else 128 * jj
        diag = (jj >= 0) if qg < 8 else (kt == NKT - 1)
        ncol = nq - cs
        sb = STB[cnt["st"] % 4]
        cnt["st"] += 1
        P.op("pe", lambda h: h.matmul(C.ps[0:nk, sb, 0:ncol], KA[i][m][0:68, kt * 128:kt * 128 + nk],
                                      QA[i][m][0:68, q0 + cs:q0 + nq], start=True, stop=not diag),
             reads=[b_KA[i][m], b_QA[i][m]], writes=[C.b_ps[sb]])
        if diag:
            dc = min(128, ncol)
            P.op("pe", lambda h: h.matmul(C.ps[0:nk, sb, 0:dc], C.ident_bf[0:nk, 0:nk], C.mneg_bf[0:nk, 0:dc],
                                          start=False, stop=True),
                 reads=[C.b_ident, C.b_mneg], writes=[C.b_ps[sb]])
        p_t, b_p = pT.next()
        P.op("act", lambda h: h.activation(p_t[0:nk, 0:ncol], C.ps[0:nk, sb, 0:ncol], AF.Exp, scale=0.125),
             reads=[C.b_ps[sb]], writes=[b_p])
        return (nk, cs, ncol, p_t, b_p)

    def unit_back(i, nq, kt, m, first, last, fr):
        nk, cs, ncol, p_t, b_p = fr
        P.op("pe", lambda h: h.matmul(C.ps[:, 3 + m, cs:nq], VH[i][0:nk, kt, :], p_t[0:nk, 0:ncol], start=first, stop=last),
             reads=[b_VH[i], b_p], writes=[C.b_ps[3 + m]])
        P.op("pe", lambda h: h.matmul(C.ps[:, 5 + m, cs:nq], C.ones_bf[0:nk, :], p_t[0:nk, 0:ncol], start=first, stop=last),
             reads=[C.b_ones, b_p], writes=[C.b_ps[5 + m]])

    def norm_a(hd, q0, nq):
        a_t, b_a = r0.next()
        c_t, b_c = r1.next()
        o_t, b_o = oo.next()
        e0, b_e0 = ev0.next()
        e1, b_e1 = ev1.next()
        P.op("dve", lambda h: h.tensor_copy(a_t[:, :nq], C.ps[:, 5, :nq]), reads=[C.b_ps[5]], writes=[b_a])
        P.op("act", lambda h: h.activation(e0[:, :nq], C.ps[:, 3, :nq], AF.Copy), reads=[C.b_ps[3]], writes=[b_e0])
        P.op("dve", lambda h: h.tensor_copy(c_t[:, :nq], C.ps[:, 6, :nq]), reads=[C.b_ps[6]], writes=[b_c])
        P.op("act", lambda h: h.activation(e1[:, :nq], C.ps[:, 4, :nq], AF.Copy), reads=[C.b_ps[4]], writes=[b_e1])
        P.op("dve", lambda h: h.reciprocal(a_t[:, :nq], a_t[:, :nq]), reads=[b_a], writes=[b_a])
        P.op("dve", lambda h: h.reciprocal(c_t[:, :nq], c_t[:, :nq]), reads=[b_c], writes=[b_c])
        P.op("dve", lambda h: h.tensor_tensor(a_t[:, :nq], e0[:, :nq], a_t[:, :nq], ALU.mult), reads=[b_e0, b_a], writes=[b_a])
        P.op("dve", lambda h: h.tensor_tensor(c_t[:, :nq], e1[:, :nq], c_t[:, :nq], ALU.mult), reads=[b_e1, b_c], writes=[b_c])
        P.op("dve", lambda h: h.scalar_tensor_tensor(o_t[:, :nq], c_t[:, :nq], neg_lam, a_t[:, :nq], ALU.mult, ALU.add),
             reads=[b_a, b_c, b_lam], writes=[b_o])
        s_t, b_s = sq.next()
        P.op("act", lambda h: h.activation(s_t[:, :nq], o_t[:, :nq], AF.Square), reads=[b_o], writes=[b_s])
        return (hd, q0, nq, o_t, b_o, s_t, b_s)

    def norm_b(st):
        hd, q0, nq, o_t, b_o, s_t, b_s = st
        sb = STB[cnt["st"] % 4]
        cnt["st"] += 1
        P.op("pe", lambda h: h.matmul(C.ps[:, sb, :nq], C.ones_bf[:], s_t[:, :nq], start=True, stop=True),
             reads=[C.b_ones, b_s], writes=[C.b_ps[sb]])
        rstd_from_psum(P, C, C.ps[:, sb, :], C.b_ps[sb], nq, rs, b_rs, 128.0)
        n_t, b_n = onb.next()
        P.op("dve", lambda h: h.scalar_tensor_tensor(n_t[:, :nq], o_t[:, :nq], gsub, rs[:, :nq], ALU.mult, ALU.mult),
             reads=[b_o, b_rs, b_lam], writes=[b_n])
        P.dma("sp", lambda h: h.dma_start(out=oT[hd * 128:(hd + 1) * 128, q0:q0 + nq], in_=n_t[:, :nq]),
              reads=[b_n], track=b_n, dwrites=[db_o])

    def attn_head(hd, i):
        units = []
        for qg in range(ngr):
            q0 = 512 * qg
            nq = min(512, L - q0)
            kts = list(range(0, min(4 * qg + 4, NKT)))
            for kt in kts:
                for m in range(2):
                    units.append((qg, q0, nq, kt, m, kt == kts[0], kt == kts[-1]))
        pend = []
        norms = []

        def retire():
            (qg, q0, nq, kt, m, first, last), fr = pend.pop(0)
            unit_back(i, nq, kt, m, first, last, fr)
            for nm in norms:
                nm[1] -= 1
            while norms and norms[0][1] <= 0:
                norm_b(norms.pop(0)[0])
            if last and m == 1:
                norms.append([norm_a(hd, q0, nq), 4])

        for u in units:
            pend.append((u, unit_front(i, u[0], u[1], u[2], u[3], u[4])))
            if len(pend) > LA:
                retire()
        while pend:
            retire()
        while norms:
            norm_b(norms.pop(0)[0])

    load_head(0, 0)
    for hd in range(8):
        i = hd % 2
        if hd + 1 < 8:
            load_head(hd + 1, 1 - i)
        attn_head(hd, i)


def build_program():
    nc = bass.Bass("TRN2", target_bir_lowering=False)
    C = Ctx()
    declare_io(nc, C)
    yT = nc.dram_tensor("yT", [D, SEQ], F32, kind="ExternalOutput").ap()
    hB = nc.dram_tensor("hB", [D, LC], F32, kind="Internal").ap()
    hC = nc.dram_tensor("hC", [D, LC], F32, kind="Internal").ap()
    oT = nc.dram_tensor("oT", [D, L], BF16, kind="Internal").ap()
    qT = nc.dram_tensor("qT", [D, L], BF16, kind="Internal").ap()
    kT = nc.dram_tensor("kT", [D, L], BF16, kind="Internal").ap()
    vtok = nc.dram_tensor("vtok", [NKT * 128, D], BF16, kind="Internal").ap()
    d_h0, d_hB, d_hC, d_oT, d_qT, d_kT, d_v, d_y = [DBuf(n) for n in ("h0", "hB", "hC", "oT", "qT", "kT", "vtok", "yT")]
    with ExitStack() as st:
        P = Prog(nc, st)
        setup_common(P, C)
        zt = P.sbuf("zt", [128, 2], F32)
        b_zt = P.buf("zt")
        P.op("pool", lambda h: h.memset(zt[:], 0.0), writes=[b_zt])
        for hx, dx in ((hB, d_hB), (hC, d_hC)):
            for c in range(8):
                P.dma("sp", lambda h, hx=hx, c=c: h.dma_start(out=hx[c * 128:(c + 1) * 128, 0:2], in_=zt[:]),
                      reads=[b_zt], track=b_zt, dwrites=[dx])
        P.begin_phase()
        mla_phase(P, C, 0, 0, C.h0, d_h0, oT, d_oT)
        P.end_phase()
        P.begin_phase()
        proj_phase(P, C, 0, C.mla_w_o[0], oT, d_oT, C.h0, d_h0, hB, d_hB, "p0_")
        P.end_phase()
        P.begin_phase()
        ffn_phase(P, C, 0, hB, d_hB, hC, d_hC, PAD, 0)
        P.end_phase()
        P.begin_phase()
        sc_phase(P, C, 1, hC, d_hC, hB, d_hB)
        P.end_phase()
        P.begin_phase()
        ffn_phase(P, C, 1, hB, d_hB, hC, d_hC, PAD, 0)
        P.end_phase()
        P.begin_phase()
        diff_phase_a(P, C, 2, hC, d_hC, qT, d_qT, kT, d_kT, vtok, d_v)
        P.end_phase()
        P.begin_phase()
        diff_phase_b(P, C, 2, qT, d_qT, kT, d_kT, vtok, d_v, oT, d_oT)
        P.end_phase()
        P.begin_phase()
        proj_phase(P, C, 2, C.diff_w_o[0], oT, d_oT, hC, d_hC, hB, d_hB, "p2_")
        P.end_phase()
        P.begin_phase()
        ffn_phase(P, C, 2, hB, d_hB, hC, d_hC, PAD, 0)
        P.end_phase()
        P.begin_phase()
        mla_phase(P, C, 3, 1, hC, d_hC, oT, d_oT)
        P.end_phase()
        P.begin_phase()
        proj_phase(P, C, 3, C.mla_w_o[1], oT, d_oT, hC, d_hC, hB, d_hB, "p3_")
        P.end_phase()
        P.begin_phase()
        ffn_phase(P, C, 3, hB, d_hB, yT, d_y, -NMETA, NMETA)
        P.end_phase()
        stats = P.stats
    return nc, stats


_CACHE = {}


def kernel(**inputs):
    inp = {k: np.asarray(v) for k, v in inputs.items()}
    x = inp["x"].astype(np.float32, copy=False)
    B = x.shape[0]
    meta = inp["meta_tokens"].astype(np.float32, copy=False)
    if "nc" not in _CACHE:
        _CACHE["nc"] = build_program()[0]
    nc = _CACHE["nc"]
    shared = {"small": pack_small(inp)}
    shared.update(make_consts())
    for k in ("mla_w_in", "mla_w_uq", "mla_w_ukv", "mla_w_o", "sc_w_in", "sc_w_out", "diff_w_in", "diff_w_o",
              "ffn_w_up", "ffn_w_down"):
        shared[k] = np.ascontiguousarray(inp[k], dtype=np.float32)
    in_maps = []
    for b in range(B):
        h0 = np.zeros((D, LC), np.float32)
        h0[:, PAD:PAD + NMETA] = meta.T
        h0[:, PAD + NMETA:] = x[b].T
        m = dict(shared)
        m["h0"] = h0
        in_maps.append(m)
    res = run_bass_kernel_spmd(nc, in_maps, core_ids=list(range(B)))
    out = np.empty((B, SEQ, D), np.float32)
    for b in range(B):
        out[b] = np.asarray(res.results[b]["yT"]).T
    return out
```

```python
import math
import numpy as np
import concourse.bass as bass
import concourse.mybir as mybir
from concourse.bass_utils import run_bass_kernel_spmd
from contextlib import ExitStack

F32 = mybir.dt.float32
BF16 = mybir.dt.bfloat16
ALU = mybir.AluOpType
AF = mybir.ActivationFunctionType

D = 1024
SEQ = 4096
NMETA = 16
L = SEQ + NMETA
PAD = 2
LC = L + PAD
DEPTH = 4
EPS = 1e-6
FF = 2816
NKT = 33
NCORES = 8
DUP_QK = False


class Buf:
    __slots__ = ("name", "lw", "rd", "didx", "dcnt", "dphase")

    def __init__(self, name):
        self.name = name
        self.lw = None
        self.rd = []
        self.didx = -1
        self.dcnt = 0
        self.dphase = -1


class DBuf:
    __slots__ = ("name", "wd", "rdd")

    def __init__(self, name):
        self.name = name
        self.wd = {}
        self.rdd = {}


class Op:
    __slots__ = ("eng", "idx", "fn", "deps", "is_dma", "didx", "inc")

    def __init__(self, eng, idx, fn, deps, is_dma=False, didx=-1):
        self.eng = eng
        self.idx = idx
        self.fn = fn
        self.deps = deps
        self.is_dma = is_dma
        self.didx = didx
        self.inc = False


class Prog:
    ENGS = ("pe", "act", "dve", "pool", "sp")

    def __init__(self, nc, stack, npool=72):
        self.nc = nc
        self.outer = stack
        self.scope = stack
        self.esem = {e: stack.enter_context(nc.semaphore("sem_" + e)) for e in self.ENGS}
        self.pool = [stack.enter_context(nc.semaphore("dp%d" % i)) for i in range(npool)]
        self.pool_cnt = [0] * npool
        self.pool_used = 0
        self.rank_base = {e: 0 for e in self.ENGS}
        self.waited = {e: {} for e in self.ENGS}
        self.ops = {e: [] for e in self.ENGS}
        self.phase = 0
        self.stats = []

    def begin_phase(self):
        self.scope = ExitStack()
        self.scope.__enter__()

    def sbuf(self, name, shape, dt):
        return self.scope.enter_context(self.nc.sbuf_tensor(name, shape, dt))

    def psum(self, name, shape, dt=F32):
        return self.scope.enter_context(self.nc.psum_tensor(name, shape, dt))

    def buf(self, name):
        return Buf(name)

    def bufs_n(self, name, n):
        return [self.buf("%s%d" % (name, i)) for i in range(n)]

    def _deps(self, eng, reads, writes, is_dma):
        deps = []
        for b in reads:
            if b.lw is not None:
                deps.append(b.lw)
        for b in writes:
            if b.lw is not None:
                d = b.lw
                if is_dma or d[0] == "d" or d[1] != eng:
                    deps.append(d)
            for d in b.rd:
                if is_dma or d[0] == "d" or d[1] != eng:
                    deps.append(d)
        return deps

    def op(self, eng, fn, reads=(), writes=()):
        lst = self.ops[eng]
        idx = len(lst)
        deps = self._deps(eng, reads, writes, False)
        lst.append(Op(eng, idx, fn, deps))
        me = ("c", eng, idx, self.phase)
        for b in reads:
            b.rd.append(me)
        for b in writes:
            b.lw = me
            b.rd = []

    def dma(self, eng, fn, reads=(), writes=(), track=None, dreads=(), dwrites=()):
        lst = self.ops[eng]
        idx = len(lst)
        deps = self._deps(eng, reads, writes, True)
        for db in dreads:
            for t, v in db.wd.items():
                deps.append(("d", t, v))
        for db in dwrites:
            for t, v in db.rdd.items():
                deps.append(("d", t, v))
            for t, v in db.wd.items():
                deps.append(("d", t, v))
        if track.dphase != self.phase:
            track.dphase = self.phase
            track.didx = self.pool_used
            self.pool_used += 1
            assert self.pool_used <= len(self.pool), "out of dma semaphores"
            track.dcnt = self.pool_cnt[track.didx]
        track.dcnt += 16
        self.pool_cnt[track.didx] = track.dcnt
        val = track.dcnt
        di = track.didx
        lst.append(Op(eng, idx, fn, deps, True, di))
        me = ("d", di, val)
        for b in reads:
            b.rd.append(me)
        for b in writes:
            b.lw = me
            b.rd = []
        for db in dreads:
            db.rdd[di] = max(val, db.rdd.get(di, 0))
        for db in dwrites:
            db.wd[di] = max(val, db.wd.get(di, 0))

    def end_phase(self):
        nc = self.nc
        ph = self.phase
        need = {e: set() for e in self.ENGS}
        for e in self.ENGS:
            for o in self.ops[e]:
                for d in o.deps:
                    if d[0] == "c" and d[3] == ph:
                        need[d[1]].add(d[2])
        comp = [e for e in self.ENGS if e != "sp"]
        for e in comp:
            lst = self.ops[e]
            for o in reversed(lst):
                if not o.is_dma:
                    need[e].add(o.idx)
                    break
        rank = {}
        final_rank = {}
        for e in self.ENGS:
            base = self.rank_base[e]
            srt = sorted(need[e])
            for r, idx in enumerate(srt):
                rank[(e, idx)] = base + r + 1
                self.ops[e][idx].inc = True
            final_rank[e] = base + len(srt)
        final_pool = [(i, self.pool_cnt[i]) for i in range(self.pool_used)]

        def resolve(d):
            if d[0] == "c":
                if d[3] != ph:
                    return None
                return ("e", d[1]), self.esem[d[1]], rank[(d[1], d[2])]
            return ("p", d[1]), self.pool[d[1]], d[2]

        def do_waits(h, deps, waited):
            best = {}
            for d in deps:
                r = resolve(d)
                if r is None:
                    continue
                k, s, v = r
                if waited.get(k, 0) >= v:
                    continue
                if k not in best or best[k][1] < v:
                    best[k] = (s, v)
            for k, (s, v) in best.items():
                h.wait_ge(s, v)
                waited[k] = v
            return len(best)

        stats = {}

        def run(ename, h):
            waited = self.waited[ename]
            nwait = 0
            for o in self.ops[ename]:
                nwait += do_waits(h, o.deps, waited)
                ins = o.fn(h)
                if o.is_dma:
                    ins.then_inc(self.pool[o.didx], 16)
                elif o.inc:
                    ins.then_inc(self.esem[ename], 1)
            for e2 in comp:
                if e2 != ename and final_rank[e2] > waited.get(("e", e2), 0):
                    h.wait_ge(self.esem[e2], final_rank[e2])
                    waited[("e", e2)] = final_rank[e2]
            for i, v in final_pool:
                if v > waited.get(("p", i), 0):
                    h.wait_ge(self.pool[i], v)
                    waited[("p", i)] = v
            stats[ename] = (len(self.ops[ename]), nwait)

        with nc.Block() as block:
            @block.tensor
            def _(h):
                run("pe", h)

            @block.scalar
            def _(h):
                run("act", h)

            @block.vector
            def _(h):
                run("dve", h)

            @block.gpsimd
            def _(h):
                run("pool", h)

            @block.sync
            def _(h):
                run("sp", h)

        self.stats.append(stats)
        for e in self.ENGS:
            self.rank_base[e] = final_rank[e]
            self.ops[e] = []
        self.pool_used = 0
        self.phase += 1
        if self.scope is not self.outer:
            self.scope.__exit__(None, None, None)
            self.scope = self.outer


class Rot:
    def __init__(self, P, name, n, shape, dt):
        self.t = [P.sbuf("%s%d" % (name, i), shape, dt) for i in range(n)]
        self.b = [P.buf("%s%d" % (name, i)) for i in range(n)]
        self.n = n
        self.i = 0

    def next(self):
        k = self.i % self.n
        self.i += 1
        return self.t[k], self.b[k]


SP_NORMS = 0
SP_FCONV = SP_NORMS + 128
SP_SCONV = SP_FCONV + 528
SP_NQ = SP_SCONV + 24
SP_NKV = SP_NQ + 4
SP_SUBLN = SP_NKV + 2
SP_LAM = SP_SUBLN + 1
SP_TOT = SP_LAM + 256


def pack_small(inp):
    sp = np.zeros((128, SP_TOT), np.float32)
    norms = inp["norms"]
    sp[:, SP_NORMS:SP_NORMS + 128] = norms.reshape(16, 8, 128).transpose(2, 0, 1).reshape(128, 128)
    fc = inp["ffn_conv"]
    sp[:, SP_FCONV:SP_FCONV + 528] = fc.reshape(4, 3, 44, 128).transpose(3, 0, 2, 1).reshape(128, 528)
    sc = inp["sc_conv"][0]
    sp[:, SP_SCONV:SP_SCONV + 24] = sc.reshape(3, 8, 128).transpose(2, 1, 0).reshape(128, 24)
    nq = inp["mla_norm_q"]
    sp[:, SP_NQ:SP_NQ + 4] = nq.reshape(2, 2, 128).transpose(2, 0, 1).reshape(128, 4)
    nkv = inp["mla_norm_kv"]
    sp[:, SP_NKV:SP_NKV + 2] = nkv.T
    sp[:, SP_SUBLN] = inp["diff_subln"][0]
    lam = np.stack([inp["diff_lambda_q1"][0], inp["diff_lambda_k1"][0],
                    inp["diff_lambda_q2"][0], inp["diff_lambda_k2"][0]]).reshape(1, 256)
    sp[:, SP_LAM:SP_LAM + 256] = np.broadcast_to(lam, (128, 256))
    return sp


def make_consts():
    c = {}
    tri = (np.arange(128)[:, None] <= np.arange(128)[None, :]).astype(np.float32)
    c["tri"] = tri
    c["ident"] = np.eye(128, dtype=np.float32)
    c["mneg"] = ((1.0 - tri) * -30000.0).astype(np.float32)
    inv_freq = (10000.0 ** (-np.arange(0, 32, 2, dtype=np.float32) / np.float32(32))).astype(np.float32)
    ang = (np.arange(L, dtype=np.float32)[:, None] * inv_freq[None, :]).astype(np.float32)
    cos = np.cos(ang).astype(np.float32).T
    sin = np.sin(ang).astype(np.float32).T
    C = np.ones((128, L), np.float32)
    S = np.zeros((128, L), np.float32)
    C[64:80] = cos
    C[80:96] = cos
    S[64:80] = sin
    S[80:96] = sin
    c["ropeC"] = C
    c["ropeS"] = S
    pos = np.arange(L)
    hi = (pos // 64).astype(np.float32)
    lo = (pos % 64).astype(np.float32)
    aK = np.zeros((8, 4, L), np.float32)
    aQ = np.zeros((8, 4, L), np.float32)
    for h in range(8):
        slope = 2.0 ** (-(h + 1))
        aK[h, 0] = 8.0 * slope * 64.0 * hi
        aK[h, 1] = 8.0 * slope * lo
        aK[h, 2] = 1.0
        aK[h, 3] = 1.0
        aQ[h, 0] = 1.0
        aQ[h, 1] = 1.0
        aQ[h, 2] = -8.0 * slope * 64.0 * hi
        aQ[h, 3] = -8.0 * slope * lo
    c["alibiK"] = aK.reshape(32, L)
    c["alibiQ"] = aQ.reshape(32, L)
    return c


class Ctx:
    pass


def declare_io(nc, C):
    def din(name, shape, dt=F32):
        return nc.dram_tensor(name, list(shape), dt, kind="ExternalInput").ap()
    C.h0 = din("h0", [D, LC])
    C.small = din("small", [128, SP_TOT])
    C.tri = din("tri", [128, 128])
    C.ident = din("ident", [128, 128])
    C.mneg = din("mneg", [128, 128])
    C.ropeC = din("ropeC", [128, L])
    C.ropeS = din("ropeS", [128, L])
    C.alibiK = din("alibiK", [32, L])
    C.alibiQ = din("alibiQ", [32, L])
    C.mla_w_in = din("mla_w_in", [2, D, 416])
    C.mla_w_uq = din("mla_w_uq", [2, 256, 1536])
    C.mla_w_ukv = din("mla_w_ukv", [2, 128, 2048])
    C.mla_w_o = din("mla_w_o", [2, D, D])
    C.sc_w_in = din("sc_w_in", [1, D, 3 * D])
    C.sc_w_out = din("sc_w_out", [1, D, D])
    C.diff_w_in = din("diff_w_in", [1, D, 3 * D])
    C.diff_w_o = din("diff_w_o", [1, D, D])
    C.ffn_w_up = din("ffn_w_up", [4, D, 2 * FF])
    C.ffn_w_down = din("ffn_w_down", [4, FF, D])


def setup_common(P, C):
    nc = P.nc
    C.small_t = P.sbuf("small_t", [128, SP_TOT], F32)
    C.b_small = P.buf("small")
    P.dma("sp", lambda h: h.dma_start(out=C.small_t[:], in_=C.small), writes=[C.b_small], track=C.b_small)
    C.ones_bf = P.sbuf("ones_bf", [128, 128], BF16)
    C.b_ones = P.buf("ones_bf")
    P.op("pool", lambda h: h.memset(C.ones_bf[:], 1.0), writes=[C.b_ones])
    C.ones_f = P.sbuf("ones_f", [128, 128], F32)
    C.b_onesf = P.buf("ones_f")
    P.op("pool", lambda h: h.memset(C.ones_f[:], 1.0), writes=[C.b_onesf])
    C.tri_bf = P.sbuf("tri_bf", [128, 128], BF16)
    C.b_tri = P.buf("tri_bf")
    P.dma("pool", lambda h: h.dma_start(out=C.tri_bf[:], in_=C.tri), writes=[C.b_tri], track=C.b_tri)
    C.eps_t = P.sbuf("eps_t", [128, 1], F32)
    C.b_eps = P.buf("eps_t")
    P.op("pool", lambda h: h.memset(C.eps_t[:], EPS), writes=[C.b_eps])
    C.ident_bf = P.sbuf("ident_bf", [128, 128], BF16)
    C.b_ident = P.buf("ident_bf")
    P.dma("pool", lambda h: h.dma_start(out=C.ident_bf[:], in_=C.ident), writes=[C.b_ident], track=C.b_ident)
    C.mneg_bf = P.sbuf("mneg_bf", [128, 128], BF16)
    C.b_mneg = P.buf("mneg_bf")
    P.dma("pool", lambda h: h.dma_start(out=C.mneg_bf[:], in_=C.mneg), writes=[C.b_mneg], track=C.b_mneg)
    C.ps = P.psum("ps", [128, 8, 512], F32)
    C.b_ps = P.bufs_n("psb", 8)


def gain(C, l, j, c):
    col = SP_NORMS + (l * 4 + j) * 8 + c
    return C.small_t[:, col:col + 1]


def rstd_from_psum(P, C, st_ps, b_st, n, rs_t, b_rs, dim):
    P.op("act", lambda h: h.activation(rs_t[:, :n], st_ps[:, :n], AF.Ln, bias=C.eps_t[:, 0:1], scale=1.0 / dim),
         reads=[b_st, C.b_eps], writes=[b_rs])
    P.op("act", lambda h: h.activation(rs_t[:, :n], rs_t[:, :n], AF.Exp, scale=-0.5), reads=[b_rs], writes=[b_rs])


def prenorm(P, C, W, l, j, h_in, db_in, c0, n, xn, b_xn):
    st_ps = C.ps[:, 7, :]
    b_st = C.b_ps[7]
    rs1, b_rs1 = W.rs1, W.b_rs1
    for c in range(8):
        t, bt = W.hin.next()
        P.dma("sp", lambda h, t=t, c=c: h.dma_start(out=t[:, :n], in_=h_in[c * 128:(c + 1) * 128, c0:c0 + n]),
              writes=[bt], track=bt, dreads=[db_in])
        s, bs = W.sq.next()
        P.op("act", lambda h, t=t, s=s: h.activation(s[:, :n], t[:, :n], AF.Square), reads=[bt], writes=[bs])
        P.op("pe", lambda h, s=s, c=c: h.matmul(st_ps[:, :n], C.ones_bf[:], s[:, :n], start=(c == 0), stop=(c == 7)),
             reads=[bs, C.b_ones], writes=[b_st])
    rstd_from_psum(P, C, st_ps, b_st, n, rs1, b_rs1, float(D))
    for c in range(8):
        t, bt = W.hin.next()
        P.dma("sp", lambda h, t=t, c=c: h.dma_start(out=t[:, :n], in_=h_in[c * 128:(c + 1) * 128, c0:c0 + n]),
              writes=[bt], track=bt, dreads=[db_in])
        P.op("dve", lambda h, t=t, c=c: h.scalar_tensor_tensor(xn[:, c, :n], t[:, :n], gain(C, l, j, c),
                                                                rs1[:, :n], ALU.mult, ALU.mult),
             reads=[bt, b_rs1, C.b_small], writes=[b_xn])


def tail(P, C, W, l, j, mm_chunk, nv, h_in, db_in, cin0, h_out, db_out, cout0, bank0):
    st_ps = C.ps[:, 7, :]
    b_st = C.b_ps[7]
    lag = None
    flip_f(W)
    f_t, b_f, rs2, b_rs2 = W.f, W.b_f, W.rs2, W.b_rs2
    for c in range(8):
        bk = bank0 + (c % 2)
        ps = C.ps[:, bk, :]
        bps = C.b_ps[bk]
        mm_chunk(c, ps, bps)
        P.op("act", lambda h, ps=ps, c=c: h.activation(f_t[:, c, :nv], ps[:, :nv], AF.Copy), reads=[bps], writes=[b_f])
        s, bs = W.sq.next()
        P.op("act", lambda h, ps=ps, s=s: h.activation(s[:, :nv], ps[:, :nv], AF.Square), reads=[bps], writes=[bs])
        if lag is not None:
            lag()
        lag = (lambda s=s, bs=bs, c=c: P.op(
            "pe", lambda h: h.matmul(st_ps[:, :nv], C.ones_bf[:], s[:, :nv], start=(c == 0), stop=(c == 7)),
            reads=[bs, C.b_ones], writes=[b_st]))
    lag()
    rstd_from_psum(P, C, st_ps, b_st, nv, rs2, b_rs2, float(D))
    for c in range(8):
        t, bt = W.hin.next()
        P.dma("sp", lambda h, t=t, c=c: h.dma_start(out=t[:, :nv], in_=h_in[c * 128:(c + 1) * 128, cin0:cin0 + nv]),
              writes=[bt], track=bt, dreads=[db_in])
        o, bo = W.out.next()
        P.op("dve", lambda h, o=o, c=c: h.scalar_tensor_tensor(o[:, :nv], f_t[:, c, :nv], gain(C, l, j, c),
                                                                rs2[:, :nv], ALU.mult, ALU.mult),
             reads=[b_f, b_rs2, C.b_small], writes=[bo])
        P.op("pool", lambda h, o=o, t=t: h.tensor_tensor(o[:, :nv], o[:, :nv], t[:, :nv], ALU.add),
             reads=[bo, bt], writes=[bo])
        P.dma("pool", lambda h, o=o, c=c: h.dma_start(out=h_out[c * 128:(c + 1) * 128, cout0:cout0 + nv], in_=o[:, :nv]),
              reads=[bo], track=bo, dwrites=[db_out])


class Work:
    pass


def common_work(P, W, nmax, tag, nbuf=1, need_f=True, nh=3, no=3):
    W.hin = Rot(P, tag + "hin", nh, [128, nmax], F32)
    W.out = Rot(P, tag + "out", no, [128, nmax], F32)
    W.sq = Rot(P, tag + "sq", 2, [128, nmax], BF16)
    W.xs = []
    W.fs = []
    for i in range(nbuf):
        W.xs.append((P.sbuf(tag + "xn%d" % i, [128, 8, nmax], BF16), P.buf(tag + "xn"),
                     P.sbuf(tag + "rs1%d" % i, [128, nmax], F32), P.buf(tag + "rs1")))
        if need_f:
            W.fs.append((P.sbuf(tag + "f%d" % i, [128, 8, nmax], F32), P.buf(tag + "f"),
                         P.sbuf(tag + "rs2%d" % i, [128, nmax], F32), P.buf(tag + "rs2")))
    W.xi = -1
    W.fi = -1
    flip_x(W)
    if need_f:
        flip_f(W)


def flip_x(W):
    W.xi = (W.xi + 1) % len(W.xs)
    W.xn, W.b_xn, W.rs1, W.b_rs1 = W.xs[W.xi]


def set_x(W, g):
    W.xi = g % len(W.xs)
    W.xn, W.b_xn, W.rs1, W.b_rs1 = W.xs[W.xi]


def flip_f(W):
    W.fi = (W.fi + 1) % len(W.fs)
    W.f, W.b_f, W.rs2, W.b_rs2 = W.fs[W.fi]


def load_w(P, t, b, src, kchunks, per):
    v = src.rearrange("(k p) f -> p k f", p=128)
    for k0 in range(0, kchunks, per):
        k1 = min(kchunks, k0 + per)
        P.dma("pool", lambda h, k0=k0, k1=k1: h.dma_start(out=t[:, k0:k1, :], in_=v[:, k0:k1, :]), writes=[b], track=b)


FG = 412


def ffn_phase(P, C, l, h_in, db_in, h_out, db_out, out_off, out_lo):
    nmax = FG + 2
    W = Work()
    tag = "f%d_" % l
    W.hin = Rot(P, tag + "hin", 5, [128, nmax], F32)
    W.out = Rot(P, tag + "out", 2, [128, nmax], F32)
    W.sq = Rot(P, tag + "sq", 2, [128, nmax], BF16)
    xns = [(P.sbuf(tag + "xn%d" % i, [128, 8, nmax], BF16), P.buf(tag + "xn")) for i in range(2)]
    rs1 = P.sbuf(tag + "rs1", [128, nmax], F32)
    b_rs1 = P.buf(tag + "rs1")
    rs2 = P.sbuf(tag + "rs2", [128, nmax], F32)
    b_rs2 = P.buf(tag + "rs2")
    f_t = P.sbuf(tag + "f", [128, 8, nmax], F32)
    b_f = P.buf(tag + "f")
    wup = P.sbuf(tag + "wup", [128, 8, 2 * FF], BF16)
    b_wup = P.buf(tag + "wup")
    wdn = P.sbuf(tag + "wdn", [128, 22, D], BF16)
    b_wdn = P.buf(tag + "wdn")
    load_w(P, wup, b_wup, C.ffn_w_up[l], 8, 1)
    load_w(P, wdn, b_wdn, C.ffn_w_down[l], 22, 6)
    gc = Rot(P, tag + "gc", 2, [128, nmax], F32)
    uc = Rot(P, tag + "uc", 2, [128, nmax], F32)
    act = P.sbuf(tag + "act", [128, 22, nmax], BF16)
    b_act = P.buf(tag + "act")
    st1 = C.ps[:, 7, :]
    b_st1 = C.b_ps[7]
    st2 = C.ps[:, 6, :]
    b_st2 = C.b_ps[6]

    def cw(ch, j):
        col = SP_FCONV + ((l * 44) + ch) * 3 + j
        return C.small_t[:, col:col + 1]

    ngroups = (L + FG - 1) // FG

    def gdims(g):
        c0 = FG * g
        n = min(nmax, LC - c0)
        return c0, n, n - 2

    def pre_sq(g, c):
        c0, n, nv = gdims(g)
        t, bt = W.hin.next()
        P.dma("sp", lambda h: h.dma_start(out=t[:, :n], in_=h_in[c * 128:(c + 1) * 128, c0:c0 + n]),
              writes=[bt], track=bt, dreads=[db_in])
        s_, bs = W.sq.next()
        P.op("act", lambda h: h.activation(s_[:, :n], t[:, :n], AF.Square), reads=[bt], writes=[bs])
        return (s_, bs)

    def pre_mm(g, c, sb):
        c0, n, nv = gdims(g)
        s_, bs = sb
        P.op("pe", lambda h: h.matmul(st1[:, :n], C.ones_bf[:], s_[:, :n], start=(c == 0), stop=(c == 7)),
             reads=[bs, C.b_ones], writes=[b_st1])

    def pre_chain(g):
        c0, n, nv = gdims(g)
        rstd_from_psum(P, C, st1, b_st1, n, rs1, b_rs1, float(D))

    def pre_xn(g, c):
        c0, n, nv = gdims(g)
        xn, b_xn = xns[g % 2]
        t, bt = W.hin.next()
        P.dma("sp", lambda h: h.dma_start(out=t[:, :n], in_=h_in[c * 128:(c + 1) * 128, c0:c0 + n]),
              writes=[bt], track=bt, dreads=[db_in])
        P.op("dve", lambda h: h.scalar_tensor_tensor(xn[:, c, :n], t[:, :n], gain(C, l, 2, c), rs1[:, :n], ALU.mult, ALU.mult),
             reads=[bt, b_rs1, C.b_small], writes=[b_xn])

    def pre_hooks(g):
        hk = {}
        sqs = {}

        def mk_sq(c):
            def f():
                sqs[c] = pre_sq(g, c)
            return f

        def mk_mm(c):
            return lambda: pre_mm(g, c, sqs[c])

        for c in range(8):
            hk.setdefault(4 + c, []).append(mk_sq(c))
            hk.setdefault(5 + c, []).append(mk_mm(c))
        hk.setdefault(12, []).append(lambda: pre_chain(g))
        for c in range(8):
            hk.setdefault(13 + c, []).append(lambda c=c: pre_xn(g, c))
        return hk

    def tail_chunk(g, c):
        c0, n, nv = gdims(g)
        p0 = c0
        skip = max(p0, out_lo) - p0
        cin0 = c0 + 2
        cout0 = p0 + out_off
        t, bt = W.hin.next()
        P.dma("sp", lambda h: h.dma_start(out=t[:, :nv], in_=h_in[c * 128:(c + 1) * 128, cin0:cin0 + nv]),
              writes=[bt], track=bt, dreads=[db_in])
        o, bo = W.out.next()
        P.op("dve", lambda h: h.scalar_tensor_tensor(o[:, :nv], f_t[:, c, :nv], gain(C, l, 3, c), rs2[:, :nv], ALU.mult, ALU.mult),
             reads=[b_f, b_rs2, C.b_small], writes=[bo])
        P.op("pool", lambda h: h.tensor_tensor(o[:, :nv], o[:, :nv], t[:, :nv], ALU.add), reads=[bo, bt], writes=[bo])
        P.dma("pool", lambda h: h.dma_start(out=h_out[c * 128:(c + 1) * 128, cout0 + skip:cout0 + nv], in_=o[:, skip:nv]),
              reads=[bo], track=bo, dwrites=[db_out])

    def mid(g, hooks):
        c0, n, nv = gdims(g)
        xn, b_xn = xns[g % 2]
        for jp in range(22):
            banks = [(jp % 3) * 2, (jp % 3) * 2 + 1]
            tiles = []
            for which, ch in enumerate((jp, 22 + jp)):
                ps = C.ps[:, banks[which], :]
                bps = C.b_ps[banks[which]]
                for k in range(8):
                    P.op("pe", lambda h, ps=ps, k=k, ch=ch: h.matmul(ps[:, :n], wup[:, k, ch * 128:(ch + 1) * 128],
                                                                      xn[:, k, :n], start=(k == 0), stop=(k == 7)),
                         reads=[b_wup, b_xn], writes=[bps])
                tiles.append((ps, bps, ch))
            g_t, b_g = gc.next()
            u_t, b_u = uc.next()
            for (ps, bps, ch), (d_t, b_d) in zip(tiles, ((g_t, b_g), (u_t, b_u))):
                P.op("act", lambda h, ps=ps, ch=ch, d_t=d_t: h.activation(d_t[:, :nv], ps[:, 2:n], AF.Copy, scale=cw(ch, 2)),
                     reads=[bps, C.b_small], writes=[b_d])
            for tap in (1, 0):
                for (ps, bps, ch), (d_t, b_d) in zip(tiles, ((g_t, b_g), (u_t, b_u))):
                    P.op("dve", lambda h, ps=ps, ch=ch, d_t=d_t, tap=tap: h.scalar_tensor_tensor(
                        d_t[:, :nv], ps[:, tap:tap + nv], cw(ch, tap), d_t[:, :nv], ALU.mult, ALU.add),
                        reads=[bps, b_d, C.b_small], writes=[b_d])
            P.op("act", lambda h, g_t=g_t: h.activation(g_t[:, :nv], g_t[:, :nv], AF.Silu), reads=[b_g], writes=[b_g])
            P.op("pool", lambda h, g_t=g_t, u_t=u_t, jp=jp: h.tensor_tensor(act[:, jp, :nv], g_t[:, :nv], u_t[:, :nv], ALU.mult),
                 reads=[b_g, b_u], writes=[b_act])
            for fn in hooks.get(jp, ()):
                fn()

    def back(g):
        c0, n, nv = gdims(g)
        lag = None
        for c in range(8):
            bk = 2 + (c % 2)
            ps = C.ps[:, bk, :]
            bps = C.b_ps[bk]
            for k in range(22):
                P.op("pe", lambda h, ps=ps, k=k, c=c: h.matmul(ps[:, :nv], wdn[:, k, c * 128:(c + 1) * 128], act[:, k, :nv],
                                                                start=(k == 0), stop=(k == 21)),
                     reads=[b_wdn, b_act], writes=[bps])
            P.op("act", lambda h, ps=ps, c=c: h.activation(f_t[:, c, :nv], ps[:, :nv], AF.Copy), reads=[bps], writes=[b_f])
            s_, bs = W.sq.next()
            P.op("act", lambda h, ps=ps, s_=s_: h.activation(s_[:, :nv], ps[:, :nv], AF.Square), reads=[bps], writes=[bs])
            if lag is not None:
                lag()
            lag = (lambda s_=s_, bs=bs, c=c: P.op(
                "pe", lambda h: h.matmul(st2[:, :nv], C.ones_bf[:], s_[:, :nv], start=(c == 0), stop=(c == 7)),
                reads=[bs, C.b_ones], writes=[b_st2]))
        lag()
        rstd_from_psum(P, C, st2, b_st2, nv, rs2, b_rs2, float(D))

    for c in range(8):
        pre_mm(0, c, pre_sq(0, c))
    pre_chain(0)
    for c in range(8):
        pre_xn(0, c)
    for g in range(ngroups):
        hooks = pre_hooks(g + 1) if g + 1 < ngroups else {}
        if g > 0:
            for c in range(8):
                hooks.setdefault(c, []).insert(0, (lambda c=c, gg=g - 1: tail_chunk(gg, c)))
        mid(g, hooks)
        back(g)
    for c in range(8):
        tail_chunk(ngroups - 1, c)


def tail_skip(P, C, W, l, j, mm_chunk, nv, skip, h_in, db_in, cin0, h_out, db_out, cout0, bank0):
    st_ps = C.ps[:, 7, :]
    b_st = C.b_ps[7]
    lag = None
    flip_f(W)
    f_t, b_f, rs2, b_rs2 = W.f, W.b_f, W.rs2, W.b_rs2
    for c in range(8):
        bk = bank0 + (c % 2)
        ps = C.ps[:, bk, :]
        bps = C.b_ps[bk]
        mm_chunk(c, ps, bps)
        P.op("act", lambda h, ps=ps, c=c: h.activation(f_t[:, c, :nv], ps[:, :nv], AF.Copy), reads=[bps], writes=[b_f])
        s, bs = W.sq.next()
        P.op("act", lambda h, ps=ps, s=s: h.activation(s[:, :nv], ps[:, :nv], AF.Square), reads=[bps], writes=[bs])
        if lag is not None:
            lag()
        lag = (lambda s=s, bs=bs, c=c: P.op(
            "pe", lambda h: h.matmul(st_ps[:, :nv], C.ones_bf[:], s[:, :nv], start=(c == 0), stop=(c == 7)),
            reads=[bs, C.b_ones], writes=[b_st]))
    lag()
    rstd_from_psum(P, C, st_ps, b_st, nv, rs2, b_rs2, float(D))
    for c in range(8):
        t, bt = W.hin.next()
        P.dma("sp", lambda h, t=t, c=c: h.dma_start(out=t[:, :nv], in_=h_in[c * 128:(c + 1) * 128, cin0:cin0 + nv]),
              writes=[bt], track=bt, dreads=[db_in])
        o, bo = W.out.next()
        P.op("dve", lambda h, o=o, c=c: h.scalar_tensor_tensor(o[:, :nv], f_t[:, c, :nv], gain(C, l, j, c),
                                                                rs2[:, :nv], ALU.mult, ALU.mult),
             reads=[b_f, b_rs2, C.b_small], writes=[bo])
        P.op("pool", lambda h, o=o, t=t: h.tensor_tensor(o[:, :nv], o[:, :nv], t[:, :nv], ALU.add),
             reads=[bo, bt], writes=[bo])
        P.dma("sp", lambda h, o=o, c=c: h.dma_start(out=h_out[c * 128:(c + 1) * 128, cout0 + skip:cout0 + nv],
                                                     in_=o[:, skip:nv]),
              reads=[bo], track=bo, dwrites=[db_out])


def proj_phase(P, C, l, w_src, oT, db_o, h_in, db_in, h_out, db_out, tag):
    W = Work()
    common_work(P, W, 512, tag, nbuf=2, nh=8, no=4)
    wo = P.sbuf(tag + "wo", [128, 8, D], BF16)
    b_wo = P.buf(tag + "wo")
    load_w(P, wo, b_wo, w_src, 8, 4)
    og = Rot(P, tag + "og", 2, [128, 8, 512], BF16)
    ov = oT.rearrange("(c p) t -> p c t", p=128)

    def group(g):
        t0 = 512 * g
        n = min(512, L - t0)
        o_t, b_o = og.next()
        P.dma("sp", lambda h: h.dma_start(out=o_t[:, :, :n], in_=ov[:, :, t0:t0 + n]), writes=[b_o], track=b_o, dreads=[db_o])

        def mm_chunk(c, ps, bps):
            for k in range(8):
                P.op("pe", lambda h, k=k: h.matmul(ps[:, :n], wo[:, k, c * 128:(c + 1) * 128], o_t[:, k, :n],
                                                   start=(k == 0), stop=(k == 7)),
                     reads=[b_wo, b_o], writes=[bps])

        tail(P, C, W, l, 1, mm_chunk, n, h_in, db_in, PAD + t0, h_out, db_out, PAD + t0, 4)

    for g in range((L + 511) // 512):
        group(g)


SG = 510


def sc_phase(P, C, l, h_in, db_in, h_out, db_out):
    tag = "sc_"
    nmax = SG + 2
    W = Work()
    common_work(P, W, nmax, tag, nbuf=2, nh=6, no=4)
    win = P.sbuf(tag + "win", [128, 8, 3 * D], BF16)
    b_win = P.buf(tag + "win")
    wout = P.sbuf(tag + "wout", [128, 8, D], BF16)
    b_wout = P.buf(tag + "wout")
    load_w(P, win, b_win, C.sc_w_in[0], 8, 1)
    load_w(P, wout, b_wout, C.sc_w_out[0], 8, 4)
    ub = Rot(P, tag + "ub", 2, [128, nmax], F32)
    cu = Rot(P, tag + "cu", 2, [128, nmax], F32)
    yy = Rot(P, tag + "yy", 2, [128, nmax], F32)
    yb = P.sbuf(tag + "yb", [128, 8, nmax], BF16)
    b_yb = P.buf(tag + "yb")

    def cw(c, j):
        col = SP_SCONV + c * 3 + j
        return C.small_t[:, col:col + 1]

    def gdims(g):
        c0 = SG * g
        n = min(nmax, LC - c0)
        return c0, n, n - 2

    def front(g):
        c0, n, nv = gdims(g)
        set_x(W, g)
        prenorm(P, C, W, l, 0, h_in, db_in, c0, n, W.xn, W.b_xn)

    def mid(g):
        c0, n, nv = gdims(g)
        xn, b_xn = W.xs[g % 2][0], W.xs[g % 2][1]
        for i in range(8):
            bks = [(i % 2) * 3 + q for q in range(3)]
            pss = []
            for q in range(3):
                ps = C.ps[:, bks[q], :]
                bps = C.b_ps[bks[q]]
                col = q * D + i * 128
                for k in range(8):
                    P.op("pe", lambda h, ps=ps, k=k, col=col: h.matmul(ps[:, :n], win[:, k, col:col + 128], xn[:, k, :n],
                                                                        start=(k == 0), stop=(k == 7)),
                         reads=[b_win, b_xn], writes=[bps])
                pss.append((ps, bps))
            (pb, bpb), (pc, bpc), (pu, bpu) = pss
            u_t, b_u = ub.next()
            c_t, b_c = cu.next()
            y_t, b_y = yy.next()
            P.op("act", lambda h, pu=pu, u_t=u_t: h.activation(u_t[:, :n], pu[:, :n], AF.Copy), reads=[bpu], writes=[b_u])
            P.op("dve", lambda h, pc=pc, u_t=u_t, c_t=c_t: h.tensor_tensor(c_t[:, :n], pc[:, :n], u_t[:, :n], ALU.mult),
                 reads=[bpc, b_u], writes=[b_c])
            P.op("act", lambda h, c_t=c_t, y_t=y_t, i=i: h.activation(y_t[:, :nv], c_t[:, 2:n], AF.Copy, scale=cw(i, 2)),
                 reads=[b_c, C.b_small], writes=[b_y])
            for tap in (1, 0):
                P.op("dve", lambda h, c_t=c_t, y_t=y_t, i=i, tap=tap: h.scalar_tensor_tensor(
                    y_t[:, :nv], c_t[:, tap:tap + nv], cw(i, tap), y_t[:, :nv], ALU.mult, ALU.add),
                    reads=[b_c, b_y, C.b_small], writes=[b_y])
            P.op("dve", lambda h, pb=pb, y_t=y_t, i=i: h.tensor_tensor(yb[:, i, :nv], pb[:, 2:n], y_t[:, :nv], ALU.mult),
                 reads=[bpb, b_y], writes=[b_yb])

    def back(g):
        c0, n, nv = gdims(g)

        def mm_chunk(c, ps, bps):
            for k in range(8):
                P.op("pe", lambda h, k=k: h.matmul(ps[:, :nv], wout[:, k, c * 128:(c + 1) * 128], yb[:, k, :nv],
                                                   start=(k == 0), stop=(k == 7)),
                     reads=[b_wout, b_yb], writes=[bps])

        tail(P, C, W, l, 1, mm_chunk, nv, h_in, db_in, c0 + 2, h_out, db_out, c0 + 2, 4)

    ngroups = (L + SG - 1) // SG
    front(0)
    for g in range(ngroups):
        if g + 1 < ngroups:
            front(g + 1)
        mid(g)
        back(g)


def mla_phase(P, C, l, j, h_in, db_in, oT, db_o):
    tag = "ml%d_" % l
    W = Work()
    W.hin = Rot(P, tag + "hin", 3, [128, 512], F32)
    W.sq = Rot(P, tag + "sq", 2, [128, 512], BF16)
    W.xs = [(P.sbuf(tag + "xn%d" % i, [128, 8, 512], BF16), P.buf(tag + "xn"),
             P.sbuf(tag + "rs1%d" % i, [128, 512], F32), P.buf(tag + "rs1")) for i in range(2)]
    W.xi = -1
    rsq = P.sbuf(tag + "rsq", [128, 512], F32)
    b_rsq = P.buf("rsq")
    rskv = P.sbuf(tag + "rskv", [128, 512], F32)
    b_rskv = P.buf("rskv")
    win = P.sbuf(tag + "win", [128, 8, 416], BF16)
    b_win = P.buf("win")
    load_w(P, win, b_win, C.mla_w_in[j], 8, 8)
    wkr = P.sbuf(tag + "wkr", [128, 8, 96], BF16)
    b_wkr = P.buf("wkr")
    wkrr = P.sbuf(tag + "wkrr", [128, 8, 96], BF16)
    b_wkrr = P.buf("wkrr")
    P.op("pool", lambda h: h.memset(wkr[:], 0.0), writes=[b_wkr])
    P.op("pool", lambda h: h.memset(wkrr[:], 0.0), writes=[b_wkrr])
    P.op("dve", lambda h: h.tensor_copy(wkr[:, :, 64:96], win[:, :, 384:416]), reads=[b_win], writes=[b_wkr])
    P.op("dve", lambda h: h.tensor_scalar(wkrr[:, :, 64:80], win[:, :, 400:416], -1.0, None, ALU.mult), reads=[b_win], writes=[b_wkrr])
    P.op("dve", lambda h: h.tensor_copy(wkrr[:, :, 80:96], win[:, :, 384:400]), reads=[b_win], writes=[b_wkrr])
    wuq = P.sbuf(tag + "wuq", [128, 2, 16, 96], BF16)
    b_wuq = P.buf("wuq")
    wuqr = P.sbuf(tag + "wuqr", [128, 2, 16, 96], BF16)
    b_wuqr = P.buf("wuqr")
    for c in range(2):
        src = C.mla_w_uq[j][c * 128:(c + 1) * 128, :].rearrange("p (h e) -> p h e", e=96)
        P.dma("pool", lambda h, c=c, src=src: h.dma_start(out=wuq[:, c, :, :], in_=src), writes=[b_wuq], track=b_wuq)
    P.op("pool", lambda h: h.memset(wuqr[:], 0.0), writes=[b_wuqr])
    for c in range(2):
        P.op("dve", lambda h, c=c: h.tensor_scalar(wuqr[:, c, :, 64:80], wuq[:, c, :, 80:96], -1.0, None, ALU.mult),
             reads=[b_wuq], writes=[b_wuqr])
        P.op("dve", lambda h, c=c: h.tensor_copy(wuqr[:, c, :, 80:96], wuq[:, c, :, 64:80]), reads=[b_wuq], writes=[b_wuqr])
    wukv = P.sbuf(tag + "wukv", [128, 2048], BF16)
    b_wukv = P.buf("wukv")
    P.dma("pool", lambda h: h.dma_start(out=wukv[:], in_=C.mla_w_ukv[j]), writes=[b_wukv], track=b_wukv)
    rc = P.sbuf(tag + "rc", [128, L], F32)
    b_rc = P.buf("rc")
    rsn = P.sbuf(tag + "rsn", [128, L], F32)
    b_rsn = P.buf("rsn")
    P.dma("sp", lambda h: h.dma_start(out=rc[:], in_=C.ropeC), writes=[b_rc], track=b_rc)
    P.dma("sp", lambda h: h.dma_start(out=rsn[:], in_=C.ropeS), writes=[b_rsn], track=b_rsn)
    cqn = P.sbuf(tag + "cqn", [128, 2, L], BF16)
    b_cqn = P.buf("cqn")
    ckvn = P.sbuf(tag + "ckvn", [128, L], BF16)
    b_ckvn = P.buf("ckvn")
    KT = [P.sbuf(tag + "KT%d" % i, [128, L], BF16) for i in range(2)]
    b_KT = P.bufs_n("KT", 2)
    QT = [P.sbuf(tag + "QT%d" % i, [128, L], BF16) for i in range(2)]
    b_QT = P.bufs_n("QT", 2)
    VT = [P.sbuf(tag + "VT%d" % i, [128, NKT, 65], BF16) for i in range(2)]
    b_VT = P.bufs_n("VT", 2)
    for i in range(2):
        P.op("pool", lambda h, i=i: h.memset(VT[i][:, :, 64:65], 1.0), writes=[b_VT[i]])
    t1 = Rot(P, tag + "t1", 2, [128, 512], F32)
    t2 = Rot(P, tag + "t2", 2, [128, 512], F32)
    pT = Rot(P, tag + "pT", 6, [128, 512], BF16)
    osb = Rot(P, tag + "osb", 3, [128, 512], F32)
    rsum = Rot(P, tag + "rsum", 3, [128, 512], F32)
    onb = Rot(P, tag + "onb", 3, [64, 512], BF16)

    def nq_col(c):
        col = SP_NQ + j * 2 + c
        return C.small_t[:, col:col + 1]

    nkv_col = C.small_t[:, SP_NKV + j:SP_NKV + j + 1]
    ngr = (L + 511) // 512

    def frontA(g):
        t0 = 512 * g
        n = min(512, L - t0)
        flip_x(W)
        prenorm(P, C, W, l, 0, h_in, db_in, PAD + t0, n, W.xn, W.b_xn)

    def groupA(g):
        t0 = 512 * g
        n = min(512, L - t0)
        xn, b_xn = W.xs[g % 2][0], W.xs[g % 2][1]
        specs = [(0, win, b_win, 0, 128, 128), (1, win, b_win, 128, 128, 128), (2, win, b_win, 256, 128, 128)]
        for bk, wt, bw, col, m, _ in specs:
            ps = C.ps[:, bk, :]
            for k in range(8):
                P.op("pe", lambda h, ps=ps, k=k, col=col: h.matmul(ps[:, :n], win[:, k, col:col + 128], xn[:, k, :n],
                                                                    start=(k == 0), stop=(k == 7)),
                     reads=[b_win, b_xn], writes=[C.b_ps[bk]])
        for bk, wt, bw in ((3, wkr, b_wkr), (4, wkrr, b_wkrr)):
            ps = C.ps[:, bk, :]
            for k in range(8):
                P.op("pe", lambda h, ps=ps, k=k, wt=wt: h.matmul(ps[0:96, :n], wt[:, k, :], xn[:, k, :n],
                                                                  start=(k == 0), stop=(k == 7)),
                     reads=[bw, b_xn], writes=[C.b_ps[bk]])
        for c in range(2):
            s, bs = W.sq.next()
            P.op("act", lambda h, s=s, c=c: h.activation(s[:, :n], C.ps[:, c, :n], AF.Square), reads=[C.b_ps[c]], writes=[bs])
            P.op("pe", lambda h, s=s, c=c: h.matmul(C.ps[:, 5, :n], C.ones_bf[:], s[:, :n], start=(c == 0), stop=(c == 1)),
                 reads=[bs, C.b_ones], writes=[C.b_ps[5]])
        rstd_from_psum(P, C, C.ps[:, 5, :], C.b_ps[5], n, rsq, b_rsq, 256.0)
        for c in range(2):
            P.op("dve", lambda h, c=c: h.scalar_tensor_tensor(cqn[:, c, t0:t0 + n], C.ps[:, c, :n], nq_col(c), rsq[:, :n],
                                                              ALU.mult, ALU.mult),
                 reads=[C.b_ps[c], b_rsq, C.b_small], writes=[b_cqn])
        s, bs = W.sq.next()
        P.op("act", lambda h, s=s: h.activation(s[:, :n], C.ps[:, 2, :n], AF.Square), reads=[C.b_ps[2]], writes=[bs])
        P.op("pe", lambda h, s=s: h.matmul(C.ps[:, 6, :n], C.ones_bf[:], s[:, :n], start=True, stop=True),
             reads=[bs, C.b_ones], writes=[C.b_ps[6]])
        rstd_from_psum(P, C, C.ps[:, 6, :], C.b_ps[6], n, rskv, b_rskv, 128.0)
        P.op("dve", lambda h: h.scalar_tensor_tensor(ckvn[:, t0:t0 + n], C.ps[:, 2, :n], nkv_col, rskv[:, :n], ALU.mult, ALU.mult),
             reads=[C.b_ps[2], b_rskv, C.b_small], writes=[b_ckvn])
        a_t, b_a = t1.next()
        b_t, b_b = t2.next()
        P.op("dve", lambda h: h.tensor_tensor(a_t[64:96, :n], C.ps[64:96, 3, :n], rc[64:96, t0:t0 + n], ALU.mult),
             reads=[C.b_ps[3], b_rc], writes=[b_a])
        P.op("dve", lambda h: h.tensor_tensor(b_t[64:96, :n], C.ps[64:96, 4, :n], rsn[64:96, t0:t0 + n], ALU.mult),
             reads=[C.b_ps[4], b_rsn], writes=[b_b])
        for i in range(2):
            P.op("pool", lambda h, i=i: h.tensor_tensor(KT[i][64:96, t0:t0 + n], a_t[64:96, :n], b_t[64:96, :n], ALU.add),
                 reads=[b_a, b_b], writes=[b_KT[i]])

    frontA(0)
    for g in range(ngr):
        if g + 1 < ngr:
            frontA(g + 1)
        groupA(g)

    scale = 96.0 ** -0.5

    def head_proj_v(hd, i, kt):
        nk = min(128, L - 128 * kt)
        bk = (3, 6)[kt % 2]
        P.op("pe", lambda h: h.matmul(C.ps[0:nk, bk, 0:64], ckvn[:, kt * 128:kt * 128 + nk],
                                      wukv[:, hd * 128 + 64:hd * 128 + 128], start=True, stop=True),
             reads=[b_ckvn, b_wukv], writes=[C.b_ps[bk]])
        P.op("dve", lambda h: h.tensor_copy(VT[i][0:nk, kt, 0:64], C.ps[0:nk, bk, 0:64]),
             reads=[C.b_ps[bk]], writes=[b_VT[i]])

    def head_proj_items(hd, i):
        items = []
        for g in range(ngr):
            items.append(lambda g=g: head_proj_g(hd, i, g))
        for kt0 in range(0, NKT, 2):
            def vv(kt0=kt0):
                for kt in range(kt0, min(NKT, kt0 + 2)):
                    head_proj_v(hd, i, kt)
            items.append(vv)
        return items

    def head_proj(hd, i):
        for it in head_proj_items(hd, i):
            it()

    def head_proj_g(hd, i, g):
        t0 = 512 * g
        n = min(512, L - t0)
        for bk, wt, bw in ((3, wuq, b_wuq), (6, wuqr, b_wuqr)):
            for c in range(2):
                P.op("pe", lambda h, bk=bk, wt=wt, c=c: h.matmul(C.ps[0:96, bk, :n], wt[:, c, hd, :], cqn[:, c, t0:t0 + n],
                                                                  start=(c == 0), stop=(c == 1)),
                     reads=[bw, b_cqn], writes=[C.b_ps[bk]])
        a_t, b_a = t1.next()
        b_t, b_b = t2.next()
        P.op("dve", lambda h: h.tensor_tensor(a_t[0:96, :n], C.ps[0:96, 3, :n], rc[0:96, t0:t0 + n], ALU.mult),
             reads=[C.b_ps[3], b_rc], writes=[b_a])
        P.op("dve", lambda h: h.tensor_tensor(b_t[0:96, :n], C.ps[0:96, 6, :n], rsn[0:96, t0:t0 + n], ALU.mult),
             reads=[C.b_ps[6], b_rsn], writes=[b_b])
        P.op("pool", lambda h: h.tensor_tensor(QT[i][0:96, t0:t0 + n], a_t[0:96, :n], b_t[0:96, :n], ALU.add),
             reads=[b_a, b_b], writes=[b_QT[i]])
        P.op("pe", lambda h: h.matmul(C.ps[0:64, 7, :n], wukv[:, hd * 128:hd * 128 + 64], ckvn[:, t0:t0 + n], start=True, stop=True),
             reads=[b_wukv, b_ckvn], writes=[C.b_ps[7]])
        P.op("dve", lambda h: h.tensor_copy(KT[i][0:64, t0:t0 + n], C.ps[0:64, 7, :n]), reads=[C.b_ps[7]], writes=[b_KT[i]])

    cnt = {"st": 0, "ob": 0}
    LA = 2

    def norm_a(hd, q0, nq, ob):
        o_t, b_o = osb.next()
        r_t, b_r = rsum.next()
        P.op("dve", lambda h: h.tensor_copy(o_t[0:65, :nq], C.ps[0:65, ob, :nq]), reads=[C.b_ps[ob]], writes=[b_o])
        P.op("dve", lambda h: h.reciprocal(r_t[64:65, :nq], o_t[64:65, :nq]), reads=[b_o], writes=[b_r])
        return (hd, q0, nq, o_t, b_o, r_t, b_r)

    def norm_b(st):
        hd, q0, nq, o_t, b_o, r_t, b_r = st
        n_t, b_n = onb.next()
        P.op("pe", lambda h: h.matmul(C.ps[0:64, 7, :nq], C.ones_f[64:65, 0:64], r_t[64:65, :nq], start=True, stop=True),
             reads=[C.b_onesf, b_r], writes=[C.b_ps[7]])
        P.op("dve", lambda h: h.tensor_tensor(n_t[0:64, :nq], C.ps[0:64, 7, :nq], o_t[0:64, :nq], ALU.mult),
             reads=[C.b_ps[7], b_o], writes=[b_n])
        P.dma("sp", lambda h: h.dma_start(out=oT[hd * 64:(hd + 1) * 64, q0:q0 + nq], in_=n_t[0:64, :nq]),
              reads=[b_n], track=b_n, dwrites=[db_o])

    def blk_front(i, qg, q0, nq, kt):
        nk = min(128, L - 128 * kt)
        jj = kt - 4 * qg
        cs = 0 if (jj < 0 or qg == 8) else 128 * jj
        diag = (jj >= 0) if qg < 8 else (kt == NKT - 1)
        ncol = nq - cs
        sb = cnt["st"] % 3
        cnt["st"] += 1
        if DUP_QK and ncol >= 256:
            P.op("pe", lambda h: h.matmul(C.ps[0:nk, sb, 0:ncol], KT[i][0:96, kt * 128:kt * 128 + nk],
                                          QT[i][0:96, q0 + cs:q0 + nq], start=True, stop=True),
                 reads=[b_KT[i], b_QT[i]], writes=[C.b_ps[sb]])
        P.op("pe", lambda h: h.matmul(C.ps[0:nk, sb, 0:ncol], KT[i][0:96, kt * 128:kt * 128 + nk],
                                      QT[i][0:96, q0 + cs:q0 + nq], start=True, stop=not diag),
             reads=[b_KT[i], b_QT[i]], writes=[C.b_ps[sb]])
        if diag:
            dc = min(128, ncol)
            P.op("pe", lambda h: h.matmul(C.ps[0:nk, sb, 0:dc], C.ident_bf[0:nk, 0:nk], C.mneg_bf[0:nk, 0:dc],
                                          start=False, stop=True),
                 reads=[C.b_ident, C.b_mneg], writes=[C.b_ps[sb]])
        p_t, b_p = pT.next()
        P.op("act", lambda h: h.activation(p_t[0:nk, 0:ncol], C.ps[0:nk, sb, 0:ncol], AF.Exp, scale=scale),
             reads=[C.b_ps[sb]], writes=[b_p])
        return (nk, cs, ncol, p_t, b_p)

    def blk_back(i, nq, ob, kt, first, last, fr):
        nk, cs, ncol, p_t, b_p = fr
        P.op("pe", lambda h: h.matmul(C.ps[0:65, ob, cs:nq], VT[i][0:nk, kt, 0:65], p_t[0:nk, 0:ncol],
                                      start=first, stop=last),
             reads=[b_VT[i], b_p], writes=[C.b_ps[ob]])

    def attn_head(hd, i, items):
        blocks = []
        for qg in range(ngr):
            q0 = 512 * qg
            nq = min(512, L - q0)
            ob = 4 + (cnt["ob"] % 2)
            cnt["ob"] += 1
            kts = list(range(0, min(4 * qg + 4, NKT)))
            for kt in kts:
                blocks.append((qg, q0, nq, ob, kt, kt == kts[0], kt == kts[-1]))
        pend = []
        norms = []

        def retire():
            (qg, q0, nq, ob, kt, first, last), fr = pend.pop(0)
            blk_back(i, nq, ob, kt, first, last, fr)
            for nm in norms:
                nm[1] -= 1
            while norms and norms[0][1] <= 0:
                norm_b(norms.pop(0)[0])
            if last:
                norms.append([norm_a(hd, q0, nq, ob), 8])

        every = max(1, (len(blocks) - 8) // max(1, len(items)))
        for bi, b in enumerate(blocks):
            pend.append((b, blk_front(i, b[0], b[1], b[2], b[4])))
            if len(pend) > LA:
                retire()
            if items and bi % every == every - 1:
                items.pop(0)()
        while pend:
            retire()
        while norms:
            norm_b(norms.pop(0)[0])
        while items:
            items.pop(0)()

    head_proj(0, 0)
    for hd in range(16):
        i = hd % 2
        items = head_proj_items(hd + 1, 1 - i) if hd + 1 < 16 else []
        attn_head(hd, i, items)


def diff_phase_a(P, C, l, h_in, db_in, qT, db_q, kT, db_k, vtok, db_v):
    tag = "da_"
    W = Work()
    W.hin = Rot(P, tag + "hin", 3, [128, 512], F32)
    W.sq = Rot(P, tag + "sq", 2, [128, 512], BF16)
    W.xs = [(P.sbuf(tag + "xn%d" % i, [128, 8, 512], BF16), P.buf(tag + "xn"),
             P.sbuf(tag + "rs1%d" % i, [128, 512], F32), P.buf(tag + "rs1")) for i in range(2)]
    W.xi = -1
    win = P.sbuf(tag + "win", [128, 8, 3 * D], BF16)
    b_win = P.buf("win")
    load_w(P, win, b_win, C.diff_w_in[0], 8, 1)
    stg = Rot(P, tag + "stg", 2, [128, 16, 512], BF16)
    vst = Rot(P, tag + "vst", 2, [128, 4, D], BF16)
    qv = qT.rearrange("(c p) t -> p c t", p=128)
    kv = kT.rearrange("(c p) t -> p c t", p=128)
    cnt = {"b": 0}

    def front(g):
        t0 = 512 * g
        n = min(512, L - t0)
        flip_x(W)
        prenorm(P, C, W, l, 0, h_in, db_in, PAD + t0, n, W.xn, W.b_xn)

    def group(g):
        t0 = 512 * g
        n = min(512, L - t0)
        xn, b_xn = W.xs[g % 2][0], W.xs[g % 2][1]
        s_t, b_s = stg.next()
        for i in range(16):
            bk = cnt["b"] % 4
            cnt["b"] += 1
            ps = C.ps[:, bk, :]
            for k in range(8):
                P.op("pe", lambda h, ps=ps, k=k, i=i: h.matmul(ps[:, :n], win[:, k, i * 128:(i + 1) * 128], xn[:, k, :n],
                                                                start=(k == 0), stop=(k == 7)),
                     reads=[b_win, b_xn], writes=[C.b_ps[bk]])
            if i % 2 == 0:
                P.op("act", lambda h, ps=ps, i=i: h.activation(s_t[:, i, :n], ps[:, :n], AF.Copy), reads=[C.b_ps[bk]], writes=[b_s])
            else:
                P.op("dve", lambda h, ps=ps, i=i: h.tensor_copy(s_t[:, i, :n], ps[:, :n]), reads=[C.b_ps[bk]], writes=[b_s])
        P.dma("sp", lambda h: h.dma_start(out=qv[:, :, t0:t0 + n], in_=s_t[:, 0:8, :n]), reads=[b_s], track=b_s, dwrites=[db_q])
        P.dma("sp", lambda h: h.dma_start(out=kv[:, :, t0:t0 + n], in_=s_t[:, 8:16, :n]), reads=[b_s], track=b_s, dwrites=[db_k])
        v_t, b_v = vst.next()
        ntt = (n + 127) // 128
        for tt in range(ntt):
            nt = min(128, n - 128 * tt)
            for half in range(2):
                bk = cnt["b"] % 4
                cnt["b"] += 1
                for k in range(8):
                    P.op("pe", lambda h, bk=bk, k=k, tt=tt, nt=nt, half=half: h.matmul(
                        C.ps[0:nt, bk, :], xn[:, k, tt * 128:tt * 128 + nt], win[:, k, 2 * D + half * 512:2 * D + (half + 1) * 512],
                        start=(k == 0), stop=(k == 7)),
                        reads=[b_win, b_xn], writes=[C.b_ps[bk]])
                if half == 0:
                    P.op("act", lambda h, bk=bk, tt=tt, nt=nt: h.activation(v_t[0:nt, tt, 0:512], C.ps[0:nt, bk, :], AF.Copy),
                         reads=[C.b_ps[bk]], writes=[b_v])
                else:
                    P.op("dve", lambda h, bk=bk, tt=tt, nt=nt: h.tensor_copy(v_t[0:nt, tt, 512:1024], C.ps[0:nt, bk, :]),
                         reads=[C.b_ps[bk]], writes=[b_v])
            P.dma("sp", lambda h, tt=tt, nt=nt: h.dma_start(out=vtok[t0 + tt * 128:t0 + tt * 128 + nt, :], in_=v_t[0:nt, tt, :]),
                  reads=[b_v], track=b_v, dwrites=[db_v])

    ngr = (L + 511) // 512
    front(0)
    for g in range(ngr):
        if g + 1 < ngr:
            front(g + 1)
        group(g)


def diff_phase_b(P, C, l, qT, db_q, kT, db_k, vtok, db_v, oT, db_o):
    tag = "db_"
    lambda_init = 0.8 - 0.6 * math.exp(-0.3 * l)
    lam = P.sbuf(tag + "lam", [128, 8], F32)
    b_lam = P.buf("lam")
    lt = P.sbuf(tag + "lt", [128, 2, 64], F32)
    b_lt = P.buf("lt")
    sm = C.small_t
    for q in range(2):
        P.op("dve", lambda h, q=q: h.tensor_tensor(lt[:, q, :], sm[:, SP_LAM + q * 128:SP_LAM + q * 128 + 64],
                                                    sm[:, SP_LAM + q * 128 + 64:SP_LAM + q * 128 + 128], ALU.mult),
             reads=[C.b_small], writes=[b_lt])
        P.op("dve", lambda h, q=q: h.reduce_sum(lam[:, q:q + 1], lt[:, q, :], mybir.AxisListType.X), reads=[b_lt], writes=[b_lam])
    P.op("act", lambda h: h.activation(lam[:, 2:4], lam[:, 0:2], AF.Exp), reads=[b_lam], writes=[b_lam])
    P.op("dve", lambda h: h.tensor_tensor(lam[:, 4:5], lam[:, 3:4], lam[:, 2:3], ALU.subtract), reads=[b_lam], writes=[b_lam])
    P.op("dve", lambda h: h.tensor_scalar(lam[:, 4:5], lam[:, 4:5], -lambda_init, None, ALU.add), reads=[b_lam], writes=[b_lam])
    P.op("dve", lambda h: h.tensor_scalar(lam[:, 5:6], sm[:, SP_SUBLN:SP_SUBLN + 1], 1.0 - lambda_init, None, ALU.mult),
         reads=[C.b_small, b_lam], writes=[b_lam])
    neg_lam = lam[:, 4:5]
    gsub = lam[:, 5:6]

    QA = [[P.sbuf(tag + "QA%d%d" % (i, m), [68, L], BF16) for m in range(2)] for i in range(2)]
    KA = [[P.sbuf(tag + "KA%d%d" % (i, m), [68, L], BF16) for m in range(2)] for i in range(2)]
    b_QA = [[P.buf("QA") for m in range(2)] for i in range(2)]
    b_KA = [[P.buf("KA") for m in range(2)] for i in range(2)]
    VH = [P.sbuf(tag + "VH%d" % i, [128, NKT, 128], BF16) for i in range(2)]
    b_VH = P.bufs_n("VH", 2)
    pT = Rot(P, tag + "pT", 6, [128, 512], BF16)
    r0 = Rot(P, tag + "r0", 2, [128, 512], F32)
    r1 = Rot(P, tag + "r1", 2, [128, 512], F32)
    oo = Rot(P, tag + "oo", 2, [128, 512], F32)
    ev0 = Rot(P, tag + "ev0", 2, [128, 512], F32)
    ev1 = Rot(P, tag + "ev1", 2, [128, 512], F32)
    sq = Rot(P, tag + "sq", 2, [128, 512], BF16)
    rs = P.sbuf(tag + "rs", [128, 512], F32)
    b_rs = P.buf("rs")
    onb = Rot(P, tag + "onb", 2, [128, 512], BF16)
    vv = vtok.rearrange("(kt p) d -> p kt d", p=128)
    ngr = (L + 511) // 512
    cnt = {"st": 0}

    def load_head(hd, i):
        for m in range(2):
            r0_ = hd * 128 + m * 64
            P.dma("sp", lambda h, m=m, r0_=r0_: h.dma_start(out=QA[i][m][0:64, :], in_=qT[r0_:r0_ + 64, :]),
                  writes=[b_QA[i][m]], track=b_QA[i][m], dreads=[db_q])
            P.dma("pool", lambda h, m=m: h.dma_start(out=QA[i][m][64:68, :], in_=C.alibiQ[hd * 4:hd * 4 + 4, :]),
                  writes=[b_QA[i][m]], track=b_QA[i][m])
            P.dma("sp", lambda h, m=m, r0_=r0_: h.dma_start(out=KA[i][m][0:64, :], in_=kT[r0_:r0_ + 64, :]),
                  writes=[b_KA[i][m]], track=b_KA[i][m], dreads=[db_k])
            P.dma("pool", lambda h, m=m: h.dma_start(out=KA[i][m][64:68, :], in_=C.alibiK[hd * 4:hd * 4 + 4, :]),
                  writes=[b_KA[i][m]], track=b_KA[i][m])
        P.dma("sp", lambda h: h.dma_start(out=VH[i][:, :, :], in_=vv[:, :, hd * 128:(hd + 1) * 128]),
              writes=[b_VH[i]], track=b_VH[i], dreads=[db_v])

    LA = 3
    STB = (0, 1, 2, 7)

    def unit_front(i, qg, q0, nq, kt, m):
        nk = min(128, L - 128 * kt)
        jj = kt - 4 * qg
        cs = 0 if (jj < 0 or qg == 8) else 128 * jj
        diag = (jj >= 0) if qg < 8 else (kt == NKT - 1)
        ncol = nq - cs
        sb = STB[cnt["st"] % 4]
        cnt["st"] += 1
        P.op("pe", lambda h: h.matmul(C.ps[0:nk, sb, 0:ncol], KA[i][m][0:68, kt * 128:kt * 128 + nk],
                                      QA[i][m][0:68, q0 + cs:q0 + nq], start=True, stop=not diag),
             reads=[b_KA[i][m], b_QA[i][m]], writes=[C.b_ps[sb]])
        if diag:
            dc = min(128, ncol)
            P.op("pe", lambda h: h.matmul(C.ps[0:nk, sb, 0:dc], C.ident_bf[0:nk, 0:nk], C.mneg_bf[0:nk, 0:dc],
                                          start=False, stop=True),
                 reads=[C.b_ident, C.b_mneg], writes=[C.b_ps[sb]])
        p_t, b_p = pT.next()
        P.op("act", lambda h: h.activation(p_t[0:nk, 0:ncol], C.ps[0:nk, sb, 0:ncol], AF.Exp, scale=0.125),
             reads=[C.b_ps[sb]], writes=[b_p])
        return (nk, cs, ncol, p_t, b_p)

    def unit_back(i, nq, kt, m, first, last, fr):
        nk, cs, ncol, p_t, b_p = fr
        P.op("pe", lambda h: h.matmul(C.ps[:, 3 + m, cs:nq], VH[i][0:nk, kt, :], p_t[0:nk, 0:ncol], start=first, stop=last),
             reads=[b_VH[i], b_p], writes=[C.b_ps[3 + m]])
        P.op("pe", lambda h: h.matmul(C.ps[:, 5 + m, cs:nq], C.ones_bf[0:nk, :], p_t[0:nk, 0:ncol], start=first, stop=last),
             reads=[C.b_ones, b_p], writes=[C.b_ps[5 + m]])

    def norm_a(hd, q0, nq):
        a_t, b_a = r0.next()
        c_t, b_c = r1.next()
        o_t, b_o = oo.next()
        e0, b_e0 = ev0.next()
        e1, b_e1 = ev1.next()
        P.op("dve", lambda h: h.tensor_copy(a_t[:, :nq], C.ps[:, 5, :nq]), reads=[C.b_ps[5]], writes=[b_a])
        P.op("act", lambda h: h.activation(e0[:, :nq], C.ps[:, 3, :nq], AF.Copy), reads=[C.b_ps[3]], writes=[b_e0])
        P.op("dve", lambda h: h.tensor_copy(c_t[:, :nq], C.ps[:, 6, :nq]), reads=[C.b_ps[6]], writes=[b_c])
        P.op("act", lambda h: h.activation(e1[:, :nq], C.ps[:, 4, :nq], AF.Copy), reads=[C.b_ps[4]], writes=[b_e1])
        P.op("dve", lambda h: h.reciprocal(a_t[:, :nq], a_t[:, :nq]), reads=[b_a], writes=[b_a])
        P.op("dve", lambda h: h.reciprocal(c_t[:, :nq], c_t[:, :nq]), reads=[b_c], writes=[b_c])
        P.op("dve", lambda h: h.tensor_tensor(a_t[:, :nq], e0[:, :nq], a_t[:, :nq], ALU.mult), reads=[b_e0, b_a], writes=[b_a])
        P.op("dve", lambda h: h.tensor_tensor(c_t[:, :nq], e1[:, :nq], c_t[:, :nq], ALU.mult), reads=[b_e1, b_c], writes=[b_c])
        P.op("dve", lambda h: h.scalar_tensor_tensor(o_t[:, :nq], c_t[:, :nq], neg_lam, a_t[:, :nq], ALU.mult, ALU.add),
             reads=[b_a, b_c, b_lam], writes=[b_o])
        s_t, b_s = sq.next()
        P.op("act", lambda h: h.activation(s_t[:, :nq], o_t[:, :nq], AF.Square), reads=[b_o], writes=[b_s])
        return (hd, q0, nq, o_t, b_o, s_t, b_s)

    def norm_b(st):
        hd, q0, nq, o_t, b_o, s_t, b_s = st
        sb = STB[cnt["st"] % 4]
        cnt["st"] += 1
        P.op("pe", lambda h: h.matmul(C.ps[:, sb, :nq], C.ones_bf[:], s_t[:, :nq], start=True, stop=True),
             reads=[C.b_ones, b_s], writes=[C.b_ps[sb]])
        rstd_from_psum(P, C, C.ps[:, sb, :], C.b_ps[sb], nq, rs, b_rs, 128.0)
        n_t, b_n = onb.next()
        P.op("dve", lambda h: h.scalar_tensor_tensor(n_t[:, :nq], o_t[:, :nq], gsub, rs[:, :nq], ALU.mult, ALU.mult),
             reads=[b_o, b_rs, b_lam], writes=[b_n])
        P.dma("sp", lambda h: h.dma_start(out=oT[hd * 128:(hd + 1) * 128, q0:q0 + nq], in_=n_t[:, :nq]),
              reads=[b_n], track=b_n, dwrites=[db_o])

    def attn_head(hd, i):
        units = []
        for qg in range(ngr):
            q0 = 512 * qg
            nq = min(512, L - q0)
            kts = list(range(0, min(4 * qg + 4, NKT)))
            for kt in kts:
                for m in range(2):
                    units.append((qg, q0, nq, kt, m, kt == kts[0], kt == kts[-1]))
        pend = []
        norms = []

        def retire():
            (qg, q0, nq, kt, m, first, last), fr = pend.pop(0)
            unit_back(i, nq, kt, m, first, last, fr)
            for nm in norms:
                nm[1] -= 1
            while norms and norms[0][1] <= 0:
                norm_b(norms.pop(0)[0])
            if last and m == 1:
                norms.append([norm_a(hd, q0, nq), 4])

        for u in units:
            pend.append((u, unit_front(i, u[0], u[1], u[2], u[3], u[4])))
            if len(pend) > LA:
                retire()
        while pend:
            retire()
        while norms:
            norm_b(norms.pop(0)[0])

    load_head(0, 0)
    for hd in range(8):
        i = hd % 2
        if hd + 1 < 8:
            load_head(hd + 1, 1 - i)
        attn_head(hd, i)


def build_program():
    nc = bass.Bass("TRN2", target_bir_lowering=False)
    C = Ctx()
    declare_io(nc, C)
    yT = nc.dram_tensor("yT", [D, SEQ], F32, kind="ExternalOutput").ap()
    hB = nc.dram_tensor("hB", [D, LC], F32, kind="Internal").ap()
    hC = nc.dram_tensor("hC", [D, LC], F32, kind="Internal").ap()
    oT = nc.dram_tensor("oT", [D, L], BF16, kind="Internal").ap()
    qT = nc.dram_tensor("qT", [D, L], BF16, kind="Internal").ap()
    kT = nc.dram_tensor("kT", [D, L], BF16, kind="Internal").ap()
    vtok = nc.dram_tensor("vtok", [NKT * 128, D], BF16, kind="Internal").ap()
    d_h0, d_hB, d_hC, d_oT, d_qT, d_kT, d_v, d_y = [DBuf(n) for n in ("h0", "hB", "hC", "oT", "qT", "kT", "vtok", "yT")]
    with ExitStack() as st:
        P = Prog(nc, st)
        setup_common(P, C)
        zt = P.sbuf("zt", [128, 2], F32)
        b_zt = P.buf("zt")
        P.op("pool", lambda h: h.memset(zt[:], 0.0), writes=[b_zt])
        for hx, dx in ((hB, d_hB), (hC, d_hC)):
            for c in range(8):
                P.dma("sp", lambda h, hx=hx, c=c: h.dma_start(out=hx[c * 128:(c + 1) * 128, 0:2], in_=zt[:]),
                      reads=[b_zt], track=b_zt, dwrites=[dx])
        P.begin_phase()
        mla_phase(P, C, 0, 0, C.h0, d_h0, oT, d_oT)
        P.end_phase()
        P.begin_phase()
        proj_phase(P, C, 0, C.mla_w_o[0], oT, d_oT, C.h0, d_h0, hB, d_hB, "p0_")
        P.end_phase()
        P.begin_phase()
        ffn_phase(P, C, 0, hB, d_hB, hC, d_hC, PAD, 0)
        P.end_phase()
        P.begin_phase()
        sc_phase(P, C, 1, hC, d_hC, hB, d_hB)
        P.end_phase()
        P.begin_phase()
        ffn_phase(P, C, 1, hB, d_hB, hC, d_hC, PAD, 0)
        P.end_phase()
        P.begin_phase()
        diff_phase_a(P, C, 2, hC, d_hC, qT, d_qT, kT, d_kT, vtok, d_v)
        P.end_phase()
        P.begin_phase()
        diff_phase_b(P, C, 2, qT, d_qT, kT, d_kT, vtok, d_v, oT, d_oT)
        P.end_phase()
        P.begin_phase()
        proj_phase(P, C, 2, C.diff_w_o[0], oT, d_oT, hC, d_hC, hB, d_hB, "p2_")
        P.end_phase()
        P.begin_phase()
        ffn_phase(P, C, 2, hB, d_hB, hC, d_hC, PAD, 0)
        P.end_phase()
        P.begin_phase()
        mla_phase(P, C, 3, 1, hC, d_hC, oT, d_oT)
        P.end_phase()
        P.begin_phase()
        proj_phase(P, C, 3, C.mla_w_o[1], oT, d_oT, hC, d_hC, hB, d_hB, "p3_")
        P.end_phase()
        P.begin_phase()
        ffn_phase(P, C, 3, hB, d_hB, yT, d_y, -NMETA, NMETA)
        P.end_phase()
        stats = P.stats
    return nc, stats


_CACHE = {}


def kernel(**inputs):
    inp = {k: np.asarray(v) for k, v in inputs.items()}
    x = inp["x"].astype(np.float32, copy=False)
    B = x.shape[0]
    meta = inp["meta_tokens"].astype(np.float32, copy=False)
    if "nc" not in _CACHE:
        _CACHE["nc"] = build_program()[0]
    nc = _CACHE["nc"]
    shared = {"small": pack_small(inp)}
    shared.update(make_consts())
    for k in ("mla_w_in", "mla_w_uq", "mla_w_ukv", "mla_w_o", "sc_w_in", "sc_w_out", "diff_w_in", "diff_w_o",
              "ffn_w_up", "ffn_w_down"):
        shared[k] = np.ascontiguousarray(inp[k], dtype=np.float32)
    in_maps = []
    for b in range(B):
        h0 = np.zeros((D, LC), np.float32)
        h0[:, PAD:PAD + NMETA] = meta.T
        h0[:, PAD + NMETA:] = x[b].T
        m = dict(shared)
        m["h0"] = h0
        in_maps.append(m)
    res = run_bass_kernel_spmd(nc, in_maps, core_ids=list(range(B)))
    out = np.empty((B, SEQ, D), np.float32)
    for b in range(B):
        out[b] = np.asarray(res.results[b]["yT"]).T
    return out
```
